# Optimizing a Trainium2 kernel written in Bass

```python
import math
import jax, jax.numpy as jnp
from jax import lax
import numpy as np

D_MODEL = 1024
BATCH = 16
SEQ = 256
DEPTH = 4
DEC_BATCH = 2
DEC_SEQ = 4096
PAST_LEN = 256

GRID_W = 64
HEAD_DIM = 64
ATTN_Q_HEADS = 8
ATTN_KV_HEADS = 2
ATTN_GROUP = ATTN_Q_HEADS // ATTN_KV_HEADS
ATTN_WIDTH = ATTN_Q_HEADS * HEAD_DIM
ATTN_KV_WIDTH = ATTN_KV_HEADS * HEAD_DIM
ATTN_IN = ATTN_WIDTH + 2 * ATTN_KV_WIDTH
Q_BLOCK = 128
ROPE_THETA = 10000.0
ROPE_AXIS_DIM = HEAD_DIM // 2
RWKV_HEADS = 8
RWKV_HEAD = 64
RWKV_WIDTH = RWKV_HEADS * RWKV_HEAD
RWKV_DECAY_LORA = 64
RWKV_ICLR_LORA = 64
RWKV_GATE_LORA = 128
RWKV_IN = 3 * RWKV_WIDTH + RWKV_GATE_LORA + 2 * RWKV_DECAY_LORA + 2 * RWKV_ICLR_LORA
AB_IN = ATTN_IN + RWKV_IN
MIX_WIDTH = ATTN_WIDTH + RWKV_WIDTH
S5_WIDTH = D_MODEL
S5_GROUP_CH = 16
S5_GROUPS = S5_WIDTH // S5_GROUP_CH
S5_STATE = 64
D_FF = 2816
N_AB = (DEPTH + 1) // 2
N_C = DEPTH // 2
RMS_EPS = 1e-6
GN_EPS = 64e-5
F32 = jnp.float32

kernel_name = 'hybrid_prefix_diffusion_step'


def rms_norm(x, gain):
    x32 = x.astype(F32)
    y = x32 * lax.rsqrt(jnp.mean(x32 * x32, axis=-1, keepdims=True) + RMS_EPS)
    return y.astype(x.dtype) * gain


def swiglu(h, w1, w3, w2):
    return (jax.nn.silu(h @ w1) * (h @ w3)) @ w2


def adaln(cvec, ada_w, ada_b):
    m = jnp.einsum('bd,lde->ble', jax.nn.silu(cvec), ada_w) + ada_b
    return m.reshape(cvec.shape[0], DEPTH, 3, 3, D_MODEL)


def modulated_in(x, gain, m):
    return rms_norm(x, gain) * (1 + m[:, None, 1]) + m[:, None, 0]


def residual_out(x, y, gain, m, weight):
    return x + weight * m[:, None, 2] * rms_norm(y, gain)


def axial_rope(length):
    rows = length // GRID_W
    row = jnp.repeat(jnp.arange(rows), GRID_W).astype(F32)
    col = jnp.tile(jnp.arange(GRID_W), rows).astype(F32)
    inv = ROPE_THETA ** (-jnp.arange(ROPE_AXIS_DIM // 2, dtype=F32) * 2.0 / ROPE_AXIS_DIM)
    ar, ac = row[:, None] * inv, col[:, None] * inv
    ang = jnp.concatenate([ar, ar, ac, ac], axis=-1)
    return jnp.cos(ang), jnp.sin(ang)


def apply_rope(x, cos, sin):
    x1, x2, x3, x4 = jnp.split(x, 4, axis=-1)
    rot = jnp.concatenate([-x2, x1, -x4, x3], axis=-1)
    shape = (1, cos.shape[0]) + (1,) * (x.ndim - 3) + (HEAD_DIM,)
    y = x.astype(F32) * cos.reshape(shape) + rot.astype(F32) * sin.reshape(shape)
    return y.astype(x.dtype)


def blocked_attention(q, k, v):
    b, s = q.shape[:2]
    nb = s // Q_BLOCK
    qb = jnp.moveaxis(q.reshape(b, nb, Q_BLOCK, ATTN_KV_HEADS, ATTN_GROUP, HEAD_DIM), 1, 0)
    scale = HEAD_DIM ** -0.5

    def one_block(qblk):
        sc = jnp.einsum('bqkgd,btkd->bkgqt', qblk, k).astype(F32) * scale
        p = jax.nn.softmax(sc, axis=-1).astype(v.dtype)
        return jnp.einsum('bkgqt,btkd->bqkgd', p, v)

    o = lax.map(one_block, qb)
    return jnp.moveaxis(o, 0, 1).reshape(b, s, ATTN_WIDTH)


def attention_mixer(pa, q_gain, k_gain, rope, ctx_kv):
    b, L = pa.shape[:2]
    q, k, v = jnp.split(pa, [ATTN_WIDTH, ATTN_WIDTH + ATTN_KV_WIDTH], axis=-1)
    q = rms_norm(q.reshape(b, L, ATTN_KV_HEADS, ATTN_GROUP, HEAD_DIM), q_gain)
    k = rms_norm(k.reshape(b, L, ATTN_KV_HEADS, HEAD_DIM), k_gain)
    v = v.reshape(b, L, ATTN_KV_HEADS, HEAD_DIM)
    if ctx_kv is None:
        return blocked_attention(q, k, v), k, v
    cos, sin = rope
    qr = apply_rope(q, cos, sin)
    kr = apply_rope(k, cos, sin)
    keys = jnp.concatenate([kr, ctx_kv[0]], axis=1)
    vals = jnp.concatenate([v, ctx_kv[1]], axis=1)
    return blocked_attention(qr, keys, vals), k, v


def centred_shift_mix(p, mu):
    pad = jnp.pad(p, ((0, 0), (1, 1), (0, 0)))
    nb = 0.5 * (pad[:, :-2] + pad[:, 2:])
    return p + mu * (nb - p)


def wkv_scan(r, decay, k, v, kk, a, s0, reverse):
    def step(S, inp):
        r_t, w_t, k_t, v_t, kk_t, a_t = inp
        sa = jnp.einsum('bhvk,bhk->bhv', S, -kk_t)
        S = (S * w_t[:, :, None, :] + sa[..., None] * (kk_t * a_t)[:, :, None, :]
             + v_t[..., None] * k_t[:, :, None, :])
        return S, jnp.einsum('bhvk,bhk->bhv', S, r_t)

    xs = tuple(jnp.moveaxis(t, 1, 0) for t in (r, decay, k, v, kk, a))
    s_final, y = lax.scan(step, s0, xs, reverse=reverse)
    return jnp.moveaxis(y, 0, 1), s_final


def rwkv_mixer(pb, mu, w0, w_up, a0, a_up, g_up, k_k, k_a, r_k, ln_w, ln_b, s0):
    b, L = pb.shape[:2]
    pb = centred_shift_mix(pb, mu)
    sizes = [RWKV_WIDTH, RWKV_WIDTH, RWKV_WIDTH, RWKV_GATE_LORA,
             RWKV_DECAY_LORA, RWKV_DECAY_LORA, RWKV_ICLR_LORA]
    offs = np.cumsum(sizes).tolist()
    r, k, v, g_d, wd_f, wd_b, ad_f, ad_b = jnp.split(pb, offs, axis=-1)
    g = jax.nn.sigmoid(g_d) @ g_up

    def heads(t):
        return t.reshape(b, L, RWKV_HEADS, RWKV_HEAD).astype(F32)

    r32, k32, v32 = heads(r), heads(k), heads(v)
    kk = heads(k * k_k)
    kk = kk / jnp.maximum(jnp.sqrt(jnp.sum(kk * kk, axis=-1, keepdims=True)), 1e-12)
    k_a_h = k_a.reshape(RWKV_HEADS, RWKV_HEAD).astype(F32)
    r_k32 = r_k.astype(F32)
    s0 = s0.astype(F32)
    outs, finals = [], []
    for d, (wd, ad) in enumerate(((wd_f, ad_f), (wd_b, ad_b))):
        w = -jax.nn.softplus(-(w0[d] + jnp.tanh(wd) @ w_up[d]).astype(F32)) - 0.5
        decay = jnp.exp(-jnp.exp(heads(w)))
        a = jax.nn.sigmoid(heads(a0[d] + ad @ a_up[d]))
        kd = k32 * (1 + (a - 1) * k_a_h)
        yd, sd = wkv_scan(r32, decay, kd, v32, kk, a, s0[:, d], reverse=(d == 1))
        bonus = jnp.sum(r32 * kd * r_k32, axis=-1, keepdims=True) * v32
        outs.append(yd + bonus)
        finals.append(sd)
    y = outs[0] + outs[1]
    mean = jnp.mean(y, axis=-1, keepdims=True)
    var = jnp.mean(jnp.square(y - mean), axis=-1, keepdims=True)
    y = ((y - mean) * lax.rsqrt(var + GN_EPS)).reshape(b, L, RWKV_WIDTH).astype(pb.dtype)
    y = y * ln_w + ln_b
    return y * g, jnp.stack(finals, axis=1)


def s5_combine(e1, e2):
    a1r, a1i, b1r, b1i = e1
    a2r, a2i, b2r, b2i = e2
    return (a2r * a1r - a2i * a1i, a2r * a1i + a2i * a1r,
            a2r * b1r - a2i * b1i + b2r, a2r * b1i + a2i * b1r + b2i)


def s5_mixer(u, lam_re, lam_im, log_dt, b_re, b_im, c_re, c_im, d_skip, s0):
    bsz, L = u.shape[:2]
    ug = u.reshape(bsz, L, S5_GROUPS, S5_GROUP_CH).astype(F32)
    s0 = s0.astype(F32)
    ys, finals = [], []
    for d in range(2):
        lr, li = lam_re[d].astype(F32), lam_im[d].astype(F32)
        dt = jnp.exp(log_dt[d].astype(F32))[:, None]
        mag = jnp.exp(lr * dt)
        abr, abi = mag * jnp.cos(li * dt), mag * jnp.sin(li * dt)
        den = lr * lr + li * li
        fr = ((abr - 1) * lr + abi * li) / den
        fi = (abi * lr - (abr - 1) * li) / den
        br, bi = b_re[d].astype(F32), b_im[d].astype(F32)
        bbr = fr[..., None] * br - fi[..., None] * bi
        bbi = fr[..., None] * bi + fi[..., None] * br
        bur = jnp.einsum('blgc,gnc->blgn', ug, bbr)
        bui = jnp.einsum('blgc,gnc->blgn', ug, bbi)
        sr, si = s0[:, d, 0], s0[:, d, 1]
        first = 0 if d == 0 else L - 1
        bur = bur.at[:, first].add(abr * sr - abi * si)
        bui = bui.at[:, first].add(abr * si + abi * sr)
        ar = jnp.broadcast_to(abr, bur.shape)
        ai = jnp.broadcast_to(abi, bui.shape)
        _, _, hr, hi = lax.associative_scan(s5_combine, (ar, ai, bur, bui), reverse=(d == 1), axis=1)
        ys.append(jnp.einsum('blgn,gcn->blgc', hr, c_re[d].astype(F32))
                  - jnp.einsum('blgn,gcn->blgc', hi, c_im[d].astype(F32)))
        last = L - 1 if d == 0 else 0
        finals.append(jnp.stack([hr[:, last], hi[:, last]], axis=1))
    y = (ys[0] + ys[1]).reshape(bsz, L, S5_WIDTH) + d_skip.astype(F32) * u.astype(F32)
    return y.astype(u.dtype), jnp.stack(finals, axis=1)


def run_trunk(x, mods, P, ctx):
    b, L = x.shape[:2]
    rope = None if ctx is None else axial_rope(L)
    ks, vs, rws, s5s = [], [], [], []
    for l in range(DEPTH):
        m = mods[:, l]
        i = l // 2
        h = modulated_in(x, P['norm_pre'][l, 0], m[:, 0])
        f = swiglu(h, P['ffn_w1'][l, 0], P['ffn_w3'][l, 0], P['ffn_w2'][l, 0])
        x = residual_out(x, f, P['norm_post'][l, 0], m[:, 0], 0.5)
        h = modulated_in(x, P['norm_pre'][l, 1], m[:, 1])
        if l % 2 == 0:
            p = h @ P['ab_w_in'][i]
            pa, pb = p[..., :ATTN_IN], p[..., ATTN_IN:]
            if ctx is None:
                ckv = None
                s0 = jnp.zeros((b, 2, RWKV_HEADS, RWKV_HEAD, RWKV_HEAD), F32)
            else:
                ckv = (ctx[0][:, i], ctx[1][:, i])
                s0 = ctx[2][:, i]
            oa, k_c, v_c = attention_mixer(pa, P['attn_q_gain'][i], P['attn_k_gain'][i], rope, ckv)
            ob, s_fin = rwkv_mixer(pb, P['rwkv_mu'][i], P['rwkv_w0'][i], P['rwkv_w_up'][i],
                                   P['rwkv_a0'][i], P['rwkv_a_up'][i], P['rwkv_g_up'][i],
                                   P['rwkv_k_k'][i], P['rwkv_k_a'][i], P['rwkv_r_k'][i],
                                   P['rwkv_ln_w'][i], P['rwkv_ln_b'][i], s0)
            y = jnp.concatenate([oa, ob], axis=-1) @ P['ab_w_out'][i]
            if ctx is None:
                ks.append(k_c)
                vs.append(v_c)
                rws.append(s_fin)
        else:
            u = h @ P['s5_w_in'][i]
            if ctx is None:
                s0 = jnp.zeros((b, 2, 2, S5_GROUPS, S5_STATE), F32)
            else:
                s0 = ctx[3][:, i]
            ys, s_fin = s5_mixer(u, P['s5_lambda_re'][i], P['s5_lambda_im'][i], P['s5_log_dt'][i],
                                 P['s5_b_re'][i], P['s5_b_im'][i], P['s5_c_re'][i], P['s5_c_im'][i],
                                 P['s5_d'][i], s0)
            z = jax.nn.gelu(ys)
            z = z * jax.nn.sigmoid(z @ P['s5_w_glu'][i])
            y = z @ P['s5_w_out'][i]
            if ctx is None:
                s5s.append(s_fin)
        x = residual_out(x, y, P['norm_post'][l, 1], m[:, 1], 1.0)
        h = modulated_in(x, P['norm_pre'][l, 2], m[:, 2])
        f = swiglu(h, P['ffn_w1'][l, 1], P['ffn_w3'][l, 1], P['ffn_w2'][l, 1])
        x = residual_out(x, f, P['norm_post'][l, 2], m[:, 2], 0.5)
    if ctx is None:
        return x, (jnp.stack(ks, 1), jnp.stack(vs, 1), jnp.stack(rws, 1), jnp.stack(s5s, 1))
    return x, None


def setup_inputs(seed: int = 0) -> dict:
    key = jax.random.key(seed)
    ks = iter(jax.random.split(key, 64))

    def nrm(shape, s):
        return jax.random.normal(next(ks), shape, F32) * s

    def unif(shape, lo, hi):
        return jax.random.uniform(next(ks), shape, F32, lo, hi)

    D, E, F, G, N, C16 = D_MODEL, S5_WIDTH, D_FF, S5_GROUPS, S5_STATE, S5_GROUP_CH
    lam_im = jnp.pi * jnp.broadcast_to(jnp.arange(N, dtype=F32), (N_C, 2, G, N)) + nrm((N_C, 2, G, N), 0.01)
    return {
        'x_prompt': nrm((BATCH, SEQ, D), 1.0),
        'x_sample': nrm((DEC_BATCH, DEC_SEQ, D), 1.0),
        'cache_k': nrm((DEC_BATCH, N_AB, PAST_LEN, ATTN_KV_HEADS, HEAD_DIM), 1.0),
        'cache_v': nrm((DEC_BATCH, N_AB, PAST_LEN, ATTN_KV_HEADS, HEAD_DIM), 0.6),
        'state_rwkv': nrm((DEC_BATCH, N_AB, 2, RWKV_HEADS, RWKV_HEAD, RWKV_HEAD), 0.3),
        'state_s5': nrm((DEC_BATCH, N_C, 2, 2, G, N), 0.1),
        'c': nrm((DEC_BATCH, D), 1.0),
        'c_ctx': nrm((D,), 1.0),
        'ada_w': nrm((DEPTH, D, 9 * D), 0.5 * D ** -0.5),
        'ada_b': nrm((DEPTH, 9 * D), 0.02),
        'norm_pre': 1.0 + nrm((DEPTH, 3, D), 0.02),
        'norm_post': 1.0 + nrm((DEPTH, 3, D), 0.02),
        'ffn_w1': nrm((DEPTH, 2, D, F), D ** -0.5),
        'ffn_w3': nrm((DEPTH, 2, D, F), D ** -0.5),
        'ffn_w2': nrm((DEPTH, 2, F, D), F ** -0.5),
        'ab_w_in': nrm((N_AB, D, AB_IN), D ** -0.5),
        'ab_w_out': nrm((N_AB, MIX_WIDTH, D), MIX_WIDTH ** -0.5),
        'attn_q_gain': 1.0 + nrm((N_AB, HEAD_DIM), 0.02),
        'attn_k_gain': 1.0 + nrm((N_AB, HEAD_DIM), 0.02),
        'rwkv_mu': unif((N_AB, RWKV_IN), 0.0, 1.0),
        'rwkv_w0': unif((N_AB, 2, RWKV_WIDTH), -5.0, 0.0),
        'rwkv_w_up': nrm((N_AB, 2, RWKV_DECAY_LORA, RWKV_WIDTH), 0.5 * RWKV_DECAY_LORA ** -0.5),
        'rwkv_a0': nrm((N_AB, 2, RWKV_WIDTH), 0.1),
        'rwkv_a_up': nrm((N_AB, 2, RWKV_ICLR_LORA, RWKV_WIDTH), 0.5 * RWKV_ICLR_LORA ** -0.5),
        'rwkv_g_up': nrm((N_AB, RWKV_GATE_LORA, RWKV_WIDTH), RWKV_GATE_LORA ** -0.5),
        'rwkv_k_k': 0.85 + nrm((N_AB, RWKV_WIDTH), 0.02),
        'rwkv_k_a': 1.0 + nrm((N_AB, RWKV_WIDTH), 0.02),
        'rwkv_r_k': nrm((N_AB, RWKV_HEADS, RWKV_HEAD), 0.1),
        'rwkv_ln_w': 1.0 + nrm((N_AB, RWKV_WIDTH), 0.02),
        'rwkv_ln_b': nrm((N_AB, RWKV_WIDTH), 0.02),
        's5_w_in': nrm((N_C, D, E), D ** -0.5),
        's5_lambda_re': -0.5 + nrm((N_C, 2, G, N), 0.01),
        's5_lambda_im': lam_im,
        's5_log_dt': unif((N_C, 2, G), math.log(0.001), math.log(0.1)),
        's5_b_re': nrm((N_C, 2, G, N, C16), (2 * C16) ** -0.5),
        's5_b_im': nrm((N_C, 2, G, N, C16), (2 * C16) ** -0.5),
        's5_c_re': nrm((N_C, 2, G, C16, N), (2 * N) ** -0.5),
        's5_c_im': nrm((N_C, 2, G, C16, N), (2 * N) ** -0.5),
        's5_d': nrm((N_C, E), 1.0),
        's5_w_glu': nrm((N_C, E, E), E ** -0.5),
        's5_w_out': nrm((N_C, E, D), E ** -0.5),
    }


def reference(x_prompt, x_sample, cache_k, cache_v, state_rwkv, state_s5, c, c_ctx,
              ada_w, ada_b, norm_pre, norm_post, ffn_w1, ffn_w3, ffn_w2,
              ab_w_in, ab_w_out, attn_q_gain, attn_k_gain,
              rwkv_mu, rwkv_w0, rwkv_w_up, rwkv_a0, rwkv_a_up, rwkv_g_up,
              rwkv_k_k, rwkv_k_a, rwkv_r_k, rwkv_ln_w, rwkv_ln_b,
              s5_w_in, s5_lambda_re, s5_lambda_im, s5_log_dt, s5_b_re, s5_b_im,
              s5_c_re, s5_c_im, s5_d, s5_w_glu, s5_w_out):
    P = dict(norm_pre=norm_pre, norm_post=norm_post, ffn_w1=ffn_w1, ffn_w3=ffn_w3, ffn_w2=ffn_w2,
             ab_w_in=ab_w_in, ab_w_out=ab_w_out, attn_q_gain=attn_q_gain, attn_k_gain=attn_k_gain,
             rwkv_mu=rwkv_mu, rwkv_w0=rwkv_w0, rwkv_w_up=rwkv_w_up, rwkv_a0=rwkv_a0,
             rwkv_a_up=rwkv_a_up, rwkv_g_up=rwkv_g_up, rwkv_k_k=rwkv_k_k, rwkv_k_a=rwkv_k_a,
             rwkv_r_k=rwkv_r_k, rwkv_ln_w=rwkv_ln_w, rwkv_ln_b=rwkv_ln_b,
             s5_w_in=s5_w_in, s5_lambda_re=s5_lambda_re, s5_lambda_im=s5_lambda_im,
             s5_log_dt=s5_log_dt, s5_b_re=s5_b_re, s5_b_im=s5_b_im, s5_c_re=s5_c_re,
             s5_c_im=s5_c_im, s5_d=s5_d, s5_w_glu=s5_w_glu, s5_w_out=s5_w_out)
    mods_ctx = adaln(c_ctx[None], ada_w, ada_b)
    mods_lat = adaln(c, ada_w, ada_b)
    y_prompt, ctx_out = run_trunk(x_prompt, mods_ctx, P, None)
    new_k, new_v, new_rwkv, new_s5 = ctx_out
    y_sample, _ = run_trunk(x_sample, mods_lat, P, (cache_k, cache_v, state_rwkv, state_s5))
    return (y_prompt, y_sample, new_k, new_v, new_rwkv, new_s5)
```

```python
import contextlib
import numpy as np
import concourse.bass as bass
import concourse.mybir as mybir
from concourse.bass_utils import run_bass_kernel_spmd

F32 = mybir.dt.float32
F32R = mybir.dt.float32r
AF = mybir.ActivationFunctionType
ALU = mybir.AluOpType
AX = mybir.AxisListType

EPOCH = 20000
DMA_SEMS = 32
SAME_ENG_WINDOW = 3


class Buf:
    __slots__ = ("name", "w", "r")

    def __init__(self, name):
        self.name = name
        self.w = None
        self.r = []


class Op:
    __slots__ = ("eng", "fn", "deps", "idx", "eidx", "signal", "sem", "val", "dma", "slot")


class Sched:
    ENGS = ("pe", "act", "dve", "pool", "sp")

    def __init__(self, nc):
        self.nc = nc
        self.ops = []
        self.eng_ops = {e: [] for e in self.ENGS}

    def op(self, eng, fn, reads=(), writes=(), dma=False):
        o = Op()
        o.eng, o.fn, o.dma = eng, fn, dma
        o.idx = len(self.ops)
        o.eidx = len(self.eng_ops[eng])
        deps = set()
        for b in reads:
            if b.w is not None:
                deps.add(b.w)
        for b in writes:
            if b.w is not None:
                deps.add(b.w)
            deps.update(b.r)
        o.deps = deps
        o.signal = False
        o.sem = None
        o.val = 0
        o.slot = 0
        for b in reads:
            b.r.append(o.idx)
        for b in writes:
            b.w = o.idx
            b.r = []
        self.ops.append(o)
        self.eng_ops[eng].append(o)
        return o

    def finalize(self, stack):
        nc = self.nc
        ops = self.ops
        for o in ops:
            nd = []
            for d in o.deps:
                p = ops[d]
                if p.eng == o.eng and not p.dma:
                    if o.eng == "pe" and not o.dma:
                        continue
                    if o.dma or (o.eidx - p.eidx) <= SAME_ENG_WINDOW:
                        nd.append(d)
                    continue
                nd.append(d)
            o.deps = nd
            for d in nd:
                ops[d].signal = True
        for o in ops:
            if o.dma:
                o.signal = True
        self.final_waits = []
        for e in self.ENGS:
            cnt = 0
            sems = []
            dsems = [stack.enter_context(nc.semaphore(f"d_{e}_{i}")) for i in range(DMA_SEMS)] \
                if any(o.dma for o in self.eng_ops[e]) else []
            dcnt = [0] * DMA_SEMS
            dprev = [None] * DMA_SEMS
            k = 0
            for o in self.eng_ops[e]:
                if o.dma:
                    s = k % DMA_SEMS
                    k += 1
                    if dprev[s] is not None:
                        o.deps.append(dprev[s])
                    dcnt[s] += 16
                    o.sem, o.val = dsems[s], dcnt[s]
                    dprev[s] = o.idx
                elif o.signal:
                    ep = cnt // EPOCH
                    if ep >= len(sems):
                        sems.append(stack.enter_context(nc.semaphore(f"e_{e}_{ep}")))
                    o.sem, o.val = sems[ep], (cnt % EPOCH) + 1
                    cnt += 1
            for s in range(DMA_SEMS):
                if dcnt[s]:
                    self.final_waits.append((dsems[s], dcnt[s]))
            if cnt:
                self.final_waits.append((sems[-1], ((cnt - 1) % EPOCH) + 1))

    def emit(self, block):
        nc = self.nc
        ops = self.ops

        def run(eng_name, eng, last=False):
            known = {}
            for o in self.eng_ops[eng_name]:
                need = {}
                for d in o.deps:
                    p = ops[d]
                    key = id(p.sem)
                    if known.get(key, 0) >= p.val:
                        continue
                    if key not in need or need[key][1] < p.val:
                        need[key] = (p.sem, p.val)
                for key, (sem, val) in need.items():
                    eng.wait_ge(sem, val)
                    known[key] = val
                ins = o.fn(eng)
                if o.signal:
                    ins.then_inc(o.sem, 16 if o.dma else 1)
            if last:
                for sem, val in self.final_waits:
                    eng.wait_ge(sem, val)

        @block.tensor
        def _(e):
            run("pe", e)

        @block.scalar
        def _(e):
            run("act", e)

        @block.vector
        def _(e):
            run("dve", e)

        @block.gpsimd
        def _(e):
            run("pool", e)

        @block.sync
        def _(e):
            run("sp", e, last=True)


def _barrier(self):
    deps = set()
    for e in self.ENGS:
        if self.eng_ops[e]:
            deps.add(self.eng_ops[e][-1].idx)
    for o in self.ops[getattr(self, "_bar_from", 0):]:
        if o.dma:
            deps.add(o.idx)
    self._bar_from = len(self.ops)
    for e in self.ENGS:
        o = self.op(e, (lambda eng: eng.nop()))
        o.deps = set(deps)


Sched.barrier = _barrier


D = 1024
FF = 2816
KC = 8
FC = 22
TN = 512
HD = 64
RMS_EPS = 1e-6
GN_EPS = 64e-5
TWO_PI = 6.283185307179586
PI = 3.141592653589793


class Cfg:
    def __init__(self, SEQ=256, DEC_SEQ=4096, DEPTH=4, PAST=256, NPS=2):
        self.SEQ, self.DEC_SEQ, self.DEPTH, self.PAST, self.NPS = SEQ, DEC_SEQ, DEPTH, PAST, NPS
        self.TP = SEQ * NPS
        self.TALL = self.TP + DEC_SEQ
        self.NTILES = self.TALL // TN
        self.NAB = (DEPTH + 1) // 2
        self.NCL = DEPTH // 2
        assert self.TP == TN and DEC_SEQ % TN == 0


class KB:
    def __init__(self, cfg):
        self.cfg = cfg
        self.nc = bass.Bass("TRN2", target_bir_lowering=False)
        self.S = Sched(self.nc)
        self.din = {}
        self.dout = {}
        self.bufs = {}

    def inp(self, name, shape):
        t = self.nc.dram_tensor(name, list(shape), F32, kind="ExternalInput").ap()
        self.din[name] = (t, tuple(shape))
        self.bufs[name] = Buf(name)
        return t

    def outp(self, name, shape):
        t = self.nc.dram_tensor(name, list(shape), F32, kind="ExternalOutput").ap()
        self.dout[name] = (t, tuple(shape))
        self.bufs[name] = Buf(name)
        return t

    def scratch(self, name, shape):
        t = self.nc.dram_tensor(name, list(shape), F32, kind="Internal").ap()
        self.bufs[name] = Buf(name)
        return t

    def B(self, name):
        if name not in self.bufs:
            self.bufs[name] = Buf(name)
        return self.bufs[name]

    def bl(self, names):
        return [self.B(n) if isinstance(n, str) else n for n in names]

    def MM(self, out, lhsT, rhs, start, stop, R, W):
        self.S.op("pe", lambda e: e.matmul(out, lhsT, rhs, start=start, stop=stop), self.bl(R), self.bl(W))

    def TR(self, out, in_, ident, R, W):
        self.S.op("pe", lambda e: e.transpose(out, in_, ident), self.bl(R), self.bl(W))

    def ACT(self, out, in_, func, R, W, scale=1.0, bias=None):
        if bias is None:
            self.S.op("act", lambda e: e.activation(out=out, in_=in_, func=func, scale=scale), self.bl(R), self.bl(W))
        else:
            self.S.op("act", lambda e: e.activation(out=out, in_=in_, func=func, scale=scale, bias=bias),
                      self.bl(R), self.bl(W))

    def TT(self, out, a, b, op, R, W, eng="dve"):
        self.S.op(eng, lambda e: e.tensor_tensor(out=out, in0=a, in1=b, op=op), self.bl(R), self.bl(W))

    def TS(self, out, a, s1, s2, op0, op1, R, W):
        if s2 is None:
            self.S.op("dve", lambda e: e.tensor_scalar(out=out, in0=a, scalar1=s1, scalar2=None, op0=op0),
                      self.bl(R), self.bl(W))
        else:
            self.S.op("dve", lambda e: e.tensor_scalar(out=out, in0=a, scalar1=s1, scalar2=s2, op0=op0, op1=op1),
                      self.bl(R), self.bl(W))

    def STT(self, out, a, s, b, op0, op1, R, W):
        self.S.op("dve", lambda e: e.scalar_tensor_tensor(out=out, in0=a, scalar=s, in1=b, op0=op0, op1=op1),
                  self.bl(R), self.bl(W))

    def RED(self, out, in_, R, W):
        self.S.op("dve", lambda e: e.tensor_reduce(out=out, in_=in_, axis=AX.X, op=ALU.add), self.bl(R), self.bl(W))

    def RCP(self, out, in_, R, W):
        self.S.op("dve", lambda e: e.reciprocal(out=out, in_=in_), self.bl(R), self.bl(W))

    def CP(self, out, in_, R, W, eng="dve"):
        self.S.op(eng, lambda e: e.tensor_copy(out=out, in_=in_), self.bl(R), self.bl(W))

    def MSET(self, out, val, W, eng="dve"):
        self.S.op(eng, lambda e: e.memset(out, val), [], self.bl(W))

    def SCAN(self, out, d0, d1, init, R, W):
        self.S.op("dve", lambda e: e.tensor_tensor_scan(out=out, data0=d0, data1=d1, initial=init,
                                                       op0=ALU.mult, op1=ALU.add), self.bl(R), self.bl(W))

    def DMA(self, out, in_, R, W, q="sp"):
        self.S.op(q, lambda e: e.dma_start(out=out, in_=in_), self.bl(R), self.bl(W), dma=True)

    def arena_reset(self):
        self.S.barrier()
        self.aoff = 0
        self.gen = getattr(self, "gen", 0) + 1

    def alloc(self, name, shape):
        n = int(np.prod(shape[1:]))
        v = self.AR[0:shape[0], self.aoff:self.aoff + n]
        self.aoff += n
        assert self.aoff <= self.ARN, (name, self.aoff)
        if len(shape) == 3:
            v = v.rearrange("p (a b) -> p a b", a=shape[1])
        elif len(shape) == 4:
            v = v.rearrange("p (a b c) -> p a b c", a=shape[1], b=shape[2])
        return v, self.B(f"{name}@{self.gen}")

    def build(self):
        cfg, nc = self.cfg, self.nc
        L, TALL, NTL = cfg.DEPTH, cfg.TALL, cfg.NTILES
        st = contextlib.ExitStack()
        self.st = st
        I = self.inp
        self.xT = I("xT", [128, 8, TALL])
        self.condT = I("condT", [128, 8, 2])
        self.ada_w = I("ada_w", [L, 1024, 9216])
        self.ada_bT = I("ada_bT", [128, L * 72])
        self.npreT = I("npreT", [128, L * 24])
        self.npostT = I("npostT", [128, L * 24])
        self.w1L = I("w1L", [L * 2 * FC, 128, 1024])
        self.w3L = I("w3L", [L * 2 * FC, 128, 1024])
        self.w2L = I("w2L", [L * 2 * 8, 128, FC * 128])
        self.consts = I("consts", [128, 1024])
        self.yT = self.outp("yT", [128, 8, TALL])
        NCL, NPS = cfg.NCL, cfg.NPS
        self.s5inL = I("s5inL", [max(NCL, 1) * 8, 128, 1024])
        self.s5gluL = I("s5gluL", [max(NCL, 1) * 8, 128, 1024])
        self.s5outL = I("s5outL", [max(NCL, 1) * 8, 128, 1024])
        self.s5lam = I("s5lam", [max(NCL, 1), 128, 3, 64])
        self.s5s0 = I("s5s0", [max(NCL, 1), 128, 2, 64])
        self.s5b = I("s5b", [max(NCL, 1) * 4, 4096, 16])
        self.s5c = I("s5c", [max(NCL, 1) * 4, 4096, 16])
        self.s5d = I("s5d", [max(NCL, 1), 128, 8])
        self.news5 = self.outp("news5", [max(NCL, 1) * NPS * 4, 128, 32])
        self.uT = self.scratch("uT", [128, 8, TALL])
        self.yfT = self.scratch("yfT", [128, 8, TALL])
        self.zT = self.scratch("zT", [128, 8, TALL])
        self.even_decl()
        self.I32 = st.enter_context(nc.sbuf_tensor("i32", [128, 512], mybir.dt.int32))
        self.xres = self.scratch("xres", [128, 8, TALL])
        self.ARN = 49000
        self.AR = st.enter_context(nc.sbuf_tensor("arena", [128, self.ARN], F32))
        self.CT = st.enter_context(nc.sbuf_tensor("cten", [128, 1024], F32))
        self.MOD = st.enter_context(nc.sbuf_tensor("mods", [128, L * 3 * 3 * 8 * 2], F32))
        self.CS = st.enter_context(nc.sbuf_tensor("cst", [128, 8], F32))
        self.PS = [st.enter_context(nc.psum_tensor(f"ps{i}", [128, 512], F32)) for i in range(8)]
        self.pb = [self.B(f"psum{i}") for i in range(8)]
        self.ident = self.CT[:, 0:128]
        self.ones = self.CT[:, 128:256]
        self.bones = self.CT[:, 256:384]
        self.J = self.CT[:, 384:512]
        self.DMA(self.CT[:], self.consts[:, :], [], ["CT"])
        self.MSET(self.CS[:, 0:1], RMS_EPS, ["CS"])
        self.MSET(self.CS[:, 1:2], GN_EPS, ["CS"])
        self.MSET(self.CS[:, 2:3], 1.0, ["CS"])
        self.MSET(self.CS[:, 3:4], 0.0, ["CS"])
        self.DMA(self.xres[:, :, :], self.xT[:, :, :], [], ["xres"])
        self.gen = 0
        self.aoff = 0
        self.adaln()
        for l in range(L):
            self.ffn_sub(l, 0)
            if l % 2 == 0:
                self.mixer_even(l)
            else:
                self.mixer_odd(l)
            self.ffn_sub(l, 2)
        self.arena_reset()
        self.DMA(self.yT[:, :, :], self.xres[:, :, :], ["xres"], ["yT"], q="pool")
        self.S.finalize(st)
        with nc.Block() as block:
            self.S.emit(block)
        st.close()
        return nc

    def modv(self, l, s, j, cond):
        base = (((l * 3 + s) * 3 + j) * 8) * 2
        return self.MOD[:, base:base + 16].rearrange("p (e c) -> p e c", c=2)[:, :, cond]

    def adaln(self):
        cfg = self.cfg
        L = cfg.DEPTH
        self.arena_reset()
        sc, bsc = self.alloc("sc", [128, 8, 2])
        sg, bsg = self.alloc("sg", [128, 8, 2])
        ab, bab = self.alloc("ab", [128, L * 72])
        pre, bpre = self.alloc("pre", [128, L * 24])
        post, bpost = self.alloc("post", [128, L * 24])
        raw, braw = self.alloc("raw", [128, L * 9 * 8 * 2])
        wbuf = [self.alloc(f"adw{i}", [128, 8, 1024]) for i in range(2)]
        self.DMA(sc, self.condT[:, :, :], [], [bsc])
        self.DMA(ab, self.ada_bT[:, :], [], [bab])
        self.DMA(pre, self.npreT[:, :], [], [bpre])
        self.DMA(post, self.npostT[:, :], [], [bpost])
        self.ACT(sg, sc, AF.Silu, [bsc], [bsg])
        n = 0
        for l in range(L):
            for sj in range(9):
                wv, wb = wbuf[n % 2]
                src = self.ada_w[l].rearrange("(k p) e -> p k e", p=128)[:, :, sj * 1024:(sj + 1) * 1024]
                self.DMA(wv[:, 0:4, :], src[:, 0:4, :], [], [wb])
                self.DMA(wv[:, 4:8, :], src[:, 4:8, :], [], [wb], q="pool")
                ps = self.PS[n % 2]
                pbuf = self.pb[n % 2]
                for e in range(8):
                    for k in range(8):
                        self.MM(ps[:, e * 2:e * 2 + 2], wv[:, k, e * 128:(e + 1) * 128], sg[:, k, :],
                                k == 0, k == 7, [wb, bsg], [pbuf])
                o = (l * 9 + sj) * 16
                bo = (l * 9 + sj) * 8
                self.TT(raw[:, o:o + 16].rearrange("p (e c) -> p e c", c=2),
                        ps[:, 0:16].rearrange("p (e c) -> p e c", c=2),
                        ab[:, bo:bo + 8].unsqueeze(2).broadcast_to([128, 8, 2]), ALU.add,
                        [pbuf, bab], [braw])
                n += 1
        for l in range(L):
            for s in range(3):
                def rawv(j):
                    o = (l * 9 + s * 3 + j) * 16
                    return raw[:, o:o + 16].rearrange("p (e c) -> p e c", c=2)
                def modw(j):
                    base = (((l * 3 + s) * 3 + j) * 8) * 2
                    return self.MOD[:, base:base + 16].rearrange("p (e c) -> p e c", c=2)
                g8 = (l * 3 + s) * 8
                pg = pre[:, g8:g8 + 8].unsqueeze(2).broadcast_to([128, 8, 2])
                qg = post[:, g8:g8 + 8].unsqueeze(2).broadcast_to([128, 8, 2])
                self.STT(modw(0), rawv(1), 1.0, pg, ALU.add, ALU.mult, [braw, bpre], ["MOD"])
                self.CP(modw(1), rawv(0), [braw], ["MOD"])
                self.STT(modw(2), rawv(2), 0.5 if s != 1 else 1.0, qg, ALU.mult, ALU.mult, [braw, bpost], ["MOD"])

    def rstd_of(self, src, bsrc, name):
        sq, bsq = self.alloc(name + "_sq", [128, 8, TN])
        r, br = self.alloc(name + "_r", [128, TN])
        self.TT(sq, src, src, ALU.mult, [bsrc], [bsq])
        ps, pbuf = self.PS[7], self.pb[7]
        for c in range(8):
            self.MM(ps[:, :], self.ones, sq[:, c, :], c == 0, c == 7, ["CT", bsq], [pbuf])
        self.ACT(r, ps[:, :], AF.Sqrt, [pbuf, "CS"], [br], scale=1.0 / D, bias=self.CS[:, 0:1])
        self.RCP(r, r, [br], [br])
        return r, br

    def modin(self, x, bx, l, s, cond, name="h"):
        r, br = self.rstd_of(x, bx, name)
        h, bh = self.alloc(name, [128, 8, TN])
        A = self.modv(l, s, 0, cond)
        Bv = self.modv(l, s, 1, cond)
        for c in range(8):
            self.STT(h[:, c, :], x[:, c, :], A[:, c:c + 1], r, ALU.mult, ALU.mult, [bx, br, "MOD"], [bh])
            self.TS(h[:, c, :], h[:, c, :], Bv[:, c:c + 1], None, ALU.add, None, [bh, "MOD"], [bh])
        return h, bh

    def resid(self, x, bx, y, by, l, s, cond):
        r, br = self.rstd_of(y, by, "yn")
        G = self.modv(l, s, 2, cond)
        for c in range(8):
            self.STT(y[:, c, :], y[:, c, :], G[:, c:c + 1], r, ALU.mult, ALU.mult, [by, br, "MOD"], [by])
        self.TT(x, x, y, ALU.add, [bx, by], [bx])

    def tile_cond(self, t):
        return 0 if t == 0 else 1

    def ffn_sub(self, l, s):
        i = 0 if s == 0 else 1
        for t in range(self.cfg.NTILES):
            cond = self.tile_cond(t)
            self.arena_reset()
            x, bx = self.alloc("x", [128, 8, TN])
            self.DMA(x, self.xres[:, :, t * TN:(t + 1) * TN], ["xres"], [bx])
            h, bh = self.modin(x, bx, l, s, cond)
            a, ba = self.alloc("a", [128, FC, TN])
            y, by = self.alloc("y", [128, 8, TN])
            sl, bsl = self.alloc("sl", [128, TN])
            w1 = [self.alloc(f"w1_{k}", [128, 8, 128]) for k in range(2)]
            w3 = [self.alloc(f"w3_{k}", [128, 8, 128]) for k in range(2)]
            w2 = [self.alloc(f"w2_{k}", [128, FC, 128]) for k in range(2)]
            for j in range(FC):
                (w1v, w1b), (w3v, w3b) = w1[j % 2], w3[j % 2]
                row = (l * 2 + i) * FC + j
                self.DMA(w1v, self.w1L[row].rearrange("p (k m) -> p k m", k=8), [], [w1b])
                self.DMA(w3v, self.w3L[row].rearrange("p (k m) -> p k m", k=8), [], [w3b], q="pool")
                pg, pu = self.PS[(j % 2) * 2], self.PS[(j % 2) * 2 + 1]
                bg, bu = self.pb[(j % 2) * 2], self.pb[(j % 2) * 2 + 1]
                for k in range(8):
                    self.MM(pg[:, :], w1v[:, k, :], h[:, k, :], k == 0, k == 7, [w1b, bh], [bg])
                for k in range(8):
                    self.MM(pu[:, :], w3v[:, k, :], h[:, k, :], k == 0, k == 7, [w3b, bh], [bu])
                self.ACT(sl, pg[:, :], AF.Silu, [bg], [bsl])
                self.TT(a[:, j, :], sl, pu[:, :], ALU.mult, [bsl, bu], [ba])
            for e in range(8):
                w2v, w2b = w2[e % 2]
                row = (l * 2 + i) * 8 + e
                w2src = self.w2L[row].rearrange("p (k m) -> p k m", k=FC)
                self.DMA(w2v[:, 0:11, :], w2src[:, 0:11, :], [], [w2b])
                self.DMA(w2v[:, 11:FC, :], w2src[:, 11:FC, :], [], [w2b], q="pool")
                py, bpy = self.PS[4 + e % 2], self.pb[4 + e % 2]
                for k in range(FC):
                    self.MM(py[:, :], w2v[:, k, :], a[:, k, :], k == 0, k == FC - 1, [w2b, ba], [bpy])
                self.ACT(y[:, e, :], py[:, :], AF.Copy, [bpy], [by])
            self.resid(x, bx, y, by, l, s, cond)
            self.DMA(self.xres[:, :, t * TN:(t + 1) * TN], x, [bx], ["xres"], q="pool")

    def proj8(self, wL, row0, src, bsrc, dst, bdst, wname="wq"):
        wb = [self.alloc(f"{wname}{k}", [128, 8, 128]) for k in range(2)]
        for e in range(8):
            wv, wbb = wb[e % 2]
            self.DMA(wv, wL[row0 + e].rearrange("p (k m) -> p k m", k=8), [], [wbb])
            ps, pbuf = self.PS[e % 2], self.pb[e % 2]
            for k in range(8):
                self.MM(ps[:, :], wv[:, k, :], src[:, k, :], k == 0, k == 7, [wbb, bsrc], [pbuf])
            self.ACT(dst[:, e, :], ps[:, :], AF.Copy, [pbuf], [bdst])

    def sincos(self, th, bth, n, name):
        out = []
        for which, shift in (("s", 0.0), ("c", PI / 2)):
            a, ba = self.alloc(f"{name}_{which}", [128, n])
            k, bk = self.alloc(f"{name}_{which}k", [128, n])
            ki = self.I32[:, 0:n]
            self.TS(a, th, shift, None, ALU.add, None, [bth], [ba])
            self.TS(k, a, 1.0 / TWO_PI, None, ALU.mult, None, [ba], [bk])
            self.CP(ki, k, [bk], ["I32"])
            self.CP(k, ki, ["I32"], [bk])
            self.STT(a, k, -TWO_PI, a, ALU.mult, ALU.add, [bk, ba], [ba])
            self.TS(k, a, PI, -TWO_PI, ALU.is_gt, ALU.mult, [ba], [bk])
            self.TT(a, a, k, ALU.add, [ba, bk], [ba])
            self.TS(k, a, -PI, TWO_PI, ALU.is_lt, ALU.mult, [ba], [bk])
            self.TT(a, a, k, ALU.add, [ba, bk], [ba])
            self.ACT(a, a, AF.Sin, [ba], [ba])
            out.append((a, ba))
        return out[0], out[1]

    def cmul(self, outr, outi, ar, ai, br_, bi_, tmp, R, W, btmp, eng="dve"):
        self.TT(tmp, ai, bi_, ALU.mult, R, [btmp], eng=eng)
        self.TT(outr, ar, br_, ALU.mult, R, W, eng=eng)
        self.TT(outr, outr, tmp, ALU.subtract, list(W) + [btmp], W, eng=eng)
        self.TT(tmp, ai, br_, ALU.mult, R, [btmp], eng=eng)
        self.TT(outi, ar, bi_, ALU.mult, R, W, eng=eng)
        self.TT(outi, outi, tmp, ALU.add, list(W) + [btmp], W, eng=eng)

    def cmul_s(self, outr, outi, sr, si, br_, bi_, tmp1, R, W, btmp):
        self.TS(tmp1, bi_, si, None, ALU.mult, None, R, [btmp])
        self.STT(outr, br_, sr, tmp1, ALU.mult, ALU.subtract, list(R) + [btmp], W)
        self.TS(tmp1, br_, si, None, ALU.mult, None, R, [btmp])
        self.STT(outi, bi_, sr, tmp1, ALU.mult, ALU.add, list(R) + [btmp], W)

    def mixer_odd(self, l):
        cfg = self.cfg
        i = l // 2
        NTL = cfg.NTILES
        for t in range(NTL):
            self.arena_reset()
            x, bx = self.alloc("x", [128, 8, TN])
            self.DMA(x, self.xres[:, :, t * TN:(t + 1) * TN], ["xres"], [bx])
            h, bh = self.modin(x, bx, l, 1, self.tile_cond(t))
            u, bu = self.alloc("u", [128, 8, TN])
            self.proj8(self.s5inL, i * 8, h, bh, u, bu)
            self.DMA(self.uT[:, :, t * TN:(t + 1) * TN], u, [bu], ["uT"], q="pool")
        seqs = [(s * cfg.SEQ, cfg.SEQ, None, s) for s in range(cfg.NPS)] + [(cfg.TP, cfg.DEC_SEQ, i, None)]
        for d in range(2):
            self.arena_reset()
            NB = 32
            lam, blam = self.alloc("lam", [128, 6, 64])
            self.DMA(lam[:, 0:3, :], self.s5lam[i], [], [blam])
            self.DMA(lam[:, 3:5, :], self.s5s0[i], [], [blam])
            cs = slice(d * 32, d * 32 + 32)
            dsc, bdsc = self.alloc("dsc", [128, 12, 32])
            lr, li, ldt = lam[:, 0, cs], lam[:, 1, cs], lam[:, 2, cs]
            dt, lrdt, th, den, abr, abi, fr, fi, nlrdt, t0_, t1_, mag = [dsc[:, k, :] for k in range(12)]
            R0 = [blam, bdsc]
            self.ACT(dt, ldt, AF.Exp, [blam], [bdsc])
            self.TT(lrdt, lr, dt, ALU.mult, R0, [bdsc])
            self.TS(nlrdt, lrdt, -1.0, None, ALU.mult, None, R0, [bdsc])
            self.TT(th, li, dt, ALU.mult, R0, [bdsc])
            self.TT(den, lr, lr, ALU.mult, R0, [bdsc])
            self.TT(t0_, li, li, ALU.mult, R0, [bdsc])
            self.TT(den, den, t0_, ALU.add, R0, [bdsc])
            self.RCP(den, den, R0, [bdsc])
            self.ACT(mag, lrdt, AF.Exp, R0, [bdsc])
            (sn, bsn), (cn, bcn) = self.sincos(th, bdsc, 32, "ab")
            self.TT(abr, mag, cn, ALU.mult, [bdsc, bcn], [bdsc])
            self.TT(abi, mag, sn, ALU.mult, [bdsc, bsn], [bdsc])
            self.TS(t0_, abr, -1.0, None, ALU.add, None, R0, [bdsc])
            self.TT(fr, t0_, lr, ALU.mult, R0, [bdsc])
            self.TT(t1_, abi, li, ALU.mult, R0, [bdsc])
            self.TT(fr, fr, t1_, ALU.add, R0, [bdsc])
            self.TT(fr, fr, den, ALU.mult, R0, [bdsc])
            self.TT(fi, abi, lr, ALU.mult, R0, [bdsc])
            self.TT(t1_, t0_, li, ALU.mult, R0, [bdsc])
            self.TT(fi, fi, t1_, ALU.subtract, R0, [bdsc])
            self.TT(fi, fi, den, ALU.mult, R0, [bdsc])
            POSr, bT = self.alloc("POSr", [128, NB, 128])
            POSi, _ = self.alloc("POSi", [128, NB, 128])
            NEGr, _ = self.alloc("NEGr", [128, NB, 128])
            NEGi, _ = self.alloc("NEGi", [128, NB, 128])
            BBr, bBB = self.alloc("BBr", [128, NB, 128])
            BBi, _ = self.alloc("BBi", [128, NB, 128])
            CCr, bCC = self.alloc("CCr", [128, NB, 128])
            CCi, _ = self.alloc("CCi", [128, NB, 128])
            A128, bA128 = self.alloc("A128", [128, 2, NB])
            iota = self.CT[:, 512:640]
            mark = self.aoff
            for pb in range(NB):
                self.aoff = mark
                ang, bang = self.alloc("ang", [128, 128])
                mp, bmp = self.alloc("mp", [128, 128])
                self.TS(ang, iota, th[:, pb:pb + 1], None, ALU.mult, None, ["CT", bdsc], [bang])
                (sn, bsn), (cn, bcn) = self.sincos(ang, bang, 128, "tb")
                self.ACT(mp, iota, AF.Exp, ["CT", bdsc], [bmp], scale=lrdt[:, pb:pb + 1])
                self.TT(POSr[:, pb, :], mp, cn, ALU.mult, [bmp, bcn], [bT])
                self.TT(POSi[:, pb, :], mp, sn, ALU.mult, [bmp, bsn], [bT])
                self.ACT(mp, iota, AF.Exp, ["CT", bdsc], [bmp], scale=nlrdt[:, pb:pb + 1])
                self.TT(NEGr[:, pb, :], mp, cn, ALU.mult, [bmp, bcn], [bT])
                self.STT(NEGi[:, pb, :], mp, -1.0, sn, ALU.mult, ALU.mult, [bmp, bsn], [bT])
                a128t, ba128t = self.alloc("a128t", [128, 1])
                self.cmul_s(A128[:, 0, pb:pb + 1], A128[:, 1, pb:pb + 1], abr[:, pb:pb + 1], abi[:, pb:pb + 1],
                            POSr[:, pb, 127:128], POSi[:, pb, 127:128], a128t, [bT, bdsc], [bA128], ba128t)
                braw, bbraw = self.alloc("braw", [128, 2, 16])
                self.DMA(braw[:, 0, :], self.s5b[(i * 2 + d) * 2 + 0][pb * 128:(pb + 1) * 128, :], [], [bbraw])
                self.DMA(braw[:, 1, :], self.s5b[(i * 2 + d) * 2 + 1][pb * 128:(pb + 1) * 128, :], [], [bbraw])
                srcr, bsr = self.alloc("srcr", [128, 128])
                srci, bsi = self.alloc("srci", [128, 128])
                tb, btb = self.alloc("tb16", [128, 16])
                self.MSET(srcr, 0.0, [bsr])
                self.MSET(srci, 0.0, [bsi])
                off = 32 * (pb % 4)
                frp, fip = fr[:, pb:pb + 1], fi[:, pb:pb + 1]
                for g2 in range(2):
                    rs = slice(g2 * 64, g2 * 64 + 64)
                    c0 = off + g2 * 16
                    self.TS(tb[rs, :], braw[rs, 1, :], fip[rs, :], None, ALU.mult, None, [bbraw, bdsc], [btb])
                    self.STT(srcr[rs, c0:c0 + 16], braw[rs, 0, :], frp[rs, :], tb[rs, :], ALU.mult, ALU.subtract,
                             [bbraw, bdsc, btb], [bsr])
                    self.TS(tb[rs, :], braw[rs, 0, :], fip[rs, :], None, ALU.mult, None, [bbraw, bdsc], [btb])
                    self.STT(srci[rs, c0:c0 + 16], braw[rs, 1, :], frp[rs, :], tb[rs, :], ALU.mult, ALU.add,
                             [bbraw, bdsc, btb], [bsi])
                self.TR(self.PS[0][:, 0:128], srcr, self.ident, [bsr, "CT"], [self.pb[0]])
                self.ACT(BBr[:, pb, :], self.PS[0][:, 0:128], AF.Copy, [self.pb[0]], [bBB])
                self.TR(self.PS[1][:, 0:128], srci, self.ident, [bsi, "CT"], [self.pb[1]])
                self.ACT(BBi[:, pb, :], self.PS[1][:, 0:128], AF.Copy, [self.pb[1]], [bBB])
                self.MSET(CCr[:, pb, :], 0.0, [bCC])
                self.MSET(CCi[:, pb, :], 0.0, [bCC])
                for g2 in range(2):
                    rs = slice(g2 * 64, g2 * 64 + 64)
                    c0 = off + g2 * 16
                    r0 = pb * 128 + g2 * 64
                    self.DMA(CCr[rs, pb, c0:c0 + 16], self.s5c[(i * 2 + d) * 2 + 0][r0:r0 + 64, :], [], [bCC])
                    self.DMA(CCi[rs, pb, c0:c0 + 16], self.s5c[(i * 2 + d) * 2 + 1][r0:r0 + 64, :], [], [bCC])
            self.TS(CCi, CCi, -1.0, None, ALU.mult, None, [bCC], [bCC])
            self.aoff = mark
            for (tok0, Ls, sidx, pidx) in seqs:
                self.aoff = mark
                self.S.barrier()
                NS = min(TN, Ls)
                nseg = Ls // NS
                nch = NS // 128
                hcr, bhc = self.alloc("hcr", [128, NB])
                hci, _ = self.alloc("hci", [128, NB])
                hout, bho = self.alloc("hout", [128, 2, NB])
                bhcs = [self.B(f"hc{pb_}@{self.gen}") for pb_ in range(NB)]
                if sidx is None:
                    self.MSET(hcr, 0.0, [bhc] + bhcs)
                    self.MSET(hci, 0.0, [bhc] + bhcs)
                else:
                    mr = abr if d == 0 else A128[:, 0, :]
                    mi = abi if d == 0 else A128[:, 1, :]
                    tq, btq = self.alloc("tq", [128, NB])
                    self.cmul(hcr, hci, mr, mi, lam[:, 3, cs], lam[:, 4, cs], tq, [blam, bdsc, bA128], [bhc] + bhcs, btq)
                u, bu = self.alloc("u", [128, 8, NS])
                yv, byv = self.alloc("yv", [128, 8, NS])
                XS = []
                for k_ in range(2):
                    XS.append((self.alloc(f"xr{k_}", [128, NS]), self.alloc(f"xi{k_}", [128, NS])[0],
                               self.alloc(f"pr{k_}", [128, NS]), self.alloc(f"pi{k_}", [128, NS])[0],
                               self.alloc(f"tm{k_}", [128, NS]), self.alloc(f"tm2{k_}", [128, NS]),
                               self.alloc(f"sm{k_}", [128, 8])))
                tm, btm = XS[0][4]
                dsk, bdsk = self.alloc("dsk", [128, 8])
                self.DMA(dsk, self.s5d[i], [], [bdsk])
                segs = list(range(nseg)) if d == 0 else list(range(nseg - 1, -1, -1))
                for sg_ in segs:
                    ta = tok0 + sg_ * NS
                    self.DMA(u, self.uT[:, :, ta:ta + NS], ["uT"], [bu])
                    if d == 1:
                        self.DMA(yv, self.yfT[:, :, ta:ta + NS], ["yfT"], [byv])
                    for uc in range(8):
                        py, bpy = self.PS[4 + uc % 2], self.pb[4 + uc % 2]
                        for q in range(4):
                            pb = uc * 4 + q
                            par_ = pb % 2
                            (xr, bxx), xi, (pr, bpp), pi_, (tm, btm), (tm2, btm2), (sm, bsm) = XS[par_]
                            bhc = bhcs[pb]
                            p0, p1 = self.PS[2 * par_], self.PS[2 * par_ + 1]
                            bp0, bp1 = self.pb[2 * par_], self.pb[2 * par_ + 1]
                            self.MM(p0[:, 0:NS], BBr[:, pb, :], u[:, uc, :], True, True, [bBB, bu], [bp0])
                            self.MM(p1[:, 0:NS], BBi[:, pb, :], u[:, uc, :], True, True, [bBB, bu], [bp1])
                            T1r, T1i = (NEGr, NEGi) if d == 0 else (POSr, POSi)
                            T2r, T2i = (POSr, POSi) if d == 0 else (NEGr, NEGi)
                            def bc(T):
                                return T[:, pb, :].unsqueeze(1).broadcast_to([128, nch, 128])
                            def v3(a_):
                                return a_.rearrange("p (c s) -> p c s", s=128)
                            self.cmul(v3(xr), v3(xi), bc(T1r), bc(T1i), v3(p0[:, 0:NS]), v3(p1[:, 0:NS]), v3(tm),
                                      [bT, bp0, bp1], [bxx], btm)
                            chs = list(range(nch)) if d == 0 else list(range(nch - 1, -1, -1))
                            cir, cii = hcr[:, pb:pb + 1], hci[:, pb:pb + 1]
                            a128r, a128i = A128[:, 0, pb:pb + 1], A128[:, 1, pb:pb + 1]
                            for c in chs:
                                cl = slice(c * 128, (c + 1) * 128)
                                e1 = c * 128 + 127
                                if d == 0:
                                    self.SCAN(pr[:, cl], self.ones, xr[:, cl], cir, ["CT", bxx, bhc], [bpp])
                                    self.SCAN(pi_[:, cl], self.ones, xi[:, cl], cii, ["CT", bxx, bhc], [bpp])
                                    self.cmul_s(cir, cii, a128r, a128i, pr[:, e1:e1 + 1], pi_[:, e1:e1 + 1], sm[:, 0:1],
                                                [bpp, bA128], [bhc], bsm)
                                else:
                                    self.SCAN(pr[:, cl], self.ones, xr[:, cl], 0.0, ["CT", bxx], [bpp])
                                    self.SCAN(pi_[:, cl], self.ones, xi[:, cl], 0.0, ["CT", bxx], [bpp])
                                    self.TT(sm[:, 4:5], pr[:, e1:e1 + 1], cir, ALU.add, [bpp, bhc], [bsm])
                                    self.TT(sm[:, 5:6], pi_[:, e1:e1 + 1], cii, ALU.add, [bpp, bhc], [bsm])
                                    self.STT(pr[:, cl], xr[:, cl], sm[:, 4:5], pr[:, cl], ALU.add, ALU.subtract, [bxx, bsm, bpp], [bpp])
                                    self.STT(pi_[:, cl], xi[:, cl], sm[:, 5:6], pi_[:, cl], ALU.add, ALU.subtract, [bxx, bsm, bpp], [bpp])
                                    s0 = c * 128
                                    if sg_ == segs[-1] and c == chs[-1]:
                                        self.CP(hout[:, 0, pb:pb + 1], pr[:, s0:s0 + 1], [bpp], [bho])
                                        self.CP(hout[:, 1, pb:pb + 1], pi_[:, s0:s0 + 1], [bpp], [bho])
                                    else:
                                        self.cmul_s(cir, cii, a128r, a128i, pr[:, s0:s0 + 1], pi_[:, s0:s0 + 1], sm[:, 0:1],
                                                    [bpp, bA128], [bhc], bsm)
                            if d == 0 and sg_ == segs[-1]:
                                e1 = (nch - 1) * 128 + 127
                                self.cmul(hout[:, 0, pb:pb + 1], hout[:, 1, pb:pb + 1], POSr[:, pb, 127:128], POSi[:, pb, 127:128],
                                          pr[:, e1:e1 + 1], pi_[:, e1:e1 + 1], sm[:, 3:4], [bT, bpp], [bho], bsm)
                            self.cmul(v3(xr), v3(xi), bc(T2r), bc(T2i), v3(pr), v3(pi_), v3(tm2), [bT, bpp], [bxx], btm2, eng="pool")
                            self.MM(py[:, 0:NS], CCr[:, pb, :], xr, q == 0, False, [bCC, bxx], [bpy])
                            self.MM(py[:, 0:NS], CCi[:, pb, :], xi, False, q == 3, [bCC, bxx], [bpy])
                        if d == 0:
                            self.ACT(yv[:, uc, :], py[:, 0:NS], AF.Copy, [bpy], [byv])
                        else:
                            self.TT(yv[:, uc, :], yv[:, uc, :], py[:, 0:NS], ALU.add, [byv, bpy], [byv])
                            self.STT(yv[:, uc, :], u[:, uc, :], dsk[:, uc:uc + 1], yv[:, uc, :], ALU.mult, ALU.add,
                                     [bu, bdsk, byv], [byv])
                            yy = yv[:, uc, :]
                            self.TT(tm, yy, yy, ALU.mult, [byv], [btm])
                            self.TS(tm, tm, 0.044715, 1.0, ALU.mult, ALU.add, [btm], [btm])
                            self.TT(tm, tm, yy, ALU.mult, [btm, byv], [btm])
                            self.ACT(tm, tm, AF.Tanh, [btm], [btm], scale=0.7978845608028654)
                            self.STT(yy, tm, 1.0, yy, ALU.add, ALU.mult, [btm, byv], [byv])
                            self.TS(yy, yy, 0.5, None, ALU.mult, None, [byv], [byv])
                        if d == 1 and uc == 0:
                            pass
                    if d == 0:
                        self.DMA(self.yfT[:, :, ta:ta + NS], yv, [byv], ["yfT"], q="pool")
                    else:
                        self.DMA(self.zT[:, :, ta:ta + NS], yv, [byv], ["zT"], q="pool")
                    if d == 0 and True:
                        pass
                if pidx is not None:
                    o = ((i * cfg.NPS + pidx) * 2 + d) * 2
                    self.DMA(self.news5[o + 0], hout[:, 0, :], [bho], ["news5"], q="pool")
                    self.DMA(self.news5[o + 1], hout[:, 1, :], [bho], ["news5"], q="pool")
        for t in range(NTL):
            self.arena_reset()
            x, bx = self.alloc("x", [128, 8, TN])
            z, bz = self.alloc("z", [128, 8, TN])
            gl, bgl = self.alloc("gl", [128, 8, TN])
            y, by = self.alloc("y", [128, 8, TN])
            self.DMA(x, self.xres[:, :, t * TN:(t + 1) * TN], ["xres"], [bx])
            self.DMA(z, self.zT[:, :, t * TN:(t + 1) * TN], ["zT"], [bz])
            self.proj8(self.s5gluL, i * 8, z, bz, gl, bgl)
            self.ACT(gl, gl, AF.Sigmoid, [bgl], [bgl])
            self.TT(z, z, gl, ALU.mult, [bz, bgl], [bz])
            self.proj8(self.s5outL, i * 8, z, bz, y, by, wname="wo")
            self.resid(x, bx, y, by, l, 1, self.tile_cond(t))
            self.DMA(self.xres[:, :, t * TN:(t + 1) * TN], x, [bx], ["xres"], q="pool")


def _lhsT(W):
    K, M = W.shape
    return np.ascontiguousarray(W.reshape(K // 128, 128, M).transpose(1, 0, 2))


def _fm(v):
    v = np.asarray(v)
    lead = v.shape[:-1]
    n = v.shape[-1] // 128
    a = v.reshape(lead + (n, 128))
    return np.ascontiguousarray(np.moveaxis(a, -1, 0))


def make_consts():
    c = np.zeros((128, 1024), np.float32)
    c[:, 0:128] = np.eye(128, dtype=np.float32)
    c[:, 128:256] = 1.0
    bo = np.zeros((128, 128), np.float32)
    bo[:64, :64] = 1.0
    bo[64:, 64:] = 1.0
    c[:, 256:384] = bo
    c[:, 384:512] = np.eye(128, dtype=np.float32)[::-1]
    c[:, 512:640] = np.arange(128, dtype=np.float32)[None, :]
    Rm = np.zeros((64, 64), np.float32)
    for j in range(16):
        Rm[j, 16 + j] = -1.0
        Rm[16 + j, j] = 1.0
        Rm[32 + j, 48 + j] = -1.0
        Rm[48 + j, 32 + j] = 1.0
    c[0:64, 640:704] = Rm.T
    for h in range(8):
        c[h, 704 + h * 16:704 + (h + 1) * 16] = 1.0
    for u in range(16):
        c[u, 832 + u * 8:832 + (u + 1) * 8] = 1.0
    c[0:64, 960] = 1.0
    c[64:128, 961] = 1.0
    return c


def common_inputs(cfg, P):
    L = cfg.DEPTH
    d = {}
    d["ada_w"] = np.ascontiguousarray(P["ada_w"], dtype=np.float32)
    ab = P["ada_b"].reshape(L, 9, 8, 128)
    d["ada_bT"] = np.ascontiguousarray(ab.transpose(3, 0, 1, 2).reshape(128, L * 72))
    d["npreT"] = np.ascontiguousarray(P["norm_pre"].reshape(L, 3, 8, 128).transpose(3, 0, 1, 2).reshape(128, L * 24))
    d["npostT"] = np.ascontiguousarray(P["norm_post"].reshape(L, 3, 8, 128).transpose(3, 0, 1, 2).reshape(128, L * 24))
    w1 = P["ffn_w1"].reshape(L * 2, 8, 128, FC, 128)
    d["w1L"] = np.ascontiguousarray(w1.transpose(0, 3, 2, 1, 4).reshape(L * 2 * FC, 128, 1024))
    w3 = P["ffn_w3"].reshape(L * 2, 8, 128, FC, 128)
    d["w3L"] = np.ascontiguousarray(w3.transpose(0, 3, 2, 1, 4).reshape(L * 2 * FC, 128, 1024))
    w2 = P["ffn_w2"].reshape(L * 2, FC, 128, 8, 128)
    d["w2L"] = np.ascontiguousarray(w2.transpose(0, 3, 2, 1, 4).reshape(L * 2 * 8, 128, FC * 128))
    d["consts"] = make_consts()
    return d


def core_inputs(cfg, core, x_prompt, x_sample, c, c_ctx):
    NPS = cfg.NPS
    xs = [x_prompt[core * NPS + i] for i in range(NPS)] + [x_sample[core % x_sample.shape[0]]]
    x = np.concatenate(xs, axis=0)
    d = {}
    d["xT"] = np.ascontiguousarray(x.T.reshape(8, 128, cfg.TALL).transpose(1, 0, 2))
    cond = np.stack([c_ctx, c[core % c.shape[0]]], axis=-1)
    d["condT"] = np.ascontiguousarray(cond.reshape(8, 128, 2).transpose(1, 0, 2))
    return d


def _sqL(W):
    n = W.shape[0]
    a = W.reshape(n, 8, 128, 8, 128)
    return np.ascontiguousarray(a.transpose(0, 3, 2, 1, 4).reshape(n * 8, 128, 1024))


def s5_common(cfg, P):
    d = {}
    d["s5inL"] = _sqL(P["s5_w_in"]); d["s5gluL"] = _sqL(P["s5_w_glu"]); d["s5outL"] = _sqL(P["s5_w_out"])
    N = cfg.NCL
    def pl(a):
        return a.reshape(N, 2, 32, 2, 64).transpose(0, 3, 4, 1, 2).reshape(N, 128, 64)
    ldt = np.broadcast_to(P["s5_log_dt"][..., None], (N, 2, 64, 64))
    d["s5lam"] = np.ascontiguousarray(np.stack([pl(P["s5_lambda_re"]), pl(P["s5_lambda_im"]), pl(ldt)], axis=2))
    b = np.stack([P["s5_b_re"], P["s5_b_im"]], axis=2)
    d["s5b"] = np.ascontiguousarray(b.reshape(N * 4, 4096, 16))
    c = np.stack([P["s5_c_re"], P["s5_c_im"]], axis=2)
    d["s5c"] = np.ascontiguousarray(c.transpose(0, 1, 2, 3, 5, 4).reshape(N * 4, 4096, 16))
    d["s5d"] = np.ascontiguousarray(P["s5_d"].reshape(N, 8, 128).transpose(0, 2, 1))
    return d


def s5_core(cfg, state_s5_b):
    N = cfg.NCL
    a = state_s5_b.reshape(N, 2, 2, 32, 2, 64)
    return {"s5s0": np.ascontiguousarray(a.transpose(0, 4, 5, 2, 1, 3).reshape(N, 128, 2, 64))}


def s5_unpack(cfg, news5):
    N, NPS = cfg.NCL, cfg.NPS
    a = news5.reshape(N, NPS, 2, 2, 2, 64, 32)
    return a.transpose(1, 0, 2, 3, 6, 4, 5).reshape(NPS, N, 2, 2, 64, 64)


def rope_tables(L, grid_w=64, theta=10000.0):
    rows = L // grid_w
    row = np.repeat(np.arange(rows), grid_w).astype(np.float32)
    col = np.tile(np.arange(grid_w), rows).astype(np.float32)
    inv = (np.float32(theta) ** (-np.arange(16, dtype=np.float32) * np.float32(2.0) / np.float32(32))).astype(np.float32)
    ar, ac = row[:, None] * inv, col[:, None] * inv
    ang = np.concatenate([ar, ar, ac, ac], axis=-1).astype(np.float32)
    return np.ascontiguousarray(np.stack([np.cos(ang).T, np.sin(ang).T], axis=1).astype(np.float32))


def _colsL(W, c0, m):
    a = W[:, c0:c0 + m].reshape(8, 128, m).transpose(1, 0, 2)
    return np.ascontiguousarray(a.reshape(128, 8 * m))


def even_common(cfg, P):
    N = cfg.NAB
    d = {}
    win = P["ab_w_in"]
    d["wqL"] = np.stack([_colsL(win[i], h * 64, 64) for i in range(N) for h in range(8)])
    d["wkL"] = np.stack([_colsL(win[i], 512 + h * 64, 64) for i in range(N) for h in range(2)])
    d["wvL"] = np.stack([_colsL(win[i], 640 + h * 64, 64) for i in range(N) for h in range(2)])
    d["wpbL"] = np.stack([_colsL(win[i], 768 + c * 128, 128) for i in range(N) for c in range(15)])
    d["woL"] = _sqL(P["ab_w_out"])
    d["qkg"] = np.ascontiguousarray(np.stack([P["attn_q_gain"], P["attn_k_gain"]], axis=-1))
    d["ropeT"] = rope_tables(cfg.DEC_SEQ)
    rwv = np.zeros((N, 128, 64), np.float32)
    for i in range(N):
        rwv[i, :, 0:15] = P["rwkv_mu"][i].reshape(15, 128).T
        rwv[i, :, 15:19] = P["rwkv_k_k"][i].reshape(4, 128).T
        rwv[i, :, 19:23] = P["rwkv_k_a"][i].reshape(4, 128).T
        rwv[i, :, 23:27] = P["rwkv_r_k"][i].reshape(4, 128).T
        rwv[i, :, 27:35] = P["rwkv_w0"][i].reshape(2, 4, 128).transpose(2, 0, 1).reshape(128, 8)
        rwv[i, :, 35:43] = P["rwkv_a0"][i].reshape(2, 4, 128).transpose(2, 0, 1).reshape(128, 8)
    d["rwv"] = rwv
    ups = []
    for i in range(N):
        ups += [P["rwkv_w_up"][i].reshape(128, 512), P["rwkv_a_up"][i].reshape(128, 512), P["rwkv_g_up"][i]]
    d["rwup"] = np.ascontiguousarray(np.stack(ups))
    d["rwln"] = np.ascontiguousarray(np.stack([v for i in range(N) for v in (P["rwkv_ln_w"][i], P["rwkv_ln_b"][i])]))
    return d


def even_core(cfg, cache_k_b, cache_v_b, state_rwkv_b):
    N = cfg.NAB
    d = {}
    d["ctxkT"] = np.ascontiguousarray(cache_k_b.transpose(0, 2, 3, 1).reshape(N * 2, 64, cfg.PAST))
    d["ctxv"] = np.ascontiguousarray(cache_v_b.transpose(0, 2, 1, 3).reshape(N * 2, cfg.PAST, 64))
    return d


def even_unpack(cfg, r):
    N, NPS, SEQ = cfg.NAB, cfg.NPS, cfg.SEQ
    nk = r["newk"].reshape(N, 2, 64, NPS, SEQ).transpose(3, 0, 4, 1, 2)
    nv = r["newv"].reshape(N, 2, 64, NPS, SEQ).transpose(3, 0, 4, 1, 2)
    return nk, nv, None


def rw_masks():
    s = np.arange(128)[:, None]
    tt = np.arange(128)[None, :]
    same = (s // 64) == (tt // 64)
    MUs = (same & (s < tt)).astype(np.float32)
    MUi = (same & (s <= tt)).astype(np.float32)
    MLs = (same & (s > tt)).astype(np.float32)
    MLi = (same & (s >= tt)).astype(np.float32)
    m01 = np.ones((128, 1024), np.float32)
    m01[:, ::64] = 0.0
    return np.ascontiguousarray(np.concatenate([MUs, MUi, MLs, MLi, -MUs, -MLs, m01], axis=1))


def rw_core(cfg, state_rwkv_b):
    N = cfg.NAB
    return {"rws0T": np.ascontiguousarray(state_rwkv_b.transpose(0, 4, 1, 2, 3).reshape(N, 64, 1024))}


def rw_unpack(cfg, newrwT):
    N, NPS = cfg.NAB, cfg.NPS
    a = newrwT.reshape(N, NPS, 64, 2, 8, 64)
    return a.transpose(1, 0, 3, 4, 5, 2)


def _apx(handle, offset, dims):
    return bass.AP(handle, offset, [list(d) for d in dims])


def _even_decl(self):
    cfg = self.cfg
    I = self.inp
    NAB, NPS, TALL = max(cfg.NAB, 1), cfg.NPS, cfg.TALL
    self.wqL = I("wqL", [NAB * 8, 128, 512])
    self.wkL = I("wkL", [NAB * 2, 128, 512])
    self.wvL = I("wvL", [NAB * 2, 128, 512])
    self.wpbL = I("wpbL", [NAB * 15, 128, 1024])
    self.woL = I("woL", [NAB * 8, 128, 1024])
    self.qkg = I("qkg", [NAB, 64, 2])
    self.ropeT = I("ropeT", [64, 2, cfg.DEC_SEQ])
    self.ctxkT = I("ctxkT", [NAB * 2, 64, cfg.PAST])
    self.ctxv = I("ctxv", [NAB * 2, cfg.PAST, 64])
    self.rwv = I("rwv", [NAB, 128, 64])
    self.rwup = I("rwup", [NAB * 3, 128, 512])
    self.rwln = I("rwln", [NAB * 2, 512])
    self.newk = self.outp("newk", [NAB * 2, 64, cfg.TP])
    self.newv = self.outp("newv", [NAB * 2, 64, cfg.TP])
    self.qT = self.scratch("qT", [8, 64, TALL])
    self.kT = self.scratch("kT", [2, 64, TALL])
    self.vtm = self.scratch("vtm", [2, TALL, 64])
    self.pbT = self.scratch("pbT", [128, 15, TALL])
    self.mixT = self.scratch("mixT", [128, 8, TALL])
    self.rmask = I("rmask", [128, 6 * 128 + 1024])
    self.rws0T = I("rws0T", [NAB, 64, 1024])
    self.newrwT = self.outp("newrwT", [NAB * NPS, 64, 1024])
    NBLK = TALL // 128
    self.rwQG = self.scratch("rwQG", [NBLK * 16, 64, 256])
    self.rwMW = self.scratch("rwMW", [NBLK * 16, 64, 256])
    self.rwV = self.scratch("rwV", [NBLK * 8, 64, 128])
    self.RM = self.st.enter_context(self.nc.sbuf_tensor("rmsk", [128, 6 * 128 + 1024], F32))
    self.DMA(self.RM[:, 0:896], self.rmask[:, 0:896], [], ["RM"])
    self.DMA(self.RM[:, 896:1792], self.rmask[:, 896:1792], [], ["RM"])
    self.tmq = {}
    for nm in ("r", "nkk", "v", "w0", "w1", "b0", "b1", "kd0", "kd1", "g", "bon", "y0", "y1"):
        self.tmq[nm] = self.scratch("tm_" + nm, [TALL, 512])


KB.even_decl = _even_decl


def _head_norm(self, ps, pbuf, gain, dst, bdst, n, name):
    (sq, bsq), (r, br) = self.hn_scr
    self.ACT(sq, ps[0:64, 0:n], AF.Square, [pbuf], [bsq])
    p2, pb2 = self.PS[6], self.pb[6]
    self.MM(p2[0:64, 0:n], self.ones[0:64, 0:64], sq, True, True, ["CT", bsq], [pb2])
    self.ACT(r, p2[0:64, 0:n], AF.Sqrt, [pb2, "CS"], [br], scale=1.0 / 64, bias=self.CS[0:64, 0:1])
    self.RCP(r, r, [br], [br])
    self.STT(dst, ps[0:64, 0:n], gain, r, ALU.mult, ALU.mult, [pbuf, br, "QKG"], [bdst])


def _rope(self, x, bx, tok0, n, name):
    (rot, brot), (cs_, bcs) = self.rp_scr
    self.DMA(cs_, self.ropeT[:, :, tok0:tok0 + n], [], [bcs])
    p3, pb3 = self.PS[5], self.pb[5]
    self.MM(p3[0:64, 0:n], self.CT[0:64, 640:704], x, True, True, ["CT", bx], [pb3])
    self.TT(rot, p3[0:64, 0:n], cs_[:, 1, :], ALU.mult, [pb3, bcs], [brot])
    self.TT(x, x, cs_[:, 0, :], ALU.mult, [bx, bcs], [bx])
    self.TT(x, x, rot, ALU.add, [bx, brot], [bx])


KB.head_norm = _head_norm
KB.rope = _rope


def _mixer_even(self, l):
    cfg = self.cfg
    i = l // 2
    NTL, TP, SEQ, DSEQ, PAST, NPS = cfg.NTILES, cfg.TP, cfg.SEQ, cfg.DEC_SEQ, cfg.PAST, cfg.NPS
    for t in range(NTL):
        self.arena_reset()
        x, bx = self.alloc("x", [128, 8, TN])
        self.DMA(x, self.xres[:, :, t * TN:(t + 1) * TN], ["xres"], [bx])
        h, bh = self.modin(x, bx, l, 1, self.tile_cond(t))
        self.hn_scr = (self.alloc("hnsq", [64, TN]), self.alloc("hnr", [64, TN]))
        self.rp_scr = (self.alloc("rprot", [64, TN]), self.alloc("rpcs", [64, 2, TN]))
        vt, bvt = self.alloc("vt", [128, 4, 64])
        qg, bqg = self.alloc("qkg", [64, 2])
        self.DMA(qg, self.qkg[i], [], ["QKG"])
        wh = [self.alloc(f"wh{k}", [128, 8, 64]) for k in range(2)]
        hd, bhd = self.alloc("hd", [64, TN])
        n = 0
        for kind, cnt, wL, dstT in (("q", 8, self.wqL, self.qT), ("k", 2, self.wkL, self.kT), ("v", 2, self.wvL, None)):
            for hh in range(cnt):
                wv, wb = wh[n % 2]
                self.DMA(wv, wL[i * cnt + hh].rearrange("p (k m) -> p k m", k=8), [], [wb])
                ps, pbuf = self.PS[n % 2], self.pb[n % 2]
                n += 1
                for k in range(8):
                    self.MM(ps[0:64, :], wv[:, k, :], h[:, k, :], k == 0, k == 7, [wb, bh], [pbuf])
                if kind == "v":
                    self.ACT(hd, ps[0:64, :], AF.Copy, [pbuf], [bhd])
                    if t == 0:
                        self.DMA(self.newv[i * 2 + hh], hd, [bhd], ["newv"], q="pool")
                    p4, pb4 = self.PS[4], self.pb[4]
                    for b_ in range(4):
                        self.TR(p4[:, b_ * 64:(b_ + 1) * 64], hd[:, b_ * 128:(b_ + 1) * 128], self.ident[0:64, 0:64],
                                [bhd, "CT"], [pb4])
                    self.ACT(vt, p4[:, 0:256].rearrange("p (b d) -> p b d", d=64), AF.Copy, [pb4], [bvt])
                    self.DMA(self.vtm[hh, t * TN:(t + 1) * TN, :].rearrange("(b p) d -> p b d", p=128), vt, [bvt], ["vtm"], q="pool")
                else:
                    gcol = qg[:, 0:1] if kind == "q" else qg[:, 1:2]
                    self.head_norm(ps, pbuf, gcol, hd, bhd, TN, "hn")
                    if kind == "k" and t == 0:
                        self.DMA(self.newk[i * 2 + hh], hd, [bhd], ["newk"], q="pool")
                    if t > 0:
                        self.rope(hd, bhd, (t - 1) * TN, TN, "rp")
                    self.DMA(dstT[hh, :, t * TN:(t + 1) * TN], hd, [bhd], ["qT" if kind == "q" else "kT"], q="pool")
        wp = [self.alloc(f"wp{k}", [128, 8, 128]) for k in range(2)]
        pbo, bpbo = self.alloc("pbo", [128, 15, TN])
        for c in range(15):
            wv, wb = wp[c % 2]
            self.DMA(wv, self.wpbL[i * 15 + c].rearrange("p (k m) -> p k m", k=8), [], [wb])
            ps, pbuf = self.PS[2 + c % 2], self.pb[2 + c % 2]
            for k in range(8):
                self.MM(ps[:, :], wv[:, k, :], h[:, k, :], k == 0, k == 7, [wb, bh], [pbuf])
            self.ACT(pbo[:, c, :], ps[:, :], AF.Copy, [pbuf], [bpbo])
        self.DMA(self.pbT[:, :, t * TN:(t + 1) * TN], pbo, [bpbo], ["pbT"], q="pool")
    seqs = [(s * SEQ, SEQ, False) for s in range(NPS)] + [(TP, DSEQ, True)]
    for (tok0, Ls, is_s) in seqs:
        NK = Ls + (PAST if is_s else 0)
        nkb = NK // 128
        for kv in range(2):
            self.arena_reset()
            KT, bKT = self.alloc("KT", [64, NK])
            VT, bVT = self.alloc("VT", [128, nkb, 64])
            self.DMA(KT[:, 0:Ls], self.kT[kv, :, tok0:tok0 + Ls], ["kT"], [bKT])
            self.DMA(VT[:, 0:Ls // 128, :], self.vtm[kv, tok0:tok0 + Ls, :].rearrange("(b p) d -> p b d", p=128), ["vtm"], [bVT])
            if is_s:
                self.DMA(KT[:, Ls:NK], self.ctxkT[i * 2 + kv], [], [bKT])
                self.DMA(VT[:, Ls // 128:nkb, :], self.ctxv[i * 2 + kv].rearrange("(b p) d -> p b d", p=128), [], [bVT])
            Q = [self.alloc(f"Q{k}", [64, 4, 128]) for k in range(2)]
            E = [self.alloc(f"E{k}", [128, 512]) for k in range(2)]
            O = [self.alloc(f"O{k}", [64, 4, 128]) for k in range(2)]
            rd, brd = self.alloc("rd", [64, 512])
            for qt in range(Ls // 128):
                qv, bq = Q[qt % 2]
                ov, bo = O[qt % 2]
                ta = tok0 + qt * 128
                self.DMA(qv, self.qT[kv * 4:(kv + 1) * 4, :, ta:ta + 128].rearrange("g d q -> d g q"), ["qT"], [bq])
                po, pbo_ = self.PS[2], self.pb[2]
                pd, pbd = self.PS[3], self.pb[3]
                for kb in range(nkb):
                    ps, pbuf = self.PS[kb % 2], self.pb[kb % 2]
                    ev, be = E[kb % 2]
                    self.MM(ps[:, :], KT[:, kb * 128:(kb + 1) * 128], qv.rearrange("d g q -> d (g q)"), True, True, [bKT, bq], [pbuf])
                    self.ACT(ev, ps[:, :], AF.Exp, [pbuf], [be], scale=0.125)
                    self.MM(po[0:64, :], VT[:, kb, :], ev, kb == 0, kb == nkb - 1, [bVT, be], [pbo_])
                    self.MM(pd[0:64, :], self.ones[:, 0:64], ev, kb == 0, kb == nkb - 1, ["CT", be], [pbd])
                self.RCP(rd, pd[0:64, :], [pbd], [brd])
                self.TT(ov.rearrange("d g q -> d (g q)"), po[0:64, :], rd, ALU.mult, [pbo_, brd], [bo])
                for g in range(4):
                    hq = kv * 4 + g
                    self.DMA(self.mixT[(hq % 2) * 64:(hq % 2) * 64 + 64, hq // 2, ta:ta + 128], ov[:, g, :], [bo], ["mixT"], q="pool")
    NS = 256
    segs = []
    for (tok0, Ls, is_s) in seqs:
        for s0 in range(0, Ls, NS):
            segs.append((tok0 + s0, s0 == 0, s0 + NS == Ls))
    for (ta, first, last) in segs:
        self.arena_reset()
        prm, bprm = self.alloc("prm", [128, 64])
        self.DMA(prm, self.rwv[i], [], [bprm])
        mu = prm[:, 0:15]
        k_k, k_a, r_k = prm[:, 15:19], prm[:, 19:23], prm[:, 23:27]
        w0v, a0v = prm[:, 27:35], prm[:, 35:43]
        drv, bdrv = self.alloc("drv", [128, 64])
        omu, hmu, omka, nw0 = drv[:, 0:15], drv[:, 15:30], drv[:, 30:34], drv[:, 34:42]
        self.TS(omu, mu, -1.0, 1.0, ALU.mult, ALU.add, [bprm], [bdrv])
        self.TS(hmu, mu, 0.5, None, ALU.mult, None, [bprm], [bdrv])
        self.TS(omka, k_a, -1.0, 1.0, ALU.mult, ALU.add, [bprm], [bdrv])
        self.TS(nw0, w0v, -1.0, None, ALU.mult, None, [bprm], [bdrv])
        ups, bups = self.alloc("ups", [128, 3, 512])
        self.DMA(ups, self.rwup[i * 3:(i + 1) * 3].rearrange("a p m -> p a m"), [], [bups])
        raw, braw = self.alloc("raw", [128, 15, NS + 2])
        if first:
            self.MSET(raw[:, :, 0:1], 0.0, [braw])
        if last:
            self.MSET(raw[:, :, NS + 1:NS + 2], 0.0, [braw])
        lo = 1 if first else 0
        hi = NS + 1 if last else NS + 2
        self.DMA(raw[:, :, lo:hi], self.pbT[:, :, ta - 1 + lo:ta - 1 + hi], ["pbT"], [braw])
        pm, bpm = self.alloc("pm", [128, 15, NS])
        nb_, bnb = self.alloc("nb", [128, NS])
        for c in range(15):
            self.TT(nb_, raw[:, c, 0:NS], raw[:, c, 2:NS + 2], ALU.add, [braw], [bnb])
            self.TS(nb_, nb_, hmu[:, c:c + 1], None, ALU.mult, None, [bnb, bdrv], [bnb])
            self.STT(pm[:, c, :], raw[:, c, 1:NS + 1], omu[:, c:c + 1], nb_, ALU.mult, ALU.add, [braw, bdrv, bnb], [bpm])
        r_, k_, v_ = pm[:, 0:4, :], pm[:, 4:8, :], pm[:, 8:12, :]
        gd, wd, ad = pm[:, 12, :], pm[:, 13, :], pm[:, 14, :]
        G, bG = self.alloc("G", [128, 4, NS])
        W = [self.alloc(f"W{d}", [128, 4, NS]) for d in range(2)]
        A = [self.alloc(f"A{d}", [128, 4, NS]) for d in range(2)]
        Bq = [self.alloc(f"B{d}", [128, 4, NS]) for d in range(2)]
        KD = [self.alloc(f"KD{d}", [128, 4, NS]) for d in range(2)]
        LW = [self.alloc(f"LW{d}", [128, 4, NS]) for d in range(2)]
        kk, bkk = self.alloc("kk", [128, 4, NS])
        nkk, bnkk = self.alloc("nkk", [128, 4, NS])
        bon, bbon = self.alloc("bon", [128, 4, NS])
        t1, bt1 = self.alloc("t1", [128, NS])
        t2, bt2 = self.alloc("t2", [128, NS])
        sgm, bsgm = self.alloc("sgm", [128, NS])
        twd, btwd = self.alloc("twd", [128, NS])
        self.ACT(sgm, gd, AF.Sigmoid, [bpm], [bsgm])
        self.ACT(twd, wd, AF.Tanh, [bpm], [btwd])
        for c in range(4):
            cs = slice(c * 128, (c + 1) * 128)
            ps, pbuf = self.PS[c % 2], self.pb[c % 2]
            self.MM(ps[:, 0:NS], ups[:, 2, cs], sgm, True, True, [bups, bsgm], [pbuf])
            self.ACT(G[:, c, :], ps[:, 0:NS], AF.Copy, [pbuf], [bG])
            for d in range(2):
                rs = slice(d * 64, d * 64 + 64)
                pw, pbw = self.PS[2 + d], self.pb[2 + d]
                self.MM(pw[:, 0:NS], ups[rs, 0, cs], twd[rs, :], True, True, [bups, btwd], [pbw])
                self.ACT(t1, pw[:, 0:NS], AF.Exp, [pbw, bdrv], [bt1], scale=-1.0, bias=nw0[:, d * 4 + c:d * 4 + c + 1])
                self.TS(t1, t1, 1.0, None, ALU.add, None, [bt1], [bt1])
                self.RCP(t1, t1, [bt1], [bt1])
                self.ACT(W[d][0][:, c, :], t1, AF.Exp, [bt1], [W[d][1]], scale=-0.6065306597126334)
                self.TS(LW[d][0][:, c, :], t1, -0.6065306597126334, None, ALU.mult, None, [bt1], [LW[d][1]])
                pa, pba = self.PS[4 + d], self.pb[4 + d]
                self.MM(pa[:, 0:NS], ups[rs, 1, cs], ad[rs, :], True, True, [bups, bpm], [pba])
                self.ACT(A[d][0][:, c, :], pa[:, 0:NS], AF.Sigmoid, [pba, bprm], [A[d][1]], bias=a0v[:, d * 4 + c:d * 4 + c + 1])
            self.TS(kk[:, c, :], k_[:, c, :], k_k[:, c:c + 1], None, ALU.mult, None, [bpm, bprm], [bkk])
            self.TT(t2, kk[:, c, :], kk[:, c, :], ALU.mult, [bkk], [bt2])
            pq, pbq = self.PS[6], self.pb[6]
            self.MM(pq[:, 0:NS], self.bones, t2, True, True, ["CT", bt2], [pbq])
            self.ACT(t2, pq[:, 0:NS], AF.Sqrt, [pbq], [bt2])
            self.TS(t2, t2, 1e-12, None, ALU.max, None, [bt2], [bt2])
            self.RCP(t2, t2, [bt2], [bt2])
            self.TT(kk[:, c, :], kk[:, c, :], t2, ALU.mult, [bkk, bt2], [bkk])
            self.TS(nkk[:, c, :], kk[:, c, :], -1.0, None, ALU.mult, None, [bkk], [bnkk])
            for d in range(2):
                Ad, bAd = A[d]
                self.TT(Bq[d][0][:, c, :], kk[:, c, :], Ad[:, c, :], ALU.mult, [bkk, bAd], [Bq[d][1]])
                self.TS(t2, Ad[:, c, :], k_a[:, c:c + 1], omka[:, c:c + 1], ALU.mult, ALU.add, [bAd, bprm, bdrv], [bt2])
                self.TT(KD[d][0][:, c, :], k_[:, c, :], t2, ALU.mult, [bpm, bt2], [KD[d][1]])
                self.STT(t2, r_[:, c, :], r_k[:, c:c + 1], KD[d][0][:, c, :], ALU.mult, ALU.mult, [bpm, bprm, KD[d][1]], [bt2])
                pr_, pbr = self.PS[7], self.pb[7]
                self.MM(pr_[:, 0:NS], self.bones, t2, True, True, ["CT", bt2], [pbr])
                if d == 0:
                    self.TT(bon[:, c, :], pr_[:, 0:NS], v_[:, c, :], ALU.mult, [pbr, bpm], [bbon])
                else:
                    self.TT(t2, pr_[:, 0:NS], v_[:, c, :], ALU.mult, [pbr, bpm], [bt2])
                    self.TT(bon[:, c, :], bon[:, c, :], t2, ALU.add, [bbon, bt2], [bbon])
        stg = [self.alloc(f"stg{k}", [128, 512]) for k in range(2)]
        outs = [("g", G, bG), ("bon", bon, bbon)]
        n = 0
        for (nm, src, bsrc) in outs:
            for b_ in range(NS // 128):
                ps, pbuf = self.PS[n % 2], self.pb[n % 2]
                sv, bs_ = stg[n % 2]
                n += 1
                for c in range(4):
                    self.TR(ps[:, c * 128:(c + 1) * 128], src[:, c, b_ * 128:(b_ + 1) * 128], self.ident, [bsrc, "CT"], [pbuf])
                self.ACT(sv, ps[:, :], AF.Copy, [pbuf], [bs_])
                self.DMA(self.tmq[nm][ta + b_ * 128:ta + (b_ + 1) * 128, :], sv, [bs_], ["tm_" + nm], q="pool")
        self.rw_precompute(i, ta, NS, r_, v_, bpm, kk, bkk, LW, Bq, KD)
    self.rw_scan(i, seqs)
    for t in range(NTL):
        self.arena_reset()
        lnw, blnw = self.alloc("lnw", [128, 2, 512])
        self.DMA(lnw, _apx(self.rwln.tensor, i * 2 * 512, [[0, 128], [512, 2], [1, 512]]), [], [blnw])
        for b_ in range(4):
            ta = t * TN + b_ * 128
            y, by = self.alloc("y", [128, 8, 64])
            z, bz = self.alloc("z", [128, 8, 64])
            sq, bsq = self.alloc("sq", [128, 8, 64])
            g_, bg_ = self.alloc("g", [128, 8, 64])
            st_, bst = self.alloc("st", [128, 16])
            ob, bob = self.alloc("ob", [128, 4, 128])
            def ld(dst, bd, nm, q="sp"):
                self.DMA(dst.rearrange("p h k -> p (h k)"), self.tmq[nm][ta:ta + 128, :], ["tm_" + nm], [bd], q=q)
            ld(y, by, "y0")
            ld(z, bz, "y1", q="pool")
            ld(sq, bsq, "bon")
            ld(g_, bg_, "g", q="pool")
            self.TT(y, y, z, ALU.add, [by, bz], [by])
            self.TT(y, y, sq, ALU.add, [by, bsq], [by])
            mean, var = st_[:, 0:8], st_[:, 8:16]
            self.RED(mean, y, [by], [bst])
            self.TS(mean, mean, 1.0 / 64, None, ALU.mult, None, [bst], [bst])
            self.TT(y, y, mean.unsqueeze(2).broadcast_to([128, 8, 64]), ALU.subtract, [by, bst], [by])
            self.TT(sq, y, y, ALU.mult, [by], [bsq])
            self.RED(var, sq, [bsq], [bst])
            self.ACT(var, var, AF.Sqrt, [bst, "CS"], [bst], scale=1.0 / 64, bias=self.CS[:, 1:2])
            self.RCP(var, var, [bst], [bst])
            self.TT(y, y, var.unsqueeze(2).broadcast_to([128, 8, 64]), ALU.mult, [by, bst], [by])
            yf = y.rearrange("p h k -> p (h k)")
            self.TT(yf, yf, lnw[:, 0, :], ALU.mult, [by, blnw], [by])
            self.TT(yf, yf, lnw[:, 1, :], ALU.add, [by, blnw], [by])
            self.TT(y, y, g_, ALU.mult, [by, bg_], [by])
            ps, pbuf = self.PS[b_ % 2], self.pb[b_ % 2]
            for c in range(4):
                self.TR(ps[:, c * 128:(c + 1) * 128], yf[:, c * 128:(c + 1) * 128], self.ident, [by, "CT"], [pbuf])
            self.ACT(ob, ps[:, :].rearrange("p (c q) -> p c q", c=4), AF.Copy, [pbuf], [bob])
            self.DMA(self.mixT[:, 4:8, ta:ta + 128], ob, [bob], ["mixT"], q="pool")
            self.aoff -= (4 * 512 + 16 + 512)
    for t in range(NTL):
        self.arena_reset()
        x, bx = self.alloc("x", [128, 8, TN])
        m, bm = self.alloc("m", [128, 8, TN])
        y, by = self.alloc("y", [128, 8, TN])
        self.DMA(x, self.xres[:, :, t * TN:(t + 1) * TN], ["xres"], [bx])
        self.DMA(m, self.mixT[:, :, t * TN:(t + 1) * TN], ["mixT"], [bm])
        self.proj8(self.woL, i * 8, m, bm, y, by)
        self.resid(x, bx, y, by, l, 1, self.tile_cond(t))
        self.DMA(self.xres[:, :, t * TN:(t + 1) * TN], x, [bx], ["xres"], q="pool")


KB.mixer_even = _mixer_even


def _rw_precompute(self, i, ta, NS, r_, v_, bpm, kk, bkk, LW, Bq, KD):
    nch = NS // 64
    NF = 4 * NS
    MK, bMK = self.RM, "RM"
    MUs, MUi, MLs, MLi, nMUs, nMLs = [MK[:, k * 128:(k + 1) * 128] for k in range(6)]
    m01 = MK[:, 768:768 + NF]
    cum, bcum = self.alloc("cum", [128, 4, NS])
    ex, bex = self.alloc("ex", [128, 4, NS])
    Kt, bKt = self.alloc("Kt", [128, 4, NS])
    Bt, bBt = self.alloc("Bt", [128, 4, NS])
    Kd, bKd = self.alloc("Kd", [128, 4, NS])
    Rt, bRt = self.alloc("Rt", [128, 4, NS])
    Bp, bBp = self.alloc("Bp", [128, 4, NS])
    Kp, bKp = self.alloc("Kp", [128, 4, NS])
    ge, bge = self.alloc("ge", [128, 4, nch])
    UT = []
    for k in range(2):
        UT.append(dict(NX=self.alloc(f"NX{k}", [128, 2, 128]), X2=self.alloc(f"X2{k}", [128, 2, 128]),
                       AM=self.alloc(f"AM{k}", [128, 3, 128]), AkTm=self.alloc(f"AkTm{k}", [128, 128]),
                       Tm=self.alloc(f"Tm{k}", [128, 128]), TMt=self.alloc(f"TMt{k}", [128, 3, 64]),
                       nKA=self.alloc(f"nKA{k}", [128, 192]), Dg=self.alloc(f"Dg{k}", [128, 2, 64]),
                       Vs=self.alloc(f"Vs{k}", [64, 2, 64]), Bm=self.alloc(f"Bm{k}", [128, 2, 64])))
        self.MSET(UT[-1]["Dg"][0], 0.0, [UT[-1]["Dg"][1]])
    QGs = [self.alloc(f"QGs{k}", [64, 256]) for k in range(2)]
    MWs = [self.alloc(f"MWs{k}", [64, 2, 128]) for k in range(2)]
    f2 = lambda a_: a_.rearrange("p c n -> p (c n)")
    c3 = lambda a_: a_.rearrange("p c (h s) -> p c h s", s=64)
    nun = 0
    for d in range(2):
        lw, blw = LW[d]
        self.SCAN(f2(cum), m01, f2(lw), 0.0, [bMK, blw], [bcum])
        tot_bc = c3(cum)[:, :, :, 63:64].broadcast_to([128, 4, nch, 64])
        if d == 1:
            self.TT(c3(ex), tot_bc, c3(cum), ALU.subtract, [bcum], [bex])
            self.TT(f2(ex), f2(ex), f2(lw), ALU.add, [bex, blw], [bex])
            self.ACT(ge, c3(cum)[:, :, :, 63], AF.Exp, [bcum], [bge])
            self.CP(f2(cum), f2(ex), [bex], [bcum])
        else:
            self.ACT(ge, c3(cum)[:, :, :, 63], AF.Exp, [bcum], [bge])
        self.ACT(f2(ex), f2(cum), AF.Exp, [bcum], [bex])
        self.TT(Rt, r_, ex, ALU.mult, [bpm, bex], [bRt])
        self.TT(f2(ex), f2(cum), f2(lw), ALU.subtract, [bcum, blw], [bex])
        self.ACT(f2(ex), f2(ex), AF.Exp, [bex], [bex])
        self.TT(Kt, kk, ex, ALU.mult, [bkk, bex], [bKt])
        self.ACT(f2(ex), f2(cum), AF.Exp, [bcum], [bex], scale=-1.0)
        self.TT(Bt, Bq[d][0], ex, ALU.mult, [Bq[d][1], bex], [bBt])
        self.TT(Kd, KD[d][0], ex, ALU.mult, [KD[d][1], bex], [bKd])
        ge_bc = ge.unsqueeze(3).broadcast_to([128, 4, nch, 64])
        self.TT(c3(Bp), c3(Bt), ge_bc, ALU.mult, [bBt, bge], [bBp])
        self.TT(c3(Kp), c3(Kd), ge_bc, ALU.mult, [bKd, bge], [bKp])
        mS, mI, mST, nS_, nST = (MUs, MUi, MLs, nMUs, nMLs) if d == 0 else (MLs, MLi, MUs, nMLs, nMUs)
        for blk in range(NS // 128):
            tsl = slice(blk * 128, (blk + 1) * 128)
            gblk = (ta + blk * 128) // 128
            for h in range(8):
                c = h // 2
                rs = slice((h % 2) * 64, (h % 2) * 64 + 64)
                u = nun % 2
                nun += 1
                U = UT[u]
                (NX, bNX), (X2, bX2), (AM, bAM), (AkTm, bAkT) = U["NX"], U["X2"], U["AM"], U["AkTm"]
                (Tm, bTm), (TMt, bTMt), (nKA, bnKA), (Dg, bDg), (Vs, bVs) = U["Tm"], U["TMt"], U["nKA"], U["Dg"], U["Vs"]
                Bm, bBm = U["Bm"]
                B0 = u * 4
                psA, bA = self.PS[B0], self.pb[B0]
                psB, bB = self.PS[B0 + 1][:, 0:256], self.pb[B0 + 1]
                R_ = [bBt, bKt, bRt, bKd]
                self.MM(psA[:, 0:128], Bt[rs, c, tsl], Kt[rs, c, tsl], True, True, R_, [bA])
                self.MM(psA[:, 128:256], Bt[rs, c, tsl], Rt[rs, c, tsl], True, True, R_, [bA])
                self.MM(psA[:, 256:384], Kd[rs, c, tsl], Kt[rs, c, tsl], True, True, R_, [bA])
                self.MM(psA[:, 384:512], Kd[rs, c, tsl], Rt[rs, c, tsl], True, True, R_, [bA])
                self.MM(psB[:, 0:128], Kt[rs, c, tsl], Bt[rs, c, tsl], True, True, R_, [bB])
                self.MM(psB[:, 128:256], Kt[rs, c, tsl], Kd[rs, c, tsl], True, True, R_, [bB])
                self.TT(NX[:, 0, :], psA[:, 0:128], nS_, ALU.mult, [bA, bMK], [bNX])
                self.TT(NX[:, 1, :], psB[:, 0:128], nST, ALU.mult, [bB, bMK], [bNX])
                self.TT(AM[:, 0, :], psA[:, 128:256], mI, ALU.mult, [bA, bMK], [bAM])
                self.TT(AM[:, 1, :], psA[:, 256:384], mS, ALU.mult, [bA, bMK], [bAM])
                self.TT(AM[:, 2, :], psA[:, 384:512], mI, ALU.mult, [bA, bMK], [bAM])
                self.TT(AkTm, psB[:, 128:256], mST, ALU.mult, [bB, bMK], [bAkT])
                self.TT(Tm, NX[:, 0, :], self.ident, ALU.add, [bNX, "CT"], [bTm])
                psC, bC = self.PS[B0 + 1][:, 256:512], self.pb[B0 + 1]
                psK_v, bKv = self.PS[B0][:, 0:128], self.pb[B0]
                self.TR(psC[:, 0:64], Kt[rs, c, tsl], self.ident[rs, rs], [bKt, "CT"], [bC])
                self.TR(psC[:, 64:128], Bp[rs, c, tsl], self.ident[rs, rs], [bBp, "CT"], [bC])
                self.TR(psC[:, 128:192], Kp[rs, c, tsl], self.ident[rs, rs], [bKp, "CT"], [bC])
                if d == 0:
                    for ch in range(2):
                        tch = slice(blk * 128 + ch * 64, blk * 128 + ch * 64 + 64)
                        self.TR(psK_v[0:64, ch * 64:(ch + 1) * 64],
                                v_[rs, c, tch], self.ident[rs, rs], [bpm, "CT"], [bKv])
                    self.ACT(Vs.rearrange("p a k -> p (a k)"), psK_v[0:64, 0:128], AF.Copy, [bKv], [bVs])
                    self.DMA(self.rwV[gblk * 8 + h], Vs.rearrange("p a k -> p (a k)"), [bVs], ["rwV"], q="pool")
                self.ACT(TMt.rearrange("p a k -> p (a k)"), psC[:, 0:192], AF.Copy, [bC], [bTMt])
                Xc, XTc, bXc = NX[:, 0, :], NX[:, 1, :], bNX
                Xo, bXo = X2, bX2
                for lev in range(5):
                    psX, bX = self.PS[B0 + 2][:, 0:256], self.pb[B0 + 2]
                    psT, bT_ = self.PS[B0 + 2][:, 256:512], self.pb[B0 + 2]
                    if lev < 4:
                        self.MM(psX[:, 0:128], XTc, Xc, True, True, [bXc], [bX])
                    self.MM(psX[:, 128:256], Xc, XTc, True, True, [bXc], [bX])
                    if lev < 4:
                        self.ACT(Xo.rearrange("p a k -> p (a k)"), psX[:, 0:256], AF.Copy, [bX], [bXo])
                    else:
                        self.ACT(Xo[:, 1, :], psX[:, 128:256], AF.Copy, [bX], [bXo])
                    self.MM(psT[:, 0:128], Xo[:, 1, :], Tm, True, True, [bXo, bTm], [bT_])
                    self.TT(Tm, Tm, psT[:, 0:128], ALU.add, [bTm, bT_], [bTm])
                    (Xc, XTc, bXc), (Xo, bXo) = (Xo[:, 0, :], Xo[:, 1, :], bXo), ((NX, bNX) if Xo is X2 else (X2, bX2))
                psK, bK = self.PS[B0 + 3][:, 0:192], self.pb[B0 + 3]
                self.MM(psK[:, 0:64], Tm, TMt[:, 0, :], True, True, [bTm, bTMt], [bK])
                self.MM(psK[:, 64:192], Tm, AkTm, True, True, [bTm, bAkT], [bK])
                self.ACT(nKA, psK[:, 0:192], AF.Copy, [bK], [bnKA], scale=-1.0)
                self.TS(Bm[:, 0, :], TMt[:, 1, :], self.CT[:, 960:961], None, ALU.mult, None, [bTMt, "CT"], [bBm])
                self.TS(Bm[:, 1, :], TMt[:, 1, :], self.CT[:, 961:962], None, ALU.mult, None, [bTMt, "CT"], [bBm])
                for ch in range(2):
                    gcol = ge[rs, c, blk * 2 + ch:blk * 2 + ch + 1]
                    self.TS(Dg[rs, ch, :], self.ident[rs, rs], gcol, None, ALU.mult, None, ["CT", bge], [bDg])
                (QG, bQG), (MW, bMW) = QGs[u], MWs[u]
                psQ, bQ = self.PS[B0 + 3][:, 192:448], self.pb[B0 + 3]
                psM, bM = self.PS[B0 + 2][:, 0:256], self.pb[B0 + 2]
                sel = self.ident[:, rs]
                self.MM(psQ[0:64, 0:128], nKA[:, 0:64], AM[:, 0, :], True, False, [bnKA, bAM], [bQ])
                self.MM(psQ[0:64, 0:128], sel, Rt[:, c, tsl], False, True, ["CT", bRt], [bQ])
                for ch in range(2):
                    cs_ = slice(ch * 64, ch * 64 + 64)
                    o = 128 + ch * 64
                    self.MM(psQ[0:64, o:o + 64], nKA[:, 0:64], Bm[:, ch, :], True, False, [bnKA, bBm], [bQ])
                    self.MM(psQ[0:64, o:o + 64], sel, Dg[:, ch, :], False, True, ["CT", bDg], [bQ])
                    lk = nKA[:, 64 + ch * 64:64 + ch * 64 + 64]
                    lid = self.ident[:, cs_]
                    om = ch * 128
                    self.MM(psM[0:64, om:om + 64], lk, AM[:, 0, cs_], True, False, [bnKA, bAM], [bM])
                    self.MM(psM[0:64, om:om + 64], lid, AM[:, 2, cs_], False, True, ["CT", bAM], [bM])
                    self.MM(psM[0:64, om + 64:om + 128], lk, TMt[:, 1, :], True, False, [bnKA, bTMt], [bM])
                    self.MM(psM[0:64, om + 64:om + 128], lid, TMt[:, 2, :], False, True, ["CT", bTMt], [bM])
                self.ACT(QG, psQ[0:64, 0:256], AF.Copy, [bQ], [bQG])
                self.ACT(MW.rearrange("p a k -> p (a k)"), psM[0:64, 0:256], AF.Copy, [bM], [bMW])
                uidx = gblk * 16 + d * 8 + h
                self.DMA(self.rwQG[uidx], QG, [bQG], ["rwQG"], q="sp")
                self.DMA(self.rwMW[uidx], MW.rearrange("p a k -> p (a k)"), [bMW], ["rwMW"], q="pool")


def _rw_scan(self, i, seqs):
    cfg = self.cfg
    for si, (tok0, Ls, is_s) in enumerate(seqs):
        self.arena_reset()
        NB = Ls // 128
        ST = [self.alloc(f"ST{k}", [64, 16, 64]) for k in range(2)]
        Yst = [self.alloc(f"Yst{k}", [64, 16, 64]) for k in range(2)]
        QGt = [[self.alloc(f"QG{k}{d}", [64, 8, 256]) for d in range(2)] for k in range(2)]
        MWt = [[self.alloc(f"MW{k}{d}", [64, 8, 256]) for d in range(2)] for k in range(2)]
        Vt = [[self.alloc(f"V{k}{d}", [64, 8, 128]) for d in range(2)] for k in range(2)]
        f2 = lambda a_: a_.rearrange("p u v -> p (u v)")
        if is_s:
            self.DMA(f2(ST[0][0]), self.rws0T[i], [], [ST[0][1]])
        else:
            self.MSET(f2(ST[0][0]), 0.0, [ST[0][1]])
        cur = 0
        ny = 0
        for b in range(NB):
            par = b % 2
            blks = (tok0 // 128 + b, tok0 // 128 + NB - 1 - b)
            for d in range(2):
                g = blks[d]
                self.DMA(QGt[par][d][0], self.rwQG[g * 16 + d * 8:g * 16 + d * 8 + 8].rearrange("u p m -> p u m"),
                         ["rwQG"], [QGt[par][d][1]], q="sp")
                self.DMA(MWt[par][d][0], self.rwMW[g * 16 + d * 8:g * 16 + d * 8 + 8].rearrange("u p m -> p u m"),
                         ["rwMW"], [MWt[par][d][1]], q="pool")
                self.DMA(Vt[par][d][0], self.rwV[g * 8:g * 8 + 8].rearrange("u p m -> p u m"), ["rwV"], [Vt[par][d][1]], q="sp")
            for step in range(2):
                Sc, bSc = ST[cur]
                Sn, bSn = ST[1 - cur]
                yv, by = Yst[ny % 2]
                ny += 1
                for d in range(2):
                    ch = step if d == 0 else 1 - step
                    cs_ = slice(ch * 64, ch * 64 + 64)
                    QG, bQG = QGt[par][d]
                    MW, bMW = MWt[par][d]
                    V, bV = Vt[par][d]
                    psY, bY = self.PS[d], self.pb[d]
                    psS, bS = self.PS[2 + d], self.pb[2 + d]
                    for h in range(8):
                        u = d * 8 + h
                        o = h * 64
                        vv = V[:, h, ch * 64:ch * 64 + 64]
                        self.MM(psY[0:64, o:o + 64], QG[:, h, ch * 64:ch * 64 + 64], Sc[:, u, :], True, False, [bQG, bSc], [bY])
                        self.MM(psY[0:64, o:o + 64], MW[:, h, ch * 128:ch * 128 + 64], vv, False, True, [bMW, bV], [bY])
                        self.MM(psS[0:64, o:o + 64], QG[:, h, 128 + ch * 64:128 + ch * 64 + 64], Sc[:, u, :], True, False, [bQG, bSc], [bS])
                        self.MM(psS[0:64, o:o + 64], MW[:, h, ch * 128 + 64:ch * 128 + 128], vv, False, True, [bMW, bV], [bS])
                    self.ACT(f2(yv)[:, d * 512:(d + 1) * 512], psY[0:64, :], AF.Copy, [bY], [by])
                    self.CP(f2(Sn)[:, d * 512:(d + 1) * 512], psS[0:64, :], [bS], [bSn])
                    tokd = blks[d] * 128 + ch * 64
                    self.DMA(self.tmq["y0" if d == 0 else "y1"][tokd:tokd + 64, :], f2(yv)[:, d * 512:(d + 1) * 512],
                             [by], ["tm_y0" if d == 0 else "tm_y1"], q="pool")
                cur = 1 - cur
        if not is_s:
            self.DMA(self.newrwT[i * cfg.NPS + si], f2(ST[cur][0]), [ST[cur][1]], ["newrwT"], q="pool")


KB.rw_precompute = _rw_precompute
KB.rw_scan = _rw_scan


def kernel(x_prompt, x_sample, cache_k, cache_v, state_rwkv, state_s5, c, c_ctx, **P):
    f = lambda a: np.asarray(a, np.float32)
    x_prompt, x_sample, cache_k, cache_v = f(x_prompt), f(x_sample), f(cache_k), f(cache_v)
    state_rwkv, state_s5, c, c_ctx = f(state_rwkv), f(state_s5), f(c), f(c_ctx)
    P = {k: f(v) for k, v in P.items()}
    B, SEQ, _ = x_prompt.shape
    DB, DSEQ, _ = x_sample.shape
    L = P["ada_w"].shape[0]
    ncores = 8
    cfg = Cfg(SEQ=SEQ, DEC_SEQ=DSEQ, DEPTH=L, PAST=cache_k.shape[2], NPS=B // ncores)
    kb = KB(cfg)
    nc = kb.build()
    common = common_inputs(cfg, P)
    common.update(s5_common(cfg, P))
    common.update(even_common(cfg, P))
    common["rmask"] = rw_masks()
    in_maps = []
    for core in range(ncores):
        b = core % DB
        d = dict(common)
        d.update(core_inputs(cfg, core, x_prompt, x_sample, c, c_ctx))
        d.update(s5_core(cfg, state_s5[b]))
        d.update(even_core(cfg, cache_k[b], cache_v[b], state_rwkv[b]))
        d.update(rw_core(cfg, state_rwkv[b]))
        in_maps.append(d)
    res = run_bass_kernel_spmd(nc, in_maps, core_ids=list(range(ncores)))
    NPS, NAB = cfg.NPS, cfg.NAB
    y_prompt = np.zeros((B, SEQ, D), np.float32)
    y_sample = np.zeros((DB, DSEQ, D), np.float32)
    new_s5 = np.zeros((B, cfg.NCL, 2, 2, 64, 64), np.float32)
    new_k = np.zeros((B, NAB, SEQ, 2, 64), np.float32)
    new_v = np.zeros((B, NAB, SEQ, 2, 64), np.float32)
    new_rwkv = np.zeros((B, NAB, 2, 8, 64, 64), np.float32)
    for core in range(ncores):
        r = res.results[core]
        y = r["yT"].transpose(1, 0, 2).reshape(D, cfg.TALL).T
        sl = slice(core * NPS, (core + 1) * NPS)
        y_prompt[sl] = y[:cfg.TP].reshape(NPS, SEQ, D)
        if core < DB:
            y_sample[core] = y[cfg.TP:]
        new_s5[sl] = s5_unpack(cfg, r["news5"])
        nk, nv, rw = even_unpack(cfg, r)
        new_k[sl], new_v[sl], new_rwkv[sl] = nk, nv, rw_unpack(cfg, r["newrwT"])
    return (y_prompt, y_sample, new_k, new_v, new_rwkv, new_s5)
```

```python
import contextlib
import numpy as np
import concourse.bass as bass
import concourse.mybir as mybir
from concourse.bass_utils import run_bass_kernel_spmd

F32 = mybir.dt.float32
F32R = mybir.dt.float32r
AF = mybir.ActivationFunctionType
ALU = mybir.AluOpType
AX = mybir.AxisListType

EPOCH = 20000
DMA_SEMS = 32
SAME_ENG_WINDOW = 3


class Buf:
    __slots__ = ("name", "w", "r")

    def __init__(self, name):
        self.name = name
        self.w = None
        self.r = []


class Op:
    __slots__ = ("eng", "fn", "deps", "idx", "eidx", "signal", "sem", "val", "dma", "slot")


class Sched:
    ENGS = ("pe", "act", "dve", "pool", "sp")

    def __init__(self, nc):
        self.nc = nc
        self.ops = []
        self.eng_ops = {e: [] for e in self.ENGS}

    def op(self, eng, fn, reads=(), writes=(), dma=False):
        o = Op()
        o.eng, o.fn, o.dma = eng, fn, dma
        o.idx = len(self.ops)
        o.eidx = len(self.eng_ops[eng])
        deps = set()
        for b in reads:
            if b.w is not None:
                deps.add(b.w)
        for b in writes:
            if b.w is not None:
                deps.add(b.w)
            deps.update(b.r)
        o.deps = deps
        o.signal = False
        o.sem = None
        o.val = 0
        o.slot = 0
        for b in reads:
            b.r.append(o.idx)
        for b in writes:
            b.w = o.idx
            b.r = []
        self.ops.append(o)
        self.eng_ops[eng].append(o)
        return o

    def finalize(self, stack):
        nc = self.nc
        ops = self.ops
        for o in ops:
            nd = []
            for d in o.deps:
                p = ops[d]
                if p.eng == o.eng and not p.dma:
                    if o.eng == "pe" and not o.dma:
                        continue
                    if o.dma or (o.eidx - p.eidx) <= SAME_ENG_WINDOW:
                        nd.append(d)
                    continue
                nd.append(d)
            o.deps = nd
            for d in nd:
                ops[d].signal = True
        for o in ops:
            if o.dma:
                o.signal = True
        self.final_waits = []
        for e in self.ENGS:
            cnt = 0
            sems = []
            dsems = [stack.enter_context(nc.semaphore(f"d_{e}_{i}")) for i in range(DMA_SEMS)] \
                if any(o.dma for o in self.eng_ops[e]) else []
            dcnt = [0] * DMA_SEMS
            dprev = [None] * DMA_SEMS
            k = 0
            for o in self.eng_ops[e]:
                if o.dma:
                    s = k % DMA_SEMS
                    k += 1
                    if dprev[s] is not None:
                        o.deps.append(dprev[s])
                    dcnt[s] += 16
                    o.sem, o.val = dsems[s], dcnt[s]
                    dprev[s] = o.idx
                elif o.signal:
                    ep = cnt // EPOCH
                    if ep >= len(sems):
                        sems.append(stack.enter_context(nc.semaphore(f"e_{e}_{ep}")))
                    o.sem, o.val = sems[ep], (cnt % EPOCH) + 1
                    cnt += 1
            for s in range(DMA_SEMS):
                if dcnt[s]:
                    self.final_waits.append((dsems[s], dcnt[s]))
            if cnt:
                self.final_waits.append((sems[-1], ((cnt - 1) % EPOCH) + 1))

    def emit(self, block):
        nc = self.nc
        ops = self.ops

        def run(eng_name, eng, last=False):
            known = {}
            for o in self.eng_ops[eng_name]:
                need = {}
                for d in o.deps:
                    p = ops[d]
                    key = id(p.sem)
                    if known.get(key, 0) >= p.val:
                        continue
                    if key not in need or need[key][1] < p.val:
                        need[key] = (p.sem, p.val)
                for key, (sem, val) in need.items():
                    eng.wait_ge(sem, val)
                    known[key] = val
                ins = o.fn(eng)
                if o.signal:
                    ins.then_inc(o.sem, 16 if o.dma else 1)
            if last:
                for sem, val in self.final_waits:
                    eng.wait_ge(sem, val)

        @block.tensor
        def _(e):
            run("pe", e)

        @block.scalar
        def _(e):
            run("act", e)

        @block.vector
        def _(e):
            run("dve", e)

        @block.gpsimd
        def _(e):
            run("pool", e)

        @block.sync
        def _(e):
            run("sp", e, last=True)


def _barrier(self):
    deps = set()
    for e in self.ENGS:
        if self.eng_ops[e]:
            deps.add(self.eng_ops[e][-1].idx)
    for o in self.ops[getattr(self, "_bar_from", 0):]:
        if o.dma:
            deps.add(o.idx)
    self._bar_from = len(self.ops)
    for e in self.ENGS:
        o = self.op(e, (lambda eng: eng.nop()))
        o.deps = set(deps)


Sched.barrier = _barrier


D = 1024
FF = 2816
KC = 8
FC = 22
TN = 512
HD = 64
RMS_EPS = 1e-6
GN_EPS = 64e-5
TWO_PI = 6.283185307179586
PI = 3.141592653589793


class Cfg:
    def __init__(self, SEQ=256, DEC_SEQ=4096, DEPTH=4, PAST=256, NPS=2):
        self.SEQ, self.DEC_SEQ, self.DEPTH, self.PAST, self.NPS = SEQ, DEC_SEQ, DEPTH, PAST, NPS
        self.TP = SEQ * NPS
        self.TALL = self.TP + DEC_SEQ
        self.NTILES = self.TALL // TN
        self.NAB = (DEPTH + 1) // 2
        self.NCL = DEPTH // 2
        assert self.TP == TN and DEC_SEQ % TN == 0


class KB:
    def __init__(self, cfg):
        self.cfg = cfg
        self.nc = bass.Bass("TRN2", target_bir_lowering=False)
        self.S = Sched(self.nc)
        self.din = {}
        self.dout = {}
        self.bufs = {}

    def inp(self, name, shape):
        t = self.nc.dram_tensor(name, list(shape), F32, kind="ExternalInput").ap()
        self.din[name] = (t, tuple(shape))
        self.bufs[name] = Buf(name)
        return t

    def outp(self, name, shape):
        t = self.nc.dram_tensor(name, list(shape), F32, kind="ExternalOutput").ap()
        self.dout[name] = (t, tuple(shape))
        self.bufs[name] = Buf(name)
        return t

    def scratch(self, name, shape):
        t = self.nc.dram_tensor(name, list(shape), F32, kind="Internal").ap()
        self.bufs[name] = Buf(name)
        return t

    def B(self, name):
        if name not in self.bufs:
            self.bufs[name] = Buf(name)
        return self.bufs[name]

    def bl(self, names):
        return [self.B(n) if isinstance(n, str) else n for n in names]

    def MM(self, out, lhsT, rhs, start, stop, R, W):
        self.S.op("pe", lambda e: e.matmul(out, lhsT, rhs, start=start, stop=stop), self.bl(R), self.bl(W))

    def TR(self, out, in_, ident, R, W):
        self.S.op("pe", lambda e: e.transpose(out, in_, ident), self.bl(R), self.bl(W))

    def ACT(self, out, in_, func, R, W, scale=1.0, bias=None):
        if bias is None:
            self.S.op("act", lambda e: e.activation(out=out, in_=in_, func=func, scale=scale), self.bl(R), self.bl(W))
        else:
            self.S.op("act", lambda e: e.activation(out=out, in_=in_, func=func, scale=scale, bias=bias),
                      self.bl(R), self.bl(W))

    def TT(self, out, a, b, op, R, W, eng="dve"):
        self.S.op(eng, lambda e: e.tensor_tensor(out=out, in0=a, in1=b, op=op), self.bl(R), self.bl(W))

    def TS(self, out, a, s1, s2, op0, op1, R, W):
        if s2 is None:
            self.S.op("dve", lambda e: e.tensor_scalar(out=out, in0=a, scalar1=s1, scalar2=None, op0=op0),
                      self.bl(R), self.bl(W))
        else:
            self.S.op("dve", lambda e: e.tensor_scalar(out=out, in0=a, scalar1=s1, scalar2=s2, op0=op0, op1=op1),
                      self.bl(R), self.bl(W))

    def STT(self, out, a, s, b, op0, op1, R, W):
        self.S.op("dve", lambda e: e.scalar_tensor_tensor(out=out, in0=a, scalar=s, in1=b, op0=op0, op1=op1),
                  self.bl(R), self.bl(W))

    def RED(self, out, in_, R, W):
        self.S.op("dve", lambda e: e.tensor_reduce(out=out, in_=in_, axis=AX.X, op=ALU.add), self.bl(R), self.bl(W))

    def RCP(self, out, in_, R, W):
        self.S.op("dve", lambda e: e.reciprocal(out=out, in_=in_), self.bl(R), self.bl(W))

    def CP(self, out, in_, R, W, eng="dve"):
        self.S.op(eng, lambda e: e.tensor_copy(out=out, in_=in_), self.bl(R), self.bl(W))

    def MSET(self, out, val, W, eng="dve"):
        self.S.op(eng, lambda e: e.memset(out, val), [], self.bl(W))

    def SCAN(self, out, d0, d1, init, R, W):
        self.S.op("dve", lambda e: e.tensor_tensor_scan(out=out, data0=d0, data1=d1, initial=init,
                                                       op0=ALU.mult, op1=ALU.add), self.bl(R), self.bl(W))

    def DMA(self, out, in_, R, W, q="sp"):
        self.S.op(q, lambda e: e.dma_start(out=out, in_=in_), self.bl(R), self.bl(W), dma=True)

    def arena_reset(self):
        self.S.barrier()
        self.aoff = 0
        self.gen = getattr(self, "gen", 0) + 1

    def alloc(self, name, shape):
        n = int(np.prod(shape[1:]))
        v = self.AR[0:shape[0], self.aoff:self.aoff + n]
        self.aoff += n
        assert self.aoff <= self.ARN, (name, self.aoff)
        if len(shape) == 3:
            v = v.rearrange("p (a b) -> p a b", a=shape[1])
        elif len(shape) == 4:
            v = v.rearrange("p (a b c) -> p a b c", a=shape[1], b=shape[2])
        return v, self.B(f"{name}@{self.gen}")

    def build(self):
        cfg, nc = self.cfg, self.nc
        L, TALL, NTL = cfg.DEPTH, cfg.TALL, cfg.NTILES
        st = contextlib.ExitStack()
        self.st = st
        I = self.inp
        self.xT = I("xT", [128, 8, TALL])
        self.condT = I("condT", [128, 8, 2])
        self.ada_w = I("ada_w", [L, 1024, 9216])
        self.ada_bT = I("ada_bT", [128, L * 72])
        self.npreT = I("npreT", [128, L * 24])
        self.npostT = I("npostT", [128, L * 24])
        self.w1L = I("w1L", [L * 2 * FC, 128, 1024])
        self.w3L = I("w3L", [L * 2 * FC, 128, 1024])
        self.w2L = I("w2L", [L * 2 * 8, 128, FC * 128])
        self.consts = I("consts", [128, 1024])
        self.yT = self.outp("yT", [128, 8, TALL])
        NCL, NPS = cfg.NCL, cfg.NPS
        self.s5inL = I("s5inL", [max(NCL, 1) * 8, 128, 1024])
        self.s5gluL = I("s5gluL", [max(NCL, 1) * 8, 128, 1024])
        self.s5outL = I("s5outL", [max(NCL, 1) * 8, 128, 1024])
        self.s5lam = I("s5lam", [max(NCL, 1), 128, 3, 64])
        self.s5s0 = I("s5s0", [max(NCL, 1), 128, 2, 64])
        self.s5b = I("s5b", [max(NCL, 1) * 4, 4096, 16])
        self.s5c = I("s5c", [max(NCL, 1) * 4, 4096, 16])
        self.s5d = I("s5d", [max(NCL, 1), 128, 8])
        self.news5 = self.outp("news5", [max(NCL, 1) * NPS * 4, 128, 32])
        self.uT = self.scratch("uT", [128, 8, TALL])
        self.yfT = self.scratch("yfT", [128, 8, TALL])
        self.zT = self.scratch("zT", [128, 8, TALL])
        self.even_decl()
        self.I32 = st.enter_context(nc.sbuf_tensor("i32", [128, 512], mybir.dt.int32))
        self.xres = self.scratch("xres", [128, 8, TALL])
        self.ARN = 49000
        self.AR = st.enter_context(nc.sbuf_tensor("arena", [128, self.ARN], F32))
        self.CT = st.enter_context(nc.sbuf_tensor("cten", [128, 1024], F32))
        self.MOD = st.enter_context(nc.sbuf_tensor("mods", [128, L * 3 * 3 * 8 * 2], F32))
        self.CS = st.enter_context(nc.sbuf_tensor("cst", [128, 8], F32))
        self.PS = [st.enter_context(nc.psum_tensor(f"ps{i}", [128, 512], F32)) for i in range(8)]
        self.pb = [self.B(f"psum{i}") for i in range(8)]
        self.ident = self.CT[:, 0:128]
        self.ones = self.CT[:, 128:256]
        self.bones = self.CT[:, 256:384]
        self.J = self.CT[:, 384:512]
        self.DMA(self.CT[:], self.consts[:, :], [], ["CT"])
        self.MSET(self.CS[:, 0:1], RMS_EPS, ["CS"])
        self.MSET(self.CS[:, 1:2], GN_EPS, ["CS"])
        self.MSET(self.CS[:, 2:3], 1.0, ["CS"])
        self.MSET(self.CS[:, 3:4], 0.0, ["CS"])
        self.DMA(self.xres[:, :, :], self.xT[:, :, :], [], ["xres"])
        self.gen = 0
        self.aoff = 0
        self.adaln()
        for l in range(L):
            self.ffn_sub(l, 0)
            if l % 2 == 0:
                self.mixer_even(l)
            else:
                self.mixer_odd(l)
            self.ffn_sub(l, 2)
        self.arena_reset()
        self.DMA(self.yT[:, :, :], self.xres[:, :, :], ["xres"], ["yT"], q="pool")
        self.S.finalize(st)
        with nc.Block() as block:
            self.S.emit(block)
        st.close()
        return nc

    def modv(self, l, s, j, cond):
        base = (((l * 3 + s) * 3 + j) * 8) * 2
        return self.MOD[:, base:base + 16].rearrange("p (e c) -> p e c", c=2)[:, :, cond]

    def adaln(self):
        cfg = self.cfg
        L = cfg.DEPTH
        self.arena_reset()
        sc, bsc = self.alloc("sc", [128, 8, 2])
        sg, bsg = self.alloc("sg", [128, 8, 2])
        ab, bab = self.alloc("ab", [128, L * 72])
        pre, bpre = self.alloc("pre", [128, L * 24])
        post, bpost = self.alloc("post", [128, L * 24])
        raw, braw = self.alloc("raw", [128, L * 9 * 8 * 2])
        wbuf = [self.alloc(f"adw{i}", [128, 8, 1024]) for i in range(2)]
        self.DMA(sc, self.condT[:, :, :], [], [bsc])
        self.DMA(ab, self.ada_bT[:, :], [], [bab])
        self.DMA(pre, self.npreT[:, :], [], [bpre])
        self.DMA(post, self.npostT[:, :], [], [bpost])
        self.ACT(sg, sc, AF.Silu, [bsc], [bsg])
        n = 0
        for l in range(L):
            for sj in range(9):
                wv, wb = wbuf[n % 2]
                src = self.ada_w[l].rearrange("(k p) e -> p k e", p=128)[:, :, sj * 1024:(sj + 1) * 1024]
                self.DMA(wv[:, 0:4, :], src[:, 0:4, :], [], [wb])
                self.DMA(wv[:, 4:8, :], src[:, 4:8, :], [], [wb], q="pool")
                ps = self.PS[n % 2]
                pbuf = self.pb[n % 2]
                for e in range(8):
                    for k in range(8):
                        self.MM(ps[:, e * 2:e * 2 + 2], wv[:, k, e * 128:(e + 1) * 128], sg[:, k, :],
                                k == 0, k == 7, [wb, bsg], [pbuf])
                o = (l * 9 + sj) * 16
                bo = (l * 9 + sj) * 8
                self.TT(raw[:, o:o + 16].rearrange("p (e c) -> p e c", c=2),
                        ps[:, 0:16].rearrange("p (e c) -> p e c", c=2),
                        ab[:, bo:bo + 8].unsqueeze(2).broadcast_to([128, 8, 2]), ALU.add,
                        [pbuf, bab], [braw])
                n += 1
        for l in range(L):
            for s in range(3):
                def rawv(j):
                    o = (l * 9 + s * 3 + j) * 16
                    return raw[:, o:o + 16].rearrange("p (e c) -> p e c", c=2)
                def modw(j):
                    base = (((l * 3 + s) * 3 + j) * 8) * 2
                    return self.MOD[:, base:base + 16].rearrange("p (e c) -> p e c", c=2)
                g8 = (l * 3 + s) * 8
                pg = pre[:, g8:g8 + 8].unsqueeze(2).broadcast_to([128, 8, 2])
                qg = post[:, g8:g8 + 8].unsqueeze(2).broadcast_to([128, 8, 2])
                self.STT(modw(0), rawv(1), 1.0, pg, ALU.add, ALU.mult, [braw, bpre], ["MOD"])
                self.CP(modw(1), rawv(0), [braw], ["MOD"])
                self.STT(modw(2), rawv(2), 0.5 if s != 1 else 1.0, qg, ALU.mult, ALU.mult, [braw, bpost], ["MOD"])

    def rstd_of(self, src, bsrc, name):
        sq, bsq = self.alloc(name + "_sq", [128, 8, TN])
        r, br = self.alloc(name + "_r", [128, TN])
        self.TT(sq, src, src, ALU.mult, [bsrc], [bsq])
        ps, pbuf = self.PS[7], self.pb[7]
        for c in range(8):
            self.MM(ps[:, :], self.ones, sq[:, c, :], c == 0, c == 7, ["CT", bsq], [pbuf])
        self.ACT(r, ps[:, :], AF.Sqrt, [pbuf, "CS"], [br], scale=1.0 / D, bias=self.CS[:, 0:1])
        self.RCP(r, r, [br], [br])
        return r, br

    def modin(self, x, bx, l, s, cond, name="h"):
        r, br = self.rstd_of(x, bx, name)
        h, bh = self.alloc(name, [128, 8, TN])
        A = self.modv(l, s, 0, cond)
        Bv = self.modv(l, s, 1, cond)
        for c in range(8):
            self.STT(h[:, c, :], x[:, c, :], A[:, c:c + 1], r, ALU.mult, ALU.mult, [bx, br, "MOD"], [bh])
            self.TS(h[:, c, :], h[:, c, :], Bv[:, c:c + 1], None, ALU.add, None, [bh, "MOD"], [bh])
        return h, bh

    def resid(self, x, bx, y, by, l, s, cond):
        r, br = self.rstd_of(y, by, "yn")
        G = self.modv(l, s, 2, cond)
        for c in range(8):
            self.STT(y[:, c, :], y[:, c, :], G[:, c:c + 1], r, ALU.mult, ALU.mult, [by, br, "MOD"], [by])
        self.TT(x, x, y, ALU.add, [bx, by], [bx])

    def tile_cond(self, t):
        return 0 if t == 0 else 1

    def ffn_sub(self, l, s):
        i = 0 if s == 0 else 1
        for t in range(self.cfg.NTILES):
            cond = self.tile_cond(t)
            self.arena_reset()
            x, bx = self.alloc("x", [128, 8, TN])
            self.DMA(x, self.xres[:, :, t * TN:(t + 1) * TN], ["xres"], [bx])
            h, bh = self.modin(x, bx, l, s, cond)
            a, ba = self.alloc("a", [128, FC, TN])
            y, by = self.alloc("y", [128, 8, TN])
            sl, bsl = self.alloc("sl", [128, TN])
            w1 = [self.alloc(f"w1_{k}", [128, 8, 128]) for k in range(2)]
            w3 = [self.alloc(f"w3_{k}", [128, 8, 128]) for k in range(2)]
            w2 = [self.alloc(f"w2_{k}", [128, FC, 128]) for k in range(2)]
            for j in range(FC):
                (w1v, w1b), (w3v, w3b) = w1[j % 2], w3[j % 2]
                row = (l * 2 + i) * FC + j
                self.DMA(w1v, self.w1L[row].rearrange("p (k m) -> p k m", k=8), [], [w1b])
                self.DMA(w3v, self.w3L[row].rearrange("p (k m) -> p k m", k=8), [], [w3b], q="pool")
                pg, pu = self.PS[(j % 2) * 2], self.PS[(j % 2) * 2 + 1]
                bg, bu = self.pb[(j % 2) * 2], self.pb[(j % 2) * 2 + 1]
                for k in range(8):
                    self.MM(pg[:, :], w1v[:, k, :], h[:, k, :], k == 0, k == 7, [w1b, bh], [bg])
                for k in range(8):
                    self.MM(pu[:, :], w3v[:, k, :], h[:, k, :], k == 0, k == 7, [w3b, bh], [bu])
                self.ACT(sl, pg[:, :], AF.Silu, [bg], [bsl])
                self.TT(a[:, j, :], sl, pu[:, :], ALU.mult, [bsl, bu], [ba])
            for e in range(8):
                w2v, w2b = w2[e % 2]
                row = (l * 2 + i) * 8 + e
                w2src = self.w2L[row].rearrange("p (k m) -> p k m", k=FC)
                self.DMA(w2v[:, 0:11, :], w2src[:, 0:11, :], [], [w2b])
                self.DMA(w2v[:, 11:FC, :], w2src[:, 11:FC, :], [], [w2b], q="pool")
                py, bpy = self.PS[4 + e % 2], self.pb[4 + e % 2]
                for k in range(FC):
                    self.MM(py[:, :], w2v[:, k, :], a[:, k, :], k == 0, k == FC - 1, [w2b, ba], [bpy])
                self.ACT(y[:, e, :], py[:, :], AF.Copy, [bpy], [by])
            self.resid(x, bx, y, by, l, s, cond)
            self.DMA(self.xres[:, :, t * TN:(t + 1) * TN], x, [bx], ["xres"], q="pool")

    def proj8(self, wL, row0, src, bsrc, dst, bdst, wname="wq"):
        wb = [self.alloc(f"{wname}{k}", [128, 8, 128]) for k in range(2)]
        for e in range(8):
            wv, wbb = wb[e % 2]
            self.DMA(wv, wL[row0 + e].rearrange("p (k m) -> p k m", k=8), [], [wbb])
            ps, pbuf = self.PS[e % 2], self.pb[e % 2]
            for k in range(8):
                self.MM(ps[:, :], wv[:, k, :], src[:, k, :], k == 0, k == 7, [wbb, bsrc], [pbuf])
            self.ACT(dst[:, e, :], ps[:, :], AF.Copy, [pbuf], [bdst])

    def sincos(self, th, bth, n, name):
        out = []
        for which, shift in (("s", 0.0), ("c", PI / 2)):
            a, ba = self.alloc(f"{name}_{which}", [128, n])
            k, bk = self.alloc(f"{name}_{which}k", [128, n])
            ki = self.I32[:, 0:n]
            self.TS(a, th, shift, None, ALU.add, None, [bth], [ba])
            self.TS(k, a, 1.0 / TWO_PI, None, ALU.mult, None, [ba], [bk])
            self.CP(ki, k, [bk], ["I32"])
            self.CP(k, ki, ["I32"], [bk])
            self.STT(a, k, -TWO_PI, a, ALU.mult, ALU.add, [bk, ba], [ba])
            self.TS(k, a, PI, -TWO_PI, ALU.is_gt, ALU.mult, [ba], [bk])
            self.TT(a, a, k, ALU.add, [ba, bk], [ba])
            self.TS(k, a, -PI, TWO_PI, ALU.is_lt, ALU.mult, [ba], [bk])
            self.TT(a, a, k, ALU.add, [ba, bk], [ba])
            self.ACT(a, a, AF.Sin, [ba], [ba])
            out.append((a, ba))
        return out[0], out[1]

    def cmul(self, outr, outi, ar, ai, br_, bi_, tmp, R, W, btmp, eng="dve"):
        self.TT(tmp, ai, bi_, ALU.mult, R, [btmp], eng=eng)
        self.TT(outr, ar, br_, ALU.mult, R, W, eng=eng)
        self.TT(outr, outr, tmp, ALU.subtract, list(W) + [btmp], W, eng=eng)
        self.TT(tmp, ai, br_, ALU.mult, R, [btmp], eng=eng)
        self.TT(outi, ar, bi_, ALU.mult, R, W, eng=eng)
        self.TT(outi, outi, tmp, ALU.add, list(W) + [btmp], W, eng=eng)

    def cmul_s(self, outr, outi, sr, si, br_, bi_, tmp1, R, W, btmp):
        self.TS(tmp1, bi_, si, None, ALU.mult, None, R, [btmp])
        self.STT(outr, br_, sr, tmp1, ALU.mult, ALU.subtract, list(R) + [btmp], W)
        self.TS(tmp1, br_, si, None, ALU.mult, None, R, [btmp])
        self.STT(outi, bi_, sr, tmp1, ALU.mult, ALU.add, list(R) + [btmp], W)

    def mixer_odd(self, l):
        cfg = self.cfg
        i = l // 2
        NTL = cfg.NTILES
        for t in range(NTL):
            self.arena_reset()
            x, bx = self.alloc("x", [128, 8, TN])
            self.DMA(x, self.xres[:, :, t * TN:(t + 1) * TN], ["xres"], [bx])
            h, bh = self.modin(x, bx, l, 1, self.tile_cond(t))
            u, bu = self.alloc("u", [128, 8, TN])
            self.proj8(self.s5inL, i * 8, h, bh, u, bu)
            self.DMA(self.uT[:, :, t * TN:(t + 1) * TN], u, [bu], ["uT"], q="pool")
        seqs = [(s * cfg.SEQ, cfg.SEQ, None, s) for s in range(cfg.NPS)] + [(cfg.TP, cfg.DEC_SEQ, i, None)]
        for d in range(2):
            self.arena_reset()
            NB = 32
            lam, blam = self.alloc("lam", [128, 6, 64])
            self.DMA(lam[:, 0:3, :], self.s5lam[i], [], [blam])
            self.DMA(lam[:, 3:5, :], self.s5s0[i], [], [blam])
            cs = slice(d * 32, d * 32 + 32)
            dsc, bdsc = self.alloc("dsc", [128, 12, 32])
            lr, li, ldt = lam[:, 0, cs], lam[:, 1, cs], lam[:, 2, cs]
            dt, lrdt, th, den, abr, abi, fr, fi, nlrdt, t0_, t1_, mag = [dsc[:, k, :] for k in range(12)]
            R0 = [blam, bdsc]
            self.ACT(dt, ldt, AF.Exp, [blam], [bdsc])
            self.TT(lrdt, lr, dt, ALU.mult, R0, [bdsc])
            self.TS(nlrdt, lrdt, -1.0, None, ALU.mult, None, R0, [bdsc])
            self.TT(th, li, dt, ALU.mult, R0, [bdsc])
            self.TT(den, lr, lr, ALU.mult, R0, [bdsc])
            self.TT(t0_, li, li, ALU.mult, R0, [bdsc])
            self.TT(den, den, t0_, ALU.add, R0, [bdsc])
            self.RCP(den, den, R0, [bdsc])
            self.ACT(mag, lrdt, AF.Exp, R0, [bdsc])
            (sn, bsn), (cn, bcn) = self.sincos(th, bdsc, 32, "ab")
            self.TT(abr, mag, cn, ALU.mult, [bdsc, bcn], [bdsc])
            self.TT(abi, mag, sn, ALU.mult, [bdsc, bsn], [bdsc])
            self.TS(t0_, abr, -1.0, None, ALU.add, None, R0, [bdsc])
            self.TT(fr, t0_, lr, ALU.mult, R0, [bdsc])
            self.TT(t1_, abi, li, ALU.mult, R0, [bdsc])
            self.TT(fr, fr, t1_, ALU.add, R0, [bdsc])
            self.TT(fr, fr, den, ALU.mult, R0, [bdsc])
            self.TT(fi, abi, lr, ALU.mult, R0, [bdsc])
            self.TT(t1_, t0_, li, ALU.mult, R0, [bdsc])
            self.TT(fi, fi, t1_, ALU.subtract, R0, [bdsc])
            self.TT(fi, fi, den, ALU.mult, R0, [bdsc])
            POSr, bT = self.alloc("POSr", [128, NB, 128])
            POSi, _ = self.alloc("POSi", [128, NB, 128])
            NEGr, _ = self.alloc("NEGr", [128, NB, 128])
            NEGi, _ = self.alloc("NEGi", [128, NB, 128])
            BBr, bBB = self.alloc("BBr", [128, NB, 128])
            BBi, _ = self.alloc("BBi", [128, NB, 128])
            CCr, bCC = self.alloc("CCr", [128, NB, 128])
            CCi, _ = self.alloc("CCi", [128, NB, 128])
            A128, bA128 = self.alloc("A128", [128, 2, NB])
            iota = self.CT[:, 512:640]
            mark = self.aoff
            for pb in range(NB):
                self.aoff = mark
                ang, bang = self.alloc("ang", [128, 128])
                mp, bmp = self.alloc("mp", [128, 128])
                self.TS(ang, iota, th[:, pb:pb + 1], None, ALU.mult, None, ["CT", bdsc], [bang])
                (sn, bsn), (cn, bcn) = self.sincos(ang, bang, 128, "tb")
                self.ACT(mp, iota, AF.Exp, ["CT", bdsc], [bmp], scale=lrdt[:, pb:pb + 1])
                self.TT(POSr[:, pb, :], mp, cn, ALU.mult, [bmp, bcn], [bT])
                self.TT(POSi[:, pb, :], mp, sn, ALU.mult, [bmp, bsn], [bT])
                self.ACT(mp, iota, AF.Exp, ["CT", bdsc], [bmp], scale=nlrdt[:, pb:pb + 1])
                self.TT(NEGr[:, pb, :], mp, cn, ALU.mult, [bmp, bcn], [bT])
                self.STT(NEGi[:, pb, :], mp, -1.0, sn, ALU.mult, ALU.mult, [bmp, bsn], [bT])
                a128t, ba128t = self.alloc("a128t", [128, 1])
                self.cmul_s(A128[:, 0, pb:pb + 1], A128[:, 1, pb:pb + 1], abr[:, pb:pb + 1], abi[:, pb:pb + 1],
                            POSr[:, pb, 127:128], POSi[:, pb, 127:128], a128t, [bT, bdsc], [bA128], ba128t)
                braw, bbraw = self.alloc("braw", [128, 2, 16])
                self.DMA(braw[:, 0, :], self.s5b[(i * 2 + d) * 2 + 0][pb * 128:(pb + 1) * 128, :], [], [bbraw])
                self.DMA(braw[:, 1, :], self.s5b[(i * 2 + d) * 2 + 1][pb * 128:(pb + 1) * 128, :], [], [bbraw])
                srcr, bsr = self.alloc("srcr", [128, 128])
                srci, bsi = self.alloc("srci", [128, 128])
                tb, btb = self.alloc("tb16", [128, 16])
                self.MSET(srcr, 0.0, [bsr])
                self.MSET(srci, 0.0, [bsi])
                off = 32 * (pb % 4)
                frp, fip = fr[:, pb:pb + 1], fi[:, pb:pb + 1]
                for g2 in range(2):
                    rs = slice(g2 * 64, g2 * 64 + 64)
                    c0 = off + g2 * 16
                    self.TS(tb[rs, :], braw[rs, 1, :], fip[rs, :], None, ALU.mult, None, [bbraw, bdsc], [btb])
                    self.STT(srcr[rs, c0:c0 + 16], braw[rs, 0, :], frp[rs, :], tb[rs, :], ALU.mult, ALU.subtract,
                             [bbraw, bdsc, btb], [bsr])
                    self.TS(tb[rs, :], braw[rs, 0, :], fip[rs, :], None, ALU.mult, None, [bbraw, bdsc], [btb])
                    self.STT(srci[rs, c0:c0 + 16], braw[rs, 1, :], frp[rs, :], tb[rs, :], ALU.mult, ALU.add,
                             [bbraw, bdsc, btb], [bsi])
                self.TR(self.PS[0][:, 0:128], srcr, self.ident, [bsr, "CT"], [self.pb[0]])
                self.ACT(BBr[:, pb, :], self.PS[0][:, 0:128], AF.Copy, [self.pb[0]], [bBB])
                self.TR(self.PS[1][:, 0:128], srci, self.ident, [bsi, "CT"], [self.pb[1]])
                self.ACT(BBi[:, pb, :], self.PS[1][:, 0:128], AF.Copy, [self.pb[1]], [bBB])
                self.MSET(CCr[:, pb, :], 0.0, [bCC])
                self.MSET(CCi[:, pb, :], 0.0, [bCC])
                for g2 in range(2):
                    rs = slice(g2 * 64, g2 * 64 + 64)
                    c0 = off + g2 * 16
                    r0 = pb * 128 + g2 * 64
                    self.DMA(CCr[rs, pb, c0:c0 + 16], self.s5c[(i * 2 + d) * 2 + 0][r0:r0 + 64, :], [], [bCC])
                    self.DMA(CCi[rs, pb, c0:c0 + 16], self.s5c[(i * 2 + d) * 2 + 1][r0:r0 + 64, :], [], [bCC])
            self.TS(CCi, CCi, -1.0, None, ALU.mult, None, [bCC], [bCC])
            self.aoff = mark
            for (tok0, Ls, sidx, pidx) in seqs:
                self.aoff = mark
                self.S.barrier()
                NS = min(TN, Ls)
                nseg = Ls // NS
                nch = NS // 128
                hcr, bhc = self.alloc("hcr", [128, NB])
                hci, _ = self.alloc("hci", [128, NB])
                hout, bho = self.alloc("hout", [128, 2, NB])
                bhcs = [self.B(f"hc{pb_}@{self.gen}") for pb_ in range(NB)]
                if sidx is None:
                    self.MSET(hcr, 0.0, [bhc] + bhcs)
                    self.MSET(hci, 0.0, [bhc] + bhcs)
                else:
                    mr = abr if d == 0 else A128[:, 0, :]
                    mi = abi if d == 0 else A128[:, 1, :]
                    tq, btq = self.alloc("tq", [128, NB])
                    self.cmul(hcr, hci, mr, mi, lam[:, 3, cs], lam[:, 4, cs], tq, [blam, bdsc, bA128], [bhc] + bhcs, btq)
                u, bu = self.alloc("u", [128, 8, NS])
                yv, byv = self.alloc("yv", [128, 8, NS])
                XS = []
                for k_ in range(2):
                    XS.append((self.alloc(f"xr{k_}", [128, NS]), self.alloc(f"xi{k_}", [128, NS])[0],
                               self.alloc(f"pr{k_}", [128, NS]), self.alloc(f"pi{k_}", [128, NS])[0],
                               self.alloc(f"tm{k_}", [128, NS]), self.alloc(f"tm2{k_}", [128, NS]),
                               self.alloc(f"sm{k_}", [128, 8])))
                tm, btm = XS[0][4]
                dsk, bdsk = self.alloc("dsk", [128, 8])
                self.DMA(dsk, self.s5d[i], [], [bdsk])
                segs = list(range(nseg)) if d == 0 else list(range(nseg - 1, -1, -1))
                for sg_ in segs:
                    ta = tok0 + sg_ * NS
                    self.DMA(u, self.uT[:, :, ta:ta + NS], ["uT"], [bu])
                    if d == 1:
                        self.DMA(yv, self.yfT[:, :, ta:ta + NS], ["yfT"], [byv])
                    for uc in range(8):
                        py, bpy = self.PS[4 + uc % 2], self.pb[4 + uc % 2]
                        def _unit(q, uc=uc, py=py, bpy=bpy):
                            pb = uc * 4 + q
                            par_ = pb % 2
                            (xr, bxx), xi, (pr, bpp), pi_, (tm, btm), (tm2, btm2), (sm, bsm) = XS[par_]
                            bhc = bhcs[pb]
                            p0, p1 = self.PS[2 * par_], self.PS[2 * par_ + 1]
                            bp0, bp1 = self.pb[2 * par_], self.pb[2 * par_ + 1]
                            self.MM(p0[:, 0:NS], BBr[:, pb, :], u[:, uc, :], True, True, [bBB, bu], [bp0])
                            self.MM(p1[:, 0:NS], BBi[:, pb, :], u[:, uc, :], True, True, [bBB, bu], [bp1])
                            T1r, T1i = (NEGr, NEGi) if d == 0 else (POSr, POSi)
                            T2r, T2i = (POSr, POSi) if d == 0 else (NEGr, NEGi)
                            def bc(T):
                                return T[:, pb, :].unsqueeze(1).broadcast_to([128, nch, 128])
                            def v3(a_):
                                return a_.rearrange("p (c s) -> p c s", s=128)
                            self.cmul(v3(xr), v3(xi), bc(T1r), bc(T1i), v3(p0[:, 0:NS]), v3(p1[:, 0:NS]), v3(tm),
                                      [bT, bp0, bp1], [bxx], btm)
                            yield
                            chs = list(range(nch)) if d == 0 else list(range(nch - 1, -1, -1))
                            cir, cii = hcr[:, pb:pb + 1], hci[:, pb:pb + 1]
                            a128r, a128i = A128[:, 0, pb:pb + 1], A128[:, 1, pb:pb + 1]
                            for c in chs:
                                yield
                                cl = slice(c * 128, (c + 1) * 128)
                                e1 = c * 128 + 127
                                if d == 0:
                                    self.SCAN(pr[:, cl], self.ones, xr[:, cl], cir, ["CT", bxx, bhc], [bpp])
                                    self.SCAN(pi_[:, cl], self.ones, xi[:, cl], cii, ["CT", bxx, bhc], [bpp])
                                    self.cmul_s(cir, cii, a128r, a128i, pr[:, e1:e1 + 1], pi_[:, e1:e1 + 1], sm[:, 0:1],
                                                [bpp, bA128], [bhc], bsm)
                                else:
                                    self.SCAN(pr[:, cl], self.ones, xr[:, cl], 0.0, ["CT", bxx], [bpp])
                                    self.SCAN(pi_[:, cl], self.ones, xi[:, cl], 0.0, ["CT", bxx], [bpp])
                                    self.TT(sm[:, 4:5], pr[:, e1:e1 + 1], cir, ALU.add, [bpp, bhc], [bsm])
                                    self.TT(sm[:, 5:6], pi_[:, e1:e1 + 1], cii, ALU.add, [bpp, bhc], [bsm])
                                    self.STT(pr[:, cl], xr[:, cl], sm[:, 4:5], pr[:, cl], ALU.add, ALU.subtract, [bxx, bsm, bpp], [bpp])
                                    self.STT(pi_[:, cl], xi[:, cl], sm[:, 5:6], pi_[:, cl], ALU.add, ALU.subtract, [bxx, bsm, bpp], [bpp])
                                    s0 = c * 128
                                    if sg_ == segs[-1] and c == chs[-1]:
                                        self.CP(hout[:, 0, pb:pb + 1], pr[:, s0:s0 + 1], [bpp], [bho])
                                        self.CP(hout[:, 1, pb:pb + 1], pi_[:, s0:s0 + 1], [bpp], [bho])
                                    else:
                                        self.cmul_s(cir, cii, a128r, a128i, pr[:, s0:s0 + 1], pi_[:, s0:s0 + 1], sm[:, 0:1],
                                                    [bpp, bA128], [bhc], bsm)
                            if d == 0 and sg_ == segs[-1]:
                                e1 = (nch - 1) * 128 + 127
                                self.cmul(hout[:, 0, pb:pb + 1], hout[:, 1, pb:pb + 1], POSr[:, pb, 127:128], POSi[:, pb, 127:128],
                                          pr[:, e1:e1 + 1], pi_[:, e1:e1 + 1], sm[:, 3:4], [bT, bpp], [bho], bsm)
                            yield
                            self.cmul(v3(xr), v3(xi), bc(T2r), bc(T2i), v3(pr), v3(pi_), v3(tm2), [bT, bpp], [bxx], btm2, eng="pool")
                            self.MM(py[:, 0:NS], CCr[:, pb, :], xr, q == 0, False, [bCC, bxx], [bpy])
                            self.MM(py[:, 0:NS], CCi[:, pb, :], xi, False, q == 3, [bCC, bxx], [bpy])
                        for q0 in (0, 2):
                            gens = [_unit(q0), _unit(q0 + 1)]
                            while gens:
                                for g_ in list(gens):
                                    try:
                                        next(g_)
                                    except StopIteration:
                                        gens.remove(g_)
                        if d == 0:
                            self.ACT(yv[:, uc, :], py[:, 0:NS], AF.Copy, [bpy], [byv])
                        else:
                            self.TT(yv[:, uc, :], yv[:, uc, :], py[:, 0:NS], ALU.add, [byv, bpy], [byv])
                            self.STT(yv[:, uc, :], u[:, uc, :], dsk[:, uc:uc + 1], yv[:, uc, :], ALU.mult, ALU.add,
                                     [bu, bdsk, byv], [byv])
                            yy = yv[:, uc, :]
                            self.TT(tm, yy, yy, ALU.mult, [byv], [btm])
                            self.TS(tm, tm, 0.044715, 1.0, ALU.mult, ALU.add, [btm], [btm])
                            self.TT(tm, tm, yy, ALU.mult, [btm, byv], [btm])
                            self.ACT(tm, tm, AF.Tanh, [btm], [btm], scale=0.7978845608028654)
                            self.STT(yy, tm, 1.0, yy, ALU.add, ALU.mult, [btm, byv], [byv])
                            self.TS(yy, yy, 0.5, None, ALU.mult, None, [byv], [byv])
                        if d == 1 and uc == 0:
                            pass
                    if d == 0:
                        self.DMA(self.yfT[:, :, ta:ta + NS], yv, [byv], ["yfT"], q="pool")
                    else:
                        self.DMA(self.zT[:, :, ta:ta + NS], yv, [byv], ["zT"], q="pool")
                    if d == 0 and True:
                        pass
                if pidx is not None:
                    o = ((i * cfg.NPS + pidx) * 2 + d) * 2
                    self.DMA(self.news5[o + 0], hout[:, 0, :], [bho], ["news5"], q="pool")
                    self.DMA(self.news5[o + 1], hout[:, 1, :], [bho], ["news5"], q="pool")
        for t in range(NTL):
            self.arena_reset()
            x, bx = self.alloc("x", [128, 8, TN])
            z, bz = self.alloc("z", [128, 8, TN])
            gl, bgl = self.alloc("gl", [128, 8, TN])
            y, by = self.alloc("y", [128, 8, TN])
            self.DMA(x, self.xres[:, :, t * TN:(t + 1) * TN], ["xres"], [bx])
            self.DMA(z, self.zT[:, :, t * TN:(t + 1) * TN], ["zT"], [bz])
            self.proj8(self.s5gluL, i * 8, z, bz, gl, bgl)
            self.ACT(gl, gl, AF.Sigmoid, [bgl], [bgl])
            self.TT(z, z, gl, ALU.mult, [bz, bgl], [bz])
            self.proj8(self.s5outL, i * 8, z, bz, y, by, wname="wo")
            self.resid(x, bx, y, by, l, 1, self.tile_cond(t))
            self.DMA(self.xres[:, :, t * TN:(t + 1) * TN], x, [bx], ["xres"], q="pool")


def _lhsT(W):
    K, M = W.shape
    return np.ascontiguousarray(W.reshape(K // 128, 128, M).transpose(1, 0, 2))


def _fm(v):
    v = np.asarray(v)
    lead = v.shape[:-1]
    n = v.shape[-1] // 128
    a = v.reshape(lead + (n, 128))
    return np.ascontiguousarray(np.moveaxis(a, -1, 0))


def make_consts():
    c = np.zeros((128, 1024), np.float32)
    c[:, 0:128] = np.eye(128, dtype=np.float32)
    c[:, 128:256] = 1.0
    bo = np.zeros((128, 128), np.float32)
    bo[:64, :64] = 1.0
    bo[64:, 64:] = 1.0
    c[:, 256:384] = bo
    c[:, 384:512] = np.eye(128, dtype=np.float32)[::-1]
    c[:, 512:640] = np.arange(128, dtype=np.float32)[None, :]
    Rm = np.zeros((64, 64), np.float32)
    for j in range(16):
        Rm[j, 16 + j] = -1.0
        Rm[16 + j, j] = 1.0
        Rm[32 + j, 48 + j] = -1.0
        Rm[48 + j, 32 + j] = 1.0
    c[0:64, 640:704] = Rm.T
    for h in range(8):
        c[h, 704 + h * 16:704 + (h + 1) * 16] = 1.0
    for u in range(16):
        c[u, 832 + u * 8:832 + (u + 1) * 8] = 1.0
    c[0:64, 960] = 1.0
    c[64:128, 961] = 1.0
    return c


def common_inputs(cfg, P):
    L = cfg.DEPTH
    d = {}
    d["ada_w"] = np.ascontiguousarray(P["ada_w"], dtype=np.float32)
    ab = P["ada_b"].reshape(L, 9, 8, 128)
    d["ada_bT"] = np.ascontiguousarray(ab.transpose(3, 0, 1, 2).reshape(128, L * 72))
    d["npreT"] = np.ascontiguousarray(P["norm_pre"].reshape(L, 3, 8, 128).transpose(3, 0, 1, 2).reshape(128, L * 24))
    d["npostT"] = np.ascontiguousarray(P["norm_post"].reshape(L, 3, 8, 128).transpose(3, 0, 1, 2).reshape(128, L * 24))
    w1 = P["ffn_w1"].reshape(L * 2, 8, 128, FC, 128)
    d["w1L"] = np.ascontiguousarray(w1.transpose(0, 3, 2, 1, 4).reshape(L * 2 * FC, 128, 1024))
    w3 = P["ffn_w3"].reshape(L * 2, 8, 128, FC, 128)
    d["w3L"] = np.ascontiguousarray(w3.transpose(0, 3, 2, 1, 4).reshape(L * 2 * FC, 128, 1024))
    w2 = P["ffn_w2"].reshape(L * 2, FC, 128, 8, 128)
    d["w2L"] = np.ascontiguousarray(w2.transpose(0, 3, 2, 1, 4).reshape(L * 2 * 8, 128, FC * 128))
    d["consts"] = make_consts()
    return d


def core_inputs(cfg, core, x_prompt, x_sample, c, c_ctx):
    NPS = cfg.NPS
    xs = [x_prompt[core * NPS + i] for i in range(NPS)] + [x_sample[core % x_sample.shape[0]]]
    x = np.concatenate(xs, axis=0)
    d = {}
    d["xT"] = np.ascontiguousarray(x.T.reshape(8, 128, cfg.TALL).transpose(1, 0, 2))
    cond = np.stack([c_ctx, c[core % c.shape[0]]], axis=-1)
    d["condT"] = np.ascontiguousarray(cond.reshape(8, 128, 2).transpose(1, 0, 2))
    return d


def _sqL(W):
    n = W.shape[0]
    a = W.reshape(n, 8, 128, 8, 128)
    return np.ascontiguousarray(a.transpose(0, 3, 2, 1, 4).reshape(n * 8, 128, 1024))


def s5_common(cfg, P):
    d = {}
    d["s5inL"] = _sqL(P["s5_w_in"]); d["s5gluL"] = _sqL(P["s5_w_glu"]); d["s5outL"] = _sqL(P["s5_w_out"])
    N = cfg.NCL
    def pl(a):
        return a.reshape(N, 2, 32, 2, 64).transpose(0, 3, 4, 1, 2).reshape(N, 128, 64)
    ldt = np.broadcast_to(P["s5_log_dt"][..., None], (N, 2, 64, 64))
    d["s5lam"] = np.ascontiguousarray(np.stack([pl(P["s5_lambda_re"]), pl(P["s5_lambda_im"]), pl(ldt)], axis=2))
    b = np.stack([P["s5_b_re"], P["s5_b_im"]], axis=2)
    d["s5b"] = np.ascontiguousarray(b.reshape(N * 4, 4096, 16))
    c = np.stack([P["s5_c_re"], P["s5_c_im"]], axis=2)
    d["s5c"] = np.ascontiguousarray(c.transpose(0, 1, 2, 3, 5, 4).reshape(N * 4, 4096, 16))
    d["s5d"] = np.ascontiguousarray(P["s5_d"].reshape(N, 8, 128).transpose(0, 2, 1))
    return d


def s5_core(cfg, state_s5_b):
    N = cfg.NCL
    a = state_s5_b.reshape(N, 2, 2, 32, 2, 64)
    return {"s5s0": np.ascontiguousarray(a.transpose(0, 4, 5, 2, 1, 3).reshape(N, 128, 2, 64))}


def s5_unpack(cfg, news5):
    N, NPS = cfg.NCL, cfg.NPS
    a = news5.reshape(N, NPS, 2, 2, 2, 64, 32)
    return a.transpose(1, 0, 2, 3, 6, 4, 5).reshape(NPS, N, 2, 2, 64, 64)


def rope_tables(L, grid_w=64, theta=10000.0):
    rows = L // grid_w
    row = np.repeat(np.arange(rows), grid_w).astype(np.float32)
    col = np.tile(np.arange(grid_w), rows).astype(np.float32)
    inv = (np.float32(theta) ** (-np.arange(16, dtype=np.float32) * np.float32(2.0) / np.float32(32))).astype(np.float32)
    ar, ac = row[:, None] * inv, col[:, None] * inv
    ang = np.concatenate([ar, ar, ac, ac], axis=-1).astype(np.float32)
    return np.ascontiguousarray(np.stack([np.cos(ang).T, np.sin(ang).T], axis=1).astype(np.float32))


def _colsL(W, c0, m):
    a = W[:, c0:c0 + m].reshape(8, 128, m).transpose(1, 0, 2)
    return np.ascontiguousarray(a.reshape(128, 8 * m))


def even_common(cfg, P):
    N = cfg.NAB
    d = {}
    win = P["ab_w_in"]
    d["wqL"] = np.stack([_colsL(win[i], h * 64, 64) for i in range(N) for h in range(8)])
    d["wkL"] = np.stack([_colsL(win[i], 512 + h * 64, 64) for i in range(N) for h in range(2)])
    d["wvL"] = np.stack([_colsL(win[i], 640 + h * 64, 64) for i in range(N) for h in range(2)])
    d["wpbL"] = np.stack([_colsL(win[i], 768 + c * 128, 128) for i in range(N) for c in range(15)])
    d["woL"] = _sqL(P["ab_w_out"])
    d["qkg"] = np.ascontiguousarray(np.stack([P["attn_q_gain"], P["attn_k_gain"]], axis=-1))
    d["ropeT"] = rope_tables(cfg.DEC_SEQ)
    rwv = np.zeros((N, 128, 64), np.float32)
    for i in range(N):
        rwv[i, :, 0:15] = P["rwkv_mu"][i].reshape(15, 128).T
        rwv[i, :, 15:19] = P["rwkv_k_k"][i].reshape(4, 128).T
        rwv[i, :, 19:23] = P["rwkv_k_a"][i].reshape(4, 128).T
        rwv[i, :, 23:27] = P["rwkv_r_k"][i].reshape(4, 128).T
        rwv[i, :, 27:35] = P["rwkv_w0"][i].reshape(2, 4, 128).transpose(2, 0, 1).reshape(128, 8)
        rwv[i, :, 35:43] = P["rwkv_a0"][i].reshape(2, 4, 128).transpose(2, 0, 1).reshape(128, 8)
    d["rwv"] = rwv
    ups = []
    for i in range(N):
        ups += [P["rwkv_w_up"][i].reshape(128, 512), P["rwkv_a_up"][i].reshape(128, 512), P["rwkv_g_up"][i]]
    d["rwup"] = np.ascontiguousarray(np.stack(ups))
    d["rwln"] = np.ascontiguousarray(np.stack([v for i in range(N) for v in (P["rwkv_ln_w"][i], P["rwkv_ln_b"][i])]))
    return d


def even_core(cfg, cache_k_b, cache_v_b, state_rwkv_b):
    N = cfg.NAB
    d = {}
    d["ctxkT"] = np.ascontiguousarray(cache_k_b.transpose(0, 2, 3, 1).reshape(N * 2, 64, cfg.PAST))
    d["ctxv"] = np.ascontiguousarray(cache_v_b.transpose(0, 2, 1, 3).reshape(N * 2, cfg.PAST, 64))
    return d


def even_unpack(cfg, r):
    N, NPS, SEQ = cfg.NAB, cfg.NPS, cfg.SEQ
    nk = r["newk"].reshape(N, 2, 64, NPS, SEQ).transpose(3, 0, 4, 1, 2)
    nv = r["newv"].reshape(N, 2, 64, NPS, SEQ).transpose(3, 0, 4, 1, 2)
    return nk, nv, None


def rw_masks():
    s = np.arange(128)[:, None]
    tt = np.arange(128)[None, :]
    same = (s // 64) == (tt // 64)
    MUs = (same & (s < tt)).astype(np.float32)
    MUi = (same & (s <= tt)).astype(np.float32)
    MLs = (same & (s > tt)).astype(np.float32)
    MLi = (same & (s >= tt)).astype(np.float32)
    m01 = np.ones((128, 1024), np.float32)
    m01[:, ::64] = 0.0
    return np.ascontiguousarray(np.concatenate([MUs, MUi, MLs, MLi, -MUs, -MLs, m01], axis=1))


def rw_core(cfg, state_rwkv_b):
    N = cfg.NAB
    return {"rws0T": np.ascontiguousarray(state_rwkv_b.transpose(0, 4, 1, 2, 3).reshape(N, 64, 1024))}


def rw_unpack(cfg, newrwT):
    N, NPS = cfg.NAB, cfg.NPS
    a = newrwT.reshape(N, NPS, 64, 2, 8, 64)
    return a.transpose(1, 0, 3, 4, 5, 2)


def _apx(handle, offset, dims):
    return bass.AP(handle, offset, [list(d) for d in dims])


def _even_decl(self):
    cfg = self.cfg
    I = self.inp
    NAB, NPS, TALL = max(cfg.NAB, 1), cfg.NPS, cfg.TALL
    self.wqL = I("wqL", [NAB * 8, 128, 512])
    self.wkL = I("wkL", [NAB * 2, 128, 512])
    self.wvL = I("wvL", [NAB * 2, 128, 512])
    self.wpbL = I("wpbL", [NAB * 15, 128, 1024])
    self.woL = I("woL", [NAB * 8, 128, 1024])
    self.qkg = I("qkg", [NAB, 64, 2])
    self.ropeT = I("ropeT", [64, 2, cfg.DEC_SEQ])
    self.ctxkT = I("ctxkT", [NAB * 2, 64, cfg.PAST])
    self.ctxv = I("ctxv", [NAB * 2, cfg.PAST, 64])
    self.rwv = I("rwv", [NAB, 128, 64])
    self.rwup = I("rwup", [NAB * 3, 128, 512])
    self.rwln = I("rwln", [NAB * 2, 512])
    self.newk = self.outp("newk", [NAB * 2, 64, cfg.TP])
    self.newv = self.outp("newv", [NAB * 2, 64, cfg.TP])
    self.qT = self.scratch("qT", [8, 64, TALL])
    self.kT = self.scratch("kT", [2, 64, TALL])
    self.vtm = self.scratch("vtm", [2, TALL, 64])
    self.pbT = self.scratch("pbT", [128, 15, TALL])
    self.mixT = self.scratch("mixT", [128, 8, TALL])
    self.rmask = I("rmask", [128, 6 * 128 + 1024])
    self.rws0T = I("rws0T", [NAB, 64, 1024])
    self.newrwT = self.outp("newrwT", [NAB * NPS, 64, 1024])
    NBLK = TALL // 128
    self.rwQG = self.scratch("rwQG", [NBLK * 16, 64, 256])
    self.rwMW = self.scratch("rwMW", [NBLK * 16, 64, 256])
    self.rwV = self.scratch("rwV", [NBLK * 8, 64, 128])
    self.RM = self.st.enter_context(self.nc.sbuf_tensor("rmsk", [128, 6 * 128 + 1024], F32))
    self.DMA(self.RM[:, 0:896], self.rmask[:, 0:896], [], ["RM"])
    self.DMA(self.RM[:, 896:1792], self.rmask[:, 896:1792], [], ["RM"])
    self.tmq = {}
    for nm in ("r", "nkk", "v", "w0", "w1", "b0", "b1", "kd0", "kd1", "g", "bon", "y0", "y1"):
        self.tmq[nm] = self.scratch("tm_" + nm, [TALL, 512])


KB.even_decl = _even_decl


def _head_norm(self, ps, pbuf, gain, dst, bdst, n, name):
    (sq, bsq), (r, br) = self.hn_scr
    self.ACT(sq, ps[0:64, 0:n], AF.Square, [pbuf], [bsq])
    p2, pb2 = self.PS[6], self.pb[6]
    self.MM(p2[0:64, 0:n], self.ones[0:64, 0:64], sq, True, True, ["CT", bsq], [pb2])
    self.ACT(r, p2[0:64, 0:n], AF.Sqrt, [pb2, "CS"], [br], scale=1.0 / 64, bias=self.CS[0:64, 0:1])
    self.RCP(r, r, [br], [br])
    self.STT(dst, ps[0:64, 0:n], gain, r, ALU.mult, ALU.mult, [pbuf, br, "QKG"], [bdst])


def _rope(self, x, bx, tok0, n, name):
    (rot, brot), (cs_, bcs) = self.rp_scr
    self.DMA(cs_, self.ropeT[:, :, tok0:tok0 + n], [], [bcs])
    p3, pb3 = self.PS[5], self.pb[5]
    self.MM(p3[0:64, 0:n], self.CT[0:64, 640:704], x, True, True, ["CT", bx], [pb3])
    self.TT(rot, p3[0:64, 0:n], cs_[:, 1, :], ALU.mult, [pb3, bcs], [brot])
    self.TT(x, x, cs_[:, 0, :], ALU.mult, [bx, bcs], [bx])
    self.TT(x, x, rot, ALU.add, [bx, brot], [bx])


KB.head_norm = _head_norm
KB.rope = _rope


def _mixer_even(self, l):
    cfg = self.cfg
    i = l // 2
    NTL, TP, SEQ, DSEQ, PAST, NPS = cfg.NTILES, cfg.TP, cfg.SEQ, cfg.DEC_SEQ, cfg.PAST, cfg.NPS
    for t in range(NTL):
        self.arena_reset()
        x, bx = self.alloc("x", [128, 8, TN])
        self.DMA(x, self.xres[:, :, t * TN:(t + 1) * TN], ["xres"], [bx])
        h, bh = self.modin(x, bx, l, 1, self.tile_cond(t))
        self.hn_scr = (self.alloc("hnsq", [64, TN]), self.alloc("hnr", [64, TN]))
        self.rp_scr = (self.alloc("rprot", [64, TN]), self.alloc("rpcs", [64, 2, TN]))
        vt, bvt = self.alloc("vt", [128, 4, 64])
        qg, bqg = self.alloc("qkg", [64, 2])
        self.DMA(qg, self.qkg[i], [], ["QKG"])
        wh = [self.alloc(f"wh{k}", [128, 8, 64]) for k in range(2)]
        hd, bhd = self.alloc("hd", [64, TN])
        n = 0
        for kind, cnt, wL, dstT in (("q", 8, self.wqL, self.qT), ("k", 2, self.wkL, self.kT), ("v", 2, self.wvL, None)):
            for hh in range(cnt):
                wv, wb = wh[n % 2]
                self.DMA(wv, wL[i * cnt + hh].rearrange("p (k m) -> p k m", k=8), [], [wb])
                ps, pbuf = self.PS[n % 2], self.pb[n % 2]
                n += 1
                for k in range(8):
                    self.MM(ps[0:64, :], wv[:, k, :], h[:, k, :], k == 0, k == 7, [wb, bh], [pbuf])
                if kind == "v":
                    self.ACT(hd, ps[0:64, :], AF.Copy, [pbuf], [bhd])
                    if t == 0:
                        self.DMA(self.newv[i * 2 + hh], hd, [bhd], ["newv"], q="pool")
                    p4, pb4 = self.PS[4], self.pb[4]
                    for b_ in range(4):
                        self.TR(p4[:, b_ * 64:(b_ + 1) * 64], hd[:, b_ * 128:(b_ + 1) * 128], self.ident[0:64, 0:64],
                                [bhd, "CT"], [pb4])
                    self.ACT(vt, p4[:, 0:256].rearrange("p (b d) -> p b d", d=64), AF.Copy, [pb4], [bvt])
                    self.DMA(self.vtm[hh, t * TN:(t + 1) * TN, :].rearrange("(b p) d -> p b d", p=128), vt, [bvt], ["vtm"], q="pool")
                else:
                    gcol = qg[:, 0:1] if kind == "q" else qg[:, 1:2]
                    self.head_norm(ps, pbuf, gcol, hd, bhd, TN, "hn")
                    if kind == "k" and t == 0:
                        self.DMA(self.newk[i * 2 + hh], hd, [bhd], ["newk"], q="pool")
                    if t > 0:
                        self.rope(hd, bhd, (t - 1) * TN, TN, "rp")
                    self.DMA(dstT[hh, :, t * TN:(t + 1) * TN], hd, [bhd], ["qT" if kind == "q" else "kT"], q="pool")
        wp = [self.alloc(f"wp{k}", [128, 8, 128]) for k in range(2)]
        pbo, bpbo = self.alloc("pbo", [128, 15, TN])
        for c in range(15):
            wv, wb = wp[c % 2]
            self.DMA(wv, self.wpbL[i * 15 + c].rearrange("p (k m) -> p k m", k=8), [], [wb])
            ps, pbuf = self.PS[2 + c % 2], self.pb[2 + c % 2]
            for k in range(8):
                self.MM(ps[:, :], wv[:, k, :], h[:, k, :], k == 0, k == 7, [wb, bh], [pbuf])
            self.ACT(pbo[:, c, :], ps[:, :], AF.Copy, [pbuf], [bpbo])
        self.DMA(self.pbT[:, :, t * TN:(t + 1) * TN], pbo, [bpbo], ["pbT"], q="pool")
    seqs = [(s * SEQ, SEQ, False) for s in range(NPS)] + [(TP, DSEQ, True)]
    for (tok0, Ls, is_s) in seqs:
        NK = Ls + (PAST if is_s else 0)
        nkb = NK // 128
        for kv in range(2):
            self.arena_reset()
            KT, bKT = self.alloc("KT", [64, NK])
            VT, bVT = self.alloc("VT", [128, nkb, 64])
            self.DMA(KT[:, 0:Ls], self.kT[kv, :, tok0:tok0 + Ls], ["kT"], [bKT])
            self.DMA(VT[:, 0:Ls // 128, :], self.vtm[kv, tok0:tok0 + Ls, :].rearrange("(b p) d -> p b d", p=128), ["vtm"], [bVT])
            if is_s:
                self.DMA(KT[:, Ls:NK], self.ctxkT[i * 2 + kv], [], [bKT])
                self.DMA(VT[:, Ls // 128:nkb, :], self.ctxv[i * 2 + kv].rearrange("(b p) d -> p b d", p=128), [], [bVT])
            Q = [self.alloc(f"Q{k}", [64, 4, 128]) for k in range(2)]
            E = [self.alloc(f"E{k}", [128, 512]) for k in range(2)]
            O = [self.alloc(f"O{k}", [64, 4, 128]) for k in range(2)]
            rd, brd = self.alloc("rd", [64, 512])
            for qt in range(Ls // 128):
                qv, bq = Q[qt % 2]
                ov, bo = O[qt % 2]
                ta = tok0 + qt * 128
                self.DMA(qv, self.qT[kv * 4:(kv + 1) * 4, :, ta:ta + 128].rearrange("g d q -> d g q"), ["qT"], [bq])
                po, pbo_ = self.PS[2], self.pb[2]
                pd, pbd = self.PS[3], self.pb[3]
                for kb in range(nkb):
                    ps, pbuf = self.PS[kb % 2], self.pb[kb % 2]
                    ev, be = E[kb % 2]
                    self.MM(ps[:, :], KT[:, kb * 128:(kb + 1) * 128], qv.rearrange("d g q -> d (g q)"), True, True, [bKT, bq], [pbuf])
                    self.ACT(ev, ps[:, :], AF.Exp, [pbuf], [be], scale=0.125)
                    self.MM(po[0:64, :], VT[:, kb, :], ev, kb == 0, kb == nkb - 1, [bVT, be], [pbo_])
                    self.MM(pd[0:64, :], self.ones[:, 0:64], ev, kb == 0, kb == nkb - 1, ["CT", be], [pbd])
                self.RCP(rd, pd[0:64, :], [pbd], [brd])
                self.TT(ov.rearrange("d g q -> d (g q)"), po[0:64, :], rd, ALU.mult, [pbo_, brd], [bo])
                for g in range(4):
                    hq = kv * 4 + g
                    self.DMA(self.mixT[(hq % 2) * 64:(hq % 2) * 64 + 64, hq // 2, ta:ta + 128], ov[:, g, :], [bo], ["mixT"], q="pool")
    NS = 256
    segs = []
    for (tok0, Ls, is_s) in seqs:
        for s0 in range(0, Ls, NS):
            segs.append((tok0 + s0, s0 == 0, s0 + NS == Ls))
    for (ta, first, last) in segs:
        self.arena_reset()
        prm, bprm = self.alloc("prm", [128, 64])
        self.DMA(prm, self.rwv[i], [], [bprm])
        mu = prm[:, 0:15]
        k_k, k_a, r_k = prm[:, 15:19], prm[:, 19:23], prm[:, 23:27]
        w0v, a0v = prm[:, 27:35], prm[:, 35:43]
        drv, bdrv = self.alloc("drv", [128, 64])
        omu, hmu, omka, nw0 = drv[:, 0:15], drv[:, 15:30], drv[:, 30:34], drv[:, 34:42]
        self.TS(omu, mu, -1.0, 1.0, ALU.mult, ALU.add, [bprm], [bdrv])
        self.TS(hmu, mu, 0.5, None, ALU.mult, None, [bprm], [bdrv])
        self.TS(omka, k_a, -1.0, 1.0, ALU.mult, ALU.add, [bprm], [bdrv])
        self.TS(nw0, w0v, -1.0, None, ALU.mult, None, [bprm], [bdrv])
        ups, bups = self.alloc("ups", [128, 3, 512])
        self.DMA(ups, self.rwup[i * 3:(i + 1) * 3].rearrange("a p m -> p a m"), [], [bups])
        raw, braw = self.alloc("raw", [128, 15, NS + 2])
        if first:
            self.MSET(raw[:, :, 0:1], 0.0, [braw])
        if last:
            self.MSET(raw[:, :, NS + 1:NS + 2], 0.0, [braw])
        lo = 1 if first else 0
        hi = NS + 1 if last else NS + 2
        self.DMA(raw[:, :, lo:hi], self.pbT[:, :, ta - 1 + lo:ta - 1 + hi], ["pbT"], [braw])
        pm, bpm = self.alloc("pm", [128, 15, NS])
        nb_, bnb = self.alloc("nb", [128, NS])
        for c in range(15):
            self.TT(nb_, raw[:, c, 0:NS], raw[:, c, 2:NS + 2], ALU.add, [braw], [bnb])
            self.TS(nb_, nb_, hmu[:, c:c + 1], None, ALU.mult, None, [bnb, bdrv], [bnb])
            self.STT(pm[:, c, :], raw[:, c, 1:NS + 1], omu[:, c:c + 1], nb_, ALU.mult, ALU.add, [braw, bdrv, bnb], [bpm])
        r_, k_, v_ = pm[:, 0:4, :], pm[:, 4:8, :], pm[:, 8:12, :]
        gd, wd, ad = pm[:, 12, :], pm[:, 13, :], pm[:, 14, :]
        G, bG = self.alloc("G", [128, 4, NS])
        W = [self.alloc(f"W{d}", [128, 4, NS]) for d in range(2)]
        A = [self.alloc(f"A{d}", [128, 4, NS]) for d in range(2)]
        Bq = [self.alloc(f"B{d}", [128, 4, NS]) for d in range(2)]
        KD = [self.alloc(f"KD{d}", [128, 4, NS]) for d in range(2)]
        LW = [self.alloc(f"LW{d}", [128, 4, NS]) for d in range(2)]
        kk, bkk = self.alloc("kk", [128, 4, NS])
        nkk, bnkk = self.alloc("nkk", [128, 4, NS])
        bon, bbon = self.alloc("bon", [128, 4, NS])
        t1, bt1 = self.alloc("t1", [128, NS])
        t2, bt2 = self.alloc("t2", [128, NS])
        sgm, bsgm = self.alloc("sgm", [128, NS])
        twd, btwd = self.alloc("twd", [128, NS])
        self.ACT(sgm, gd, AF.Sigmoid, [bpm], [bsgm])
        self.ACT(twd, wd, AF.Tanh, [bpm], [btwd])
        for c in range(4):
            cs = slice(c * 128, (c + 1) * 128)
            ps, pbuf = self.PS[c % 2], self.pb[c % 2]
            self.MM(ps[:, 0:NS], ups[:, 2, cs], sgm, True, True, [bups, bsgm], [pbuf])
            self.ACT(G[:, c, :], ps[:, 0:NS], AF.Copy, [pbuf], [bG])
            for d in range(2):
                rs = slice(d * 64, d * 64 + 64)
                pw, pbw = self.PS[2 + d], self.pb[2 + d]
                self.MM(pw[:, 0:NS], ups[rs, 0, cs], twd[rs, :], True, True, [bups, btwd], [pbw])
                self.ACT(t1, pw[:, 0:NS], AF.Exp, [pbw, bdrv], [bt1], scale=-1.0, bias=nw0[:, d * 4 + c:d * 4 + c + 1])
                self.TS(t1, t1, 1.0, None, ALU.add, None, [bt1], [bt1])
                self.RCP(t1, t1, [bt1], [bt1])
                self.ACT(W[d][0][:, c, :], t1, AF.Exp, [bt1], [W[d][1]], scale=-0.6065306597126334)
                self.TS(LW[d][0][:, c, :], t1, -0.6065306597126334, None, ALU.mult, None, [bt1], [LW[d][1]])
                pa, pba = self.PS[4 + d], self.pb[4 + d]
                self.MM(pa[:, 0:NS], ups[rs, 1, cs], ad[rs, :], True, True, [bups, bpm], [pba])
                self.ACT(A[d][0][:, c, :], pa[:, 0:NS], AF.Sigmoid, [pba, bprm], [A[d][1]], bias=a0v[:, d * 4 + c:d * 4 + c + 1])
            self.TS(kk[:, c, :], k_[:, c, :], k_k[:, c:c + 1], None, ALU.mult, None, [bpm, bprm], [bkk])
            self.TT(t2, kk[:, c, :], kk[:, c, :], ALU.mult, [bkk], [bt2])
            pq, pbq = self.PS[6], self.pb[6]
            self.MM(pq[:, 0:NS], self.bones, t2, True, True, ["CT", bt2], [pbq])
            self.ACT(t2, pq[:, 0:NS], AF.Sqrt, [pbq], [bt2])
            self.TS(t2, t2, 1e-12, None, ALU.max, None, [bt2], [bt2])
            self.RCP(t2, t2, [bt2], [bt2])
            self.TT(kk[:, c, :], kk[:, c, :], t2, ALU.mult, [bkk, bt2], [bkk])
            self.TS(nkk[:, c, :], kk[:, c, :], -1.0, None, ALU.mult, None, [bkk], [bnkk])
            for d in range(2):
                Ad, bAd = A[d]
                self.TT(Bq[d][0][:, c, :], kk[:, c, :], Ad[:, c, :], ALU.mult, [bkk, bAd], [Bq[d][1]])
                self.TS(t2, Ad[:, c, :], k_a[:, c:c + 1], omka[:, c:c + 1], ALU.mult, ALU.add, [bAd, bprm, bdrv], [bt2])
                self.TT(KD[d][0][:, c, :], k_[:, c, :], t2, ALU.mult, [bpm, bt2], [KD[d][1]])
                self.STT(t2, r_[:, c, :], r_k[:, c:c + 1], KD[d][0][:, c, :], ALU.mult, ALU.mult, [bpm, bprm, KD[d][1]], [bt2])
                pr_, pbr = self.PS[7], self.pb[7]
                self.MM(pr_[:, 0:NS], self.bones, t2, True, True, ["CT", bt2], [pbr])
                if d == 0:
                    self.TT(bon[:, c, :], pr_[:, 0:NS], v_[:, c, :], ALU.mult, [pbr, bpm], [bbon])
                else:
                    self.TT(t2, pr_[:, 0:NS], v_[:, c, :], ALU.mult, [pbr, bpm], [bt2])
                    self.TT(bon[:, c, :], bon[:, c, :], t2, ALU.add, [bbon, bt2], [bbon])
        stg = [self.alloc(f"stg{k}", [128, 512]) for k in range(2)]
        outs = [("g", G, bG), ("bon", bon, bbon)]
        n = 0
        for (nm, src, bsrc) in outs:
            for b_ in range(NS // 128):
                ps, pbuf = self.PS[n % 2], self.pb[n % 2]
                sv, bs_ = stg[n % 2]
                n += 1
                for c in range(4):
                    self.TR(ps[:, c * 128:(c + 1) * 128], src[:, c, b_ * 128:(b_ + 1) * 128], self.ident, [bsrc, "CT"], [pbuf])
                self.ACT(sv, ps[:, :], AF.Copy, [pbuf], [bs_])
                self.DMA(self.tmq[nm][ta + b_ * 128:ta + (b_ + 1) * 128, :], sv, [bs_], ["tm_" + nm], q="pool")
        self.rw_precompute(i, ta, NS, r_, v_, bpm, kk, bkk, LW, Bq, KD)
    self.rw_scan(i, seqs)
    for t in range(NTL):
        self.arena_reset()
        lnw, blnw = self.alloc("lnw", [128, 2, 512])
        self.DMA(lnw, _apx(self.rwln.tensor, i * 2 * 512, [[0, 128], [512, 2], [1, 512]]), [], [blnw])
        for b_ in range(4):
            ta = t * TN + b_ * 128
            y, by = self.alloc("y", [128, 8, 64])
            z, bz = self.alloc("z", [128, 8, 64])
            sq, bsq = self.alloc("sq", [128, 8, 64])
            g_, bg_ = self.alloc("g", [128, 8, 64])
            st_, bst = self.alloc("st", [128, 16])
            ob, bob = self.alloc("ob", [128, 4, 128])
            def ld(dst, bd, nm, q="sp"):
                self.DMA(dst.rearrange("p h k -> p (h k)"), self.tmq[nm][ta:ta + 128, :], ["tm_" + nm], [bd], q=q)
            ld(y, by, "y0")
            ld(z, bz, "y1", q="pool")
            ld(sq, bsq, "bon")
            ld(g_, bg_, "g", q="pool")
            self.TT(y, y, z, ALU.add, [by, bz], [by])
            self.TT(y, y, sq, ALU.add, [by, bsq], [by])
            mean, var = st_[:, 0:8], st_[:, 8:16]
            self.RED(mean, y, [by], [bst])
            self.TS(mean, mean, 1.0 / 64, None, ALU.mult, None, [bst], [bst])
            self.TT(y, y, mean.unsqueeze(2).broadcast_to([128, 8, 64]), ALU.subtract, [by, bst], [by])
            self.TT(sq, y, y, ALU.mult, [by], [bsq])
            self.RED(var, sq, [bsq], [bst])
            self.ACT(var, var, AF.Sqrt, [bst, "CS"], [bst], scale=1.0 / 64, bias=self.CS[:, 1:2])
            self.RCP(var, var, [bst], [bst])
            self.TT(y, y, var.unsqueeze(2).broadcast_to([128, 8, 64]), ALU.mult, [by, bst], [by])
            yf = y.rearrange("p h k -> p (h k)")
            self.TT(yf, yf, lnw[:, 0, :], ALU.mult, [by, blnw], [by])
            self.TT(yf, yf, lnw[:, 1, :], ALU.add, [by, blnw], [by])
            self.TT(y, y, g_, ALU.mult, [by, bg_], [by])
            ps, pbuf = self.PS[b_ % 2], self.pb[b_ % 2]
            for c in range(4):
                self.TR(ps[:, c * 128:(c + 1) * 128], yf[:, c * 128:(c + 1) * 128], self.ident, [by, "CT"], [pbuf])
            self.ACT(ob, ps[:, :].rearrange("p (c q) -> p c q", c=4), AF.Copy, [pbuf], [bob])
            self.DMA(self.mixT[:, 4:8, ta:ta + 128], ob, [bob], ["mixT"], q="pool")
            self.aoff -= (4 * 512 + 16 + 512)
    for t in range(NTL):
        self.arena_reset()
        x, bx = self.alloc("x", [128, 8, TN])
        m, bm = self.alloc("m", [128, 8, TN])
        y, by = self.alloc("y", [128, 8, TN])
        self.DMA(x, self.xres[:, :, t * TN:(t + 1) * TN], ["xres"], [bx])
        self.DMA(m, self.mixT[:, :, t * TN:(t + 1) * TN], ["mixT"], [bm])
        self.proj8(self.woL, i * 8, m, bm, y, by)
        self.resid(x, bx, y, by, l, 1, self.tile_cond(t))
        self.DMA(self.xres[:, :, t * TN:(t + 1) * TN], x, [bx], ["xres"], q="pool")


KB.mixer_even = _mixer_even


def _rw_precompute(self, i, ta, NS, r_, v_, bpm, kk, bkk, LW, Bq, KD):
    nch = NS // 64
    NF = 4 * NS
    MK, bMK = self.RM, "RM"
    MUs, MUi, MLs, MLi, nMUs, nMLs = [MK[:, k * 128:(k + 1) * 128] for k in range(6)]
    m01 = MK[:, 768:768 + NF]
    cum, bcum = self.alloc("cum", [128, 4, NS])
    ex, bex = self.alloc("ex", [128, 4, NS])
    Kt, bKt = self.alloc("Kt", [128, 4, NS])
    Bt, bBt = self.alloc("Bt", [128, 4, NS])
    Kd, bKd = self.alloc("Kd", [128, 4, NS])
    Rt, bRt = self.alloc("Rt", [128, 4, NS])
    Bp, bBp = self.alloc("Bp", [128, 4, NS])
    Kp, bKp = self.alloc("Kp", [128, 4, NS])
    ge, bge = self.alloc("ge", [128, 4, nch])
    UT = []
    for k in range(2):
        UT.append(dict(NX=self.alloc(f"NX{k}", [128, 2, 128]), X2=self.alloc(f"X2{k}", [128, 2, 128]),
                       AM=self.alloc(f"AM{k}", [128, 3, 128]), AkTm=self.alloc(f"AkTm{k}", [128, 128]),
                       Tm=self.alloc(f"Tm{k}", [128, 128]), TMt=self.alloc(f"TMt{k}", [128, 3, 64]),
                       nKA=self.alloc(f"nKA{k}", [128, 192]), Dg=self.alloc(f"Dg{k}", [128, 2, 64]),
                       Vs=self.alloc(f"Vs{k}", [64, 2, 64]), Bm=self.alloc(f"Bm{k}", [128, 2, 64])))
        self.MSET(UT[-1]["Dg"][0], 0.0, [UT[-1]["Dg"][1]])
    QGs = [self.alloc(f"QGs{k}", [64, 256]) for k in range(2)]
    MWs = [self.alloc(f"MWs{k}", [64, 2, 128]) for k in range(2)]
    f2 = lambda a_: a_.rearrange("p c n -> p (c n)")
    c3 = lambda a_: a_.rearrange("p c (h s) -> p c h s", s=64)
    nun = 0
    for d in range(2):
        lw, blw = LW[d]
        self.SCAN(f2(cum), m01, f2(lw), 0.0, [bMK, blw], [bcum])
        tot_bc = c3(cum)[:, :, :, 63:64].broadcast_to([128, 4, nch, 64])
        if d == 1:
            self.TT(c3(ex), tot_bc, c3(cum), ALU.subtract, [bcum], [bex])
            self.TT(f2(ex), f2(ex), f2(lw), ALU.add, [bex, blw], [bex])
            self.ACT(ge, c3(cum)[:, :, :, 63], AF.Exp, [bcum], [bge])
            self.CP(f2(cum), f2(ex), [bex], [bcum])
        else:
            self.ACT(ge, c3(cum)[:, :, :, 63], AF.Exp, [bcum], [bge])
        self.ACT(f2(ex), f2(cum), AF.Exp, [bcum], [bex])
        self.TT(Rt, r_, ex, ALU.mult, [bpm, bex], [bRt])
        self.TT(f2(ex), f2(cum), f2(lw), ALU.subtract, [bcum, blw], [bex])
        self.ACT(f2(ex), f2(ex), AF.Exp, [bex], [bex])
        self.TT(Kt, kk, ex, ALU.mult, [bkk, bex], [bKt])
        self.ACT(f2(ex), f2(cum), AF.Exp, [bcum], [bex], scale=-1.0)
        self.TT(Bt, Bq[d][0], ex, ALU.mult, [Bq[d][1], bex], [bBt])
        self.TT(Kd, KD[d][0], ex, ALU.mult, [KD[d][1], bex], [bKd])
        ge_bc = ge.unsqueeze(3).broadcast_to([128, 4, nch, 64])
        self.TT(c3(Bp), c3(Bt), ge_bc, ALU.mult, [bBt, bge], [bBp])
        self.TT(c3(Kp), c3(Kd), ge_bc, ALU.mult, [bKd, bge], [bKp])
        mS, mI, mST, nS_, nST = (MUs, MUi, MLs, nMUs, nMLs) if d == 0 else (MLs, MLi, MUs, nMLs, nMUs)
        def _unit(blk, h, u, d=d, mS=mS, mI=mI, mST=mST, nS_=nS_, nST=nST):
                tsl = slice(blk * 128, (blk + 1) * 128)
                gblk = (ta + blk * 128) // 128
                c = h // 2
                rs = slice((h % 2) * 64, (h % 2) * 64 + 64)
                U = UT[u]
                (NX, bNX), (X2, bX2), (AM, bAM), (AkTm, bAkT) = U["NX"], U["X2"], U["AM"], U["AkTm"]
                (Tm, bTm), (TMt, bTMt), (nKA, bnKA), (Dg, bDg), (Vs, bVs) = U["Tm"], U["TMt"], U["nKA"], U["Dg"], U["Vs"]
                Bm, bBm = U["Bm"]
                B0 = u * 4
                psA, bA = self.PS[B0], self.pb[B0]
                psB, bB = self.PS[B0 + 1][:, 0:256], self.pb[B0 + 1]
                R_ = [bBt, bKt, bRt, bKd]
                self.MM(psA[:, 0:128], Bt[rs, c, tsl], Kt[rs, c, tsl], True, True, R_, [bA])
                self.MM(psA[:, 128:256], Bt[rs, c, tsl], Rt[rs, c, tsl], True, True, R_, [bA])
                self.MM(psA[:, 256:384], Kd[rs, c, tsl], Kt[rs, c, tsl], True, True, R_, [bA])
                self.MM(psA[:, 384:512], Kd[rs, c, tsl], Rt[rs, c, tsl], True, True, R_, [bA])
                self.MM(psB[:, 0:128], Kt[rs, c, tsl], Bt[rs, c, tsl], True, True, R_, [bB])
                self.MM(psB[:, 128:256], Kt[rs, c, tsl], Kd[rs, c, tsl], True, True, R_, [bB])
                yield
                self.TT(NX[:, 0, :], psA[:, 0:128], nS_, ALU.mult, [bA, bMK], [bNX])
                self.TT(NX[:, 1, :], psB[:, 0:128], nST, ALU.mult, [bB, bMK], [bNX])
                self.TT(AM[:, 0, :], psA[:, 128:256], mI, ALU.mult, [bA, bMK], [bAM])
                self.TT(AM[:, 1, :], psA[:, 256:384], mS, ALU.mult, [bA, bMK], [bAM])
                self.TT(AM[:, 2, :], psA[:, 384:512], mI, ALU.mult, [bA, bMK], [bAM])
                self.TT(AkTm, psB[:, 128:256], mST, ALU.mult, [bB, bMK], [bAkT])
                self.TT(Tm, NX[:, 0, :], self.ident, ALU.add, [bNX, "CT"], [bTm])
                yield
                psC, bC = self.PS[B0 + 1][:, 256:512], self.pb[B0 + 1]
                psK_v, bKv = self.PS[B0][:, 0:128], self.pb[B0]
                self.TR(psC[:, 0:64], Kt[rs, c, tsl], self.ident[rs, rs], [bKt, "CT"], [bC])
                self.TR(psC[:, 64:128], Bp[rs, c, tsl], self.ident[rs, rs], [bBp, "CT"], [bC])
                self.TR(psC[:, 128:192], Kp[rs, c, tsl], self.ident[rs, rs], [bKp, "CT"], [bC])
                if d == 0:
                    for ch in range(2):
                        tch = slice(blk * 128 + ch * 64, blk * 128 + ch * 64 + 64)
                        self.TR(psK_v[0:64, ch * 64:(ch + 1) * 64],
                                v_[rs, c, tch], self.ident[rs, rs], [bpm, "CT"], [bKv])
                    self.ACT(Vs.rearrange("p a k -> p (a k)"), psK_v[0:64, 0:128], AF.Copy, [bKv], [bVs])
                    self.DMA(self.rwV[gblk * 8 + h], Vs.rearrange("p a k -> p (a k)"), [bVs], ["rwV"], q="pool")
                self.ACT(TMt.rearrange("p a k -> p (a k)"), psC[:, 0:192], AF.Copy, [bC], [bTMt])
                Xc, XTc, bXc = NX[:, 0, :], NX[:, 1, :], bNX
                Xo, bXo = X2, bX2
                for lev in range(5):
                    yield
                    psX, bX = self.PS[B0 + 2][:, 0:256], self.pb[B0 + 2]
                    psT, bT_ = self.PS[B0 + 2][:, 256:512], self.pb[B0 + 2]
                    if lev < 4:
                        self.MM(psX[:, 0:128], XTc, Xc, True, True, [bXc], [bX])
                    self.MM(psX[:, 128:256], Xc, XTc, True, True, [bXc], [bX])
                    if lev < 4:
                        self.ACT(Xo.rearrange("p a k -> p (a k)"), psX[:, 0:256], AF.Copy, [bX], [bXo])
                    else:
                        self.ACT(Xo[:, 1, :], psX[:, 128:256], AF.Copy, [bX], [bXo])
                    yield
                    self.MM(psT[:, 0:128], Xo[:, 1, :], Tm, True, True, [bXo, bTm], [bT_])
                    self.TT(Tm, Tm, psT[:, 0:128], ALU.add, [bTm, bT_], [bTm])
                    (Xc, XTc, bXc), (Xo, bXo) = (Xo[:, 0, :], Xo[:, 1, :], bXo), ((NX, bNX) if Xo is X2 else (X2, bX2))
                yield
                psK, bK = self.PS[B0 + 3][:, 0:192], self.pb[B0 + 3]
                self.MM(psK[:, 0:64], Tm, TMt[:, 0, :], True, True, [bTm, bTMt], [bK])
                self.MM(psK[:, 64:192], Tm, AkTm, True, True, [bTm, bAkT], [bK])
                self.ACT(nKA, psK[:, 0:192], AF.Copy, [bK], [bnKA], scale=-1.0)
                self.TS(Bm[:, 0, :], TMt[:, 1, :], self.CT[:, 960:961], None, ALU.mult, None, [bTMt, "CT"], [bBm])
                self.TS(Bm[:, 1, :], TMt[:, 1, :], self.CT[:, 961:962], None, ALU.mult, None, [bTMt, "CT"], [bBm])
                for ch in range(2):
                    gcol = ge[rs, c, blk * 2 + ch:blk * 2 + ch + 1]
                    self.TS(Dg[rs, ch, :], self.ident[rs, rs], gcol, None, ALU.mult, None, ["CT", bge], [bDg])
                yield
                (QG, bQG), (MW, bMW) = QGs[u], MWs[u]
                psQ, bQ = self.PS[B0 + 3][:, 192:448], self.pb[B0 + 3]
                psM, bM = self.PS[B0 + 2][:, 0:256], self.pb[B0 + 2]
                sel = self.ident[:, rs]
                self.MM(psQ[0:64, 0:128], nKA[:, 0:64], AM[:, 0, :], True, False, [bnKA, bAM], [bQ])
                self.MM(psQ[0:64, 0:128], sel, Rt[:, c, tsl], False, True, ["CT", bRt], [bQ])
                for ch in range(2):
                    cs_ = slice(ch * 64, ch * 64 + 64)
                    o = 128 + ch * 64
                    self.MM(psQ[0:64, o:o + 64], nKA[:, 0:64], Bm[:, ch, :], True, False, [bnKA, bBm], [bQ])
                    self.MM(psQ[0:64, o:o + 64], sel, Dg[:, ch, :], False, True, ["CT", bDg], [bQ])
                    lk = nKA[:, 64 + ch * 64:64 + ch * 64 + 64]
                    lid = self.ident[:, cs_]
                    om = ch * 128
                    self.MM(psM[0:64, om:om + 64], lk, AM[:, 0, cs_], True, False, [bnKA, bAM], [bM])
                    self.MM(psM[0:64, om:om + 64], lid, AM[:, 2, cs_], False, True, ["CT", bAM], [bM])
                    self.MM(psM[0:64, om + 64:om + 128], lk, TMt[:, 1, :], True, False, [bnKA, bTMt], [bM])
                    self.MM(psM[0:64, om + 64:om + 128], lid, TMt[:, 2, :], False, True, ["CT", bTMt], [bM])
                self.ACT(QG, psQ[0:64, 0:256], AF.Copy, [bQ], [bQG])
                self.ACT(MW.rearrange("p a k -> p (a k)"), psM[0:64, 0:256], AF.Copy, [bM], [bMW])
                uidx = gblk * 16 + d * 8 + h
                self.DMA(self.rwQG[uidx], QG, [bQG], ["rwQG"], q="sp")
                self.DMA(self.rwMW[uidx], MW.rearrange("p a k -> p (a k)"), [bMW], ["rwMW"], q="pool")
        ulist = [(blk, h) for blk in range(NS // 128) for h in range(8)]
        for k0 in range(0, len(ulist), 2):
            gens = [_unit(ulist[k0][0], ulist[k0][1], 0), _unit(ulist[k0 + 1][0], ulist[k0 + 1][1], 1)]
            while gens:
                for g_ in list(gens):
                    try:
                        next(g_)
                    except StopIteration:
                        gens.remove(g_)


def _rw_scan(self, i, seqs):
    cfg = self.cfg
    for si, (tok0, Ls, is_s) in enumerate(seqs):
        self.arena_reset()
        NB = Ls // 128
        ST = [self.alloc(f"ST{k}", [64, 16, 64]) for k in range(2)]
        Yst = [self.alloc(f"Yst{k}", [64, 16, 64]) for k in range(2)]
        QGt = [[self.alloc(f"QG{k}{d}", [64, 8, 256]) for d in range(2)] for k in range(2)]
        MWt = [[self.alloc(f"MW{k}{d}", [64, 8, 256]) for d in range(2)] for k in range(2)]
        Vt = [[self.alloc(f"V{k}{d}", [64, 8, 128]) for d in range(2)] for k in range(2)]
        f2 = lambda a_: a_.rearrange("p u v -> p (u v)")
        if is_s:
            self.DMA(f2(ST[0][0]), self.rws0T[i], [], [ST[0][1]])
        else:
            self.MSET(f2(ST[0][0]), 0.0, [ST[0][1]])
        cur = 0
        ny = 0
        for b in range(NB):
            par = b % 2
            blks = (tok0 // 128 + b, tok0 // 128 + NB - 1 - b)
            for d in range(2):
                g = blks[d]
                self.DMA(QGt[par][d][0], self.rwQG[g * 16 + d * 8:g * 16 + d * 8 + 8].rearrange("u p m -> p u m"),
                         ["rwQG"], [QGt[par][d][1]], q="sp")
                self.DMA(MWt[par][d][0], self.rwMW[g * 16 + d * 8:g * 16 + d * 8 + 8].rearrange("u p m -> p u m"),
                         ["rwMW"], [MWt[par][d][1]], q="pool")
                self.DMA(Vt[par][d][0], self.rwV[g * 8:g * 8 + 8].rearrange("u p m -> p u m"), ["rwV"], [Vt[par][d][1]], q="sp")
            for step in range(2):
                Sc, bSc = ST[cur]
                Sn, bSn = ST[1 - cur]
                yv, by = Yst[ny % 2]
                ny += 1
                for d in range(2):
                    ch = step if d == 0 else 1 - step
                    cs_ = slice(ch * 64, ch * 64 + 64)
                    QG, bQG = QGt[par][d]
                    MW, bMW = MWt[par][d]
                    V, bV = Vt[par][d]
                    psY, bY = self.PS[d], self.pb[d]
                    psS, bS = self.PS[2 + d], self.pb[2 + d]
                    for h in range(8):
                        u = d * 8 + h
                        o = h * 64
                        vv = V[:, h, ch * 64:ch * 64 + 64]
                        self.MM(psY[0:64, o:o + 64], QG[:, h, ch * 64:ch * 64 + 64], Sc[:, u, :], True, False, [bQG, bSc], [bY])
                        self.MM(psY[0:64, o:o + 64], MW[:, h, ch * 128:ch * 128 + 64], vv, False, True, [bMW, bV], [bY])
                        self.MM(psS[0:64, o:o + 64], QG[:, h, 128 + ch * 64:128 + ch * 64 + 64], Sc[:, u, :], True, False, [bQG, bSc], [bS])
                        self.MM(psS[0:64, o:o + 64], MW[:, h, ch * 128 + 64:ch * 128 + 128], vv, False, True, [bMW, bV], [bS])
                    self.ACT(f2(yv)[:, d * 512:(d + 1) * 512], psY[0:64, :], AF.Copy, [bY], [by])
                    self.CP(f2(Sn)[:, d * 512:(d + 1) * 512], psS[0:64, :], [bS], [bSn])
                    tokd = blks[d] * 128 + ch * 64
                    self.DMA(self.tmq["y0" if d == 0 else "y1"][tokd:tokd + 64, :], f2(yv)[:, d * 512:(d + 1) * 512],
                             [by], ["tm_y0" if d == 0 else "tm_y1"], q="pool")
                cur = 1 - cur
        if not is_s:
            self.DMA(self.newrwT[i * cfg.NPS + si], f2(ST[cur][0]), [ST[cur][1]], ["newrwT"], q="pool")


KB.rw_precompute = _rw_precompute
KB.rw_scan = _rw_scan


def kernel(x_prompt, x_sample, cache_k, cache_v, state_rwkv, state_s5, c, c_ctx, **P):
    f = lambda a: np.asarray(a, np.float32)
    x_prompt, x_sample, cache_k, cache_v = f(x_prompt), f(x_sample), f(cache_k), f(cache_v)
    state_rwkv, state_s5, c, c_ctx = f(state_rwkv), f(state_s5), f(c), f(c_ctx)
    P = {k: f(v) for k, v in P.items()}
    B, SEQ, _ = x_prompt.shape
    DB, DSEQ, _ = x_sample.shape
    L = P["ada_w"].shape[0]
    ncores = 8
    cfg = Cfg(SEQ=SEQ, DEC_SEQ=DSEQ, DEPTH=L, PAST=cache_k.shape[2], NPS=B // ncores)
    kb = KB(cfg)
    nc = kb.build()
    common = common_inputs(cfg, P)
    common.update(s5_common(cfg, P))
    common.update(even_common(cfg, P))
    common["rmask"] = rw_masks()
    in_maps = []
    for core in range(ncores):
        b = core % DB
        d = dict(common)
        d.update(core_inputs(cfg, core, x_prompt, x_sample, c, c_ctx))
        d.update(s5_core(cfg, state_s5[b]))
        d.update(even_core(cfg, cache_k[b], cache_v[b], state_rwkv[b]))
        d.update(rw_core(cfg, state_rwkv[b]))
        in_maps.append(d)
    res = run_bass_kernel_spmd(nc, in_maps, core_ids=list(range(ncores)))
    NPS, NAB = cfg.NPS, cfg.NAB
    y_prompt = np.zeros((B, SEQ, D), np.float32)
    y_sample = np.zeros((DB, DSEQ, D), np.float32)
    new_s5 = np.zeros((B, cfg.NCL, 2, 2, 64, 64), np.float32)
    new_k = np.zeros((B, NAB, SEQ, 2, 64), np.float32)
    new_v = np.zeros((B, NAB, SEQ, 2, 64), np.float32)
    new_rwkv = np.zeros((B, NAB, 2, 8, 64, 64), np.float32)
    for core in range(ncores):
        r = res.results[core]
        y = r["yT"].transpose(1, 0, 2).reshape(D, cfg.TALL).T
        sl = slice(core * NPS, (core + 1) * NPS)
        y_prompt[sl] = y[:cfg.TP].reshape(NPS, SEQ, D)
        if core < DB:
            y_sample[core] = y[cfg.TP:]
        new_s5[sl] = s5_unpack(cfg, r["news5"])
        nk, nv, rw = even_unpack(cfg, r)
        new_k[sl], new_v[sl], new_rwkv[sl] = nk, nv, rw_unpack(cfg, r["newrwT"])
    return (y_prompt, y_sample, new_k, new_v, new_rwkv, new_s5)
```
